# Optimizing a Trainium2 kernel written in Bass

```python
import math
import jax, jax.numpy as jnp
from jax import lax
import numpy as np

D_MODEL = 1024
BATCH = 16
SEQ = 2048
DEPTH = 4

N_META = 16
EXPAND = 2
D_MIX = EXPAND * D_MODEL
D_MLSTM = D_MIX // 2
D_DIFF = D_MIX - D_MLSTM
MLSTM_HEADS = 4
MLSTM_HEAD_DIM = D_MLSTM // MLSTM_HEADS
MLSTM_CHUNK = 64
CONV_WIDTH = 4
DIFF_HEADS = 8
DIFF_HEAD_DIM = D_DIFF // DIFF_HEADS // 2
DIFF_V_DIM = 2 * DIFF_HEAD_DIM
Q_BLOCK = 128
D_IN = 5 * D_MLSTM + 2 * MLSTM_HEADS + 4 * D_DIFF
EPS = 1e-6
NEG = -1e30

kernel_name = 'hymba_mlstm_diffattn_hybrid'


def rmsnorm(x, g):
    xf = x.astype(jnp.float32)
    xf = xf * lax.rsqrt(jnp.mean(xf * xf, axis=-1, keepdims=True) + EPS)
    return (xf * g.astype(jnp.float32)).astype(x.dtype)


def head_rmsnorm(h, g, n_heads):
    B, L, D = h.shape
    hf = h.astype(jnp.float32).reshape(B, L, n_heads, D // n_heads)
    hf = hf * lax.rsqrt(jnp.mean(hf * hf, axis=-1, keepdims=True) + EPS)
    return (hf.reshape(B, L, D) * g.astype(jnp.float32)).astype(h.dtype)


def causal_conv(x, w, b):
    C = x.shape[-1]
    y = lax.conv_general_dilated(
        x, w[:, None, :].astype(x.dtype), window_strides=(1,),
        padding=[(CONV_WIDTH - 1, 0)], dimension_numbers=('NWC', 'WIO', 'NWC'),
        feature_group_count=C)
    return y + b.astype(x.dtype)


def mlstm_chunkwise(q, k, v, log_i, log_f):
    B, L, H, dh = q.shape
    pad = MLSTM_CHUNK - N_META
    pw = ((0, 0), (pad, 0), (0, 0), (0, 0))
    q = jnp.pad(q, pw) * (dh ** -0.5)
    k = jnp.pad(k, pw)
    v = jnp.pad(v, pw)
    log_i = jnp.pad(log_i, ((0, 0), (pad, 0), (0, 0)), constant_values=NEG)
    log_f = jnp.pad(log_f, ((0, 0), (pad, 0), (0, 0)))
    Lp = L + pad
    nc = Lp // MLSTM_CHUNK

    def to_chunks(a):
        a = a.reshape((B, nc, MLSTM_CHUNK, H) + a.shape[3:])
        return jnp.moveaxis(a, (1, 3), (0, 2))

    causal = jnp.tril(jnp.ones((MLSTM_CHUNK, MLSTM_CHUNK), dtype=bool))

    def step(carry, inp):
        C, n, m = carry
        qc, kc, vc, li, lf = inp
        qf = qc.astype(jnp.float32)
        kf = kc.astype(jnp.float32)
        vf = vc.astype(jnp.float32)
        b = jnp.cumsum(lf, axis=-1)
        D = b[..., :, None] - b[..., None, :] + li[..., None, :]
        D = jnp.where(causal, D, NEG)
        inter = b + m[..., None]
        m_row = jnp.maximum(inter, jnp.max(D, axis=-1))
        w_intra = jnp.exp(D - m_row[..., None])
        w_inter = jnp.exp(inter - m_row)
        s = jnp.einsum('bhtd,bhsd->bhts', qf, kf) * w_intra
        num = (jnp.einsum('bhts,bhsd->bhtd', s, vf)
               + w_inter[..., None] * jnp.einsum('bhvk,bhtk->bhtv', C, qf))
        den = jnp.sum(s, axis=-1) + w_inter * jnp.einsum('bhk,bhtk->bht', n, qf)
        h = num / jnp.maximum(jnp.abs(den), jnp.exp(-m_row))[..., None]
        b_T = b[..., -1]
        g = b_T[..., None] - b + li
        m_new = jnp.maximum(b_T + m, jnp.max(g, axis=-1))
        wk = jnp.exp(g - m_new[..., None])
        decay = jnp.exp(b_T + m - m_new)
        C_new = decay[..., None, None] * C + jnp.einsum('bhs,bhsv,bhsk->bhvk', wk, vf, kf)
        n_new = decay[..., None] * n + jnp.einsum('bhs,bhsk->bhk', wk, kf)
        return (C_new, n_new, m_new), h

    init = (jnp.zeros((B, H, dh, dh), jnp.float32),
            jnp.zeros((B, H, dh), jnp.float32),
            jnp.zeros((B, H), jnp.float32))
    xs = (to_chunks(q), to_chunks(k), to_chunks(v), to_chunks(log_i), to_chunks(log_f))
    _, h = lax.scan(step, init, xs)
    h = jnp.moveaxis(h, (0, 2), (1, 3)).reshape(B, Lp, H, dh)
    return h[:, pad:]


def diff_attention(q, k, v, lam):
    B, L, H = q.shape[:3]
    scale = DIFF_HEAD_DIM ** -0.5
    kpos = jnp.arange(L)

    def attend(qb, qpos):
        s = jnp.einsum('bqhcd,bkhcd->bhcqk', qb, k).astype(jnp.float32) * scale
        s = jnp.where(kpos[None, :] <= qpos[:, None], s, NEG)
        p = jax.nn.softmax(s, axis=-1)
        a = p[:, :, 0] - lam * p[:, :, 1]
        return jnp.einsum('bhqk,bkhv->bqhv', a.astype(v.dtype), v)

    out_meta = attend(q[:, :N_META], jnp.arange(N_META))
    n_real = L - N_META
    nb = n_real // Q_BLOCK
    q_blocks = jnp.moveaxis(
        q[:, N_META:].reshape(B, nb, Q_BLOCK, H, 2, DIFF_HEAD_DIM), 1, 0)
    pos_blocks = (N_META + jnp.arange(n_real)).reshape(nb, Q_BLOCK)
    out_real = lax.map(lambda a: attend(a[0], a[1]), (q_blocks, pos_blocks))
    out_real = jnp.moveaxis(out_real, 0, 1).reshape(B, n_real, H, DIFF_V_DIM)
    return jnp.concatenate([out_meta, out_real], axis=1)


def hybrid_layer(x, layer, g_pre, g_post, w_in, b_gates, conv_w, conv_b, g_mlstm,
                 lq1, lk1, lq2, lk2, g_diff, w_out):
    B, L, _ = x.shape
    h = rmsnorm(x, g_pre)
    proj = jnp.einsum('bld,de->ble', h, w_in)
    splits = list(np.cumsum([D_MLSTM] * 5 + [MLSTM_HEADS] * 2 + [D_DIFF] * 3))
    q_m, k_m, v_m, o_m, z_m, i_m, f_m, q_d, k_d, v_d, z_d = jnp.split(proj, splits, axis=-1)

    qk = jax.nn.silu(causal_conv(jnp.concatenate([q_m, k_m], axis=-1), conv_w, conv_b))
    q_m, k_m = jnp.split(qk, 2, axis=-1)
    heads = lambda a: a.reshape(B, L, MLSTM_HEADS, MLSTM_HEAD_DIM)
    gates = (jnp.concatenate([i_m, f_m], axis=-1) + b_gates).astype(jnp.float32)
    log_i, f_pre = jnp.split(gates, 2, axis=-1)
    log_f = jax.nn.log_sigmoid(f_pre)
    hm = mlstm_chunkwise(heads(q_m), heads(k_m), heads(v_m), log_i, log_f)
    hm = hm.reshape(B, L, D_MLSTM).astype(x.dtype) * jax.nn.sigmoid(o_m)
    hm = head_rmsnorm(hm, g_mlstm, MLSTM_HEADS) * jax.nn.silu(z_m)

    lam_init = 0.8 - 0.6 * math.exp(-0.3 * layer)
    lam = (jnp.exp(jnp.sum(lq1.astype(jnp.float32) * lk1.astype(jnp.float32)))
           - jnp.exp(jnp.sum(lq2.astype(jnp.float32) * lk2.astype(jnp.float32))) + lam_init)
    qd = q_d.reshape(B, L, DIFF_HEADS, 2, DIFF_HEAD_DIM)
    kd = k_d.reshape(B, L, DIFF_HEADS, 2, DIFF_HEAD_DIM)
    vd = v_d.reshape(B, L, DIFF_HEADS, DIFF_V_DIM)
    hd = diff_attention(qd, kd, vd, lam).reshape(B, L, D_DIFF)
    hd = head_rmsnorm(hd, g_diff, DIFF_HEADS) * (1.0 - lam_init) * jax.nn.silu(z_d)

    y = jnp.einsum('ble,ed->bld', jnp.concatenate([hm, hd], axis=-1), w_out)
    return x + rmsnorm(y, g_post)


def setup_inputs(seed: int = 0) -> dict:
    key = jax.random.key(seed)
    ks = jax.random.split(key, 16)
    nrm = jax.random.normal
    x = nrm(ks[0], (BATCH, SEQ, D_MODEL), jnp.float32)
    meta_tokens = nrm(ks[1], (N_META, D_MODEL), jnp.float32)
    pre_norm_g = 1.0 + 0.02 * nrm(ks[2], (DEPTH, D_MODEL), jnp.float32)
    post_norm_g = 1.0 + 0.02 * nrm(ks[3], (DEPTH, D_MODEL), jnp.float32)
    w_in = nrm(ks[4], (DEPTH, D_MODEL, D_IN), jnp.float32) * D_MODEL ** -0.5
    b_i = 0.1 * nrm(ks[5], (DEPTH, MLSTM_HEADS), jnp.float32)
    b_f = jnp.linspace(3.0, 6.0, MLSTM_HEADS, dtype=jnp.float32)[None, :] \
        + 0.1 * nrm(ks[6], (DEPTH, MLSTM_HEADS), jnp.float32)
    b_gates = jnp.concatenate([b_i, b_f], axis=-1)
    conv_w = nrm(ks[7], (DEPTH, CONV_WIDTH, 2 * D_MLSTM), jnp.float32) * CONV_WIDTH ** -0.5
    conv_b = 0.01 * nrm(ks[8], (DEPTH, 2 * D_MLSTM), jnp.float32)
    mlstm_norm_g = 1.0 + 0.02 * nrm(ks[9], (DEPTH, D_MLSTM), jnp.float32)
    lambda_q1 = 0.1 * nrm(ks[10], (DEPTH, DIFF_HEAD_DIM), jnp.float32)
    lambda_k1 = 0.1 * nrm(ks[11], (DEPTH, DIFF_HEAD_DIM), jnp.float32)
    lambda_q2 = 0.1 * nrm(ks[12], (DEPTH, DIFF_HEAD_DIM), jnp.float32)
    lambda_k2 = 0.1 * nrm(ks[13], (DEPTH, DIFF_HEAD_DIM), jnp.float32)
    diff_norm_g = 1.0 + 0.02 * nrm(ks[14], (DEPTH, D_DIFF), jnp.float32)
    w_out = nrm(ks[15], (DEPTH, D_MIX, D_MODEL), jnp.float32) * D_MIX ** -0.5
    return {'x': x, 'meta_tokens': meta_tokens, 'pre_norm_g': pre_norm_g,
            'post_norm_g': post_norm_g, 'w_in': w_in, 'b_gates': b_gates,
            'conv_w': conv_w, 'conv_b': conv_b, 'mlstm_norm_g': mlstm_norm_g,
            'lambda_q1': lambda_q1, 'lambda_k1': lambda_k1, 'lambda_q2': lambda_q2,
            'lambda_k2': lambda_k2, 'diff_norm_g': diff_norm_g, 'w_out': w_out}


def reference(x, meta_tokens, pre_norm_g, post_norm_g, w_in, b_gates, conv_w, conv_b,
              mlstm_norm_g, lambda_q1, lambda_k1, lambda_q2, lambda_k2, diff_norm_g, w_out):
    B = x.shape[0]
    meta = jnp.broadcast_to(meta_tokens[None].astype(x.dtype), (B, N_META, D_MODEL))
    h = jnp.concatenate([meta, x], axis=1)
    for layer in range(DEPTH):
        h = hybrid_layer(h, layer, pre_norm_g[layer], post_norm_g[layer], w_in[layer],
                         b_gates[layer], conv_w[layer], conv_b[layer], mlstm_norm_g[layer],
                         lambda_q1[layer], lambda_k1[layer], lambda_q2[layer],
                         lambda_k2[layer], diff_norm_g[layer], w_out[layer])
    return h[:, N_META:]
```

```python
import contextlib
import math
import os
import numpy as np
import ml_dtypes
import concourse.bass as bass
import concourse.mybir as mybir
from concourse.bass_utils import run_bass_kernel_spmd

F32 = mybir.dt.float32
BF16 = mybir.dt.bfloat16
AF = mybir.ActivationFunctionType
ALU = mybir.AluOpType
AX = mybir.AxisListType

ENGS = ["pe", "act", "dve", "pool", "sp"]


class Buf:
    __slots__ = ("n", "w", "r")

    def __init__(self, n=""):
        self.n = n
        self.w = None
        self.r = []


class Op:
    __slots__ = ("eng", "fn", "deps", "sig", "tok", "dma")


class Prog:
    NDMA = 8
    ROLL = 30000

    def __init__(self, nc):
        self.nc = nc
        self.ops = {e: [] for e in ENGS}
        self.dq = {e: {"next": 0, "last": [None] * self.NDMA, "cnt": [0] * self.NDMA} for e in ENGS}
        self.all_dma = []

    def add(self, eng, fn, reads=(), writes=(), dma=False):
        o = Op()
        o.eng, o.fn, o.dma, o.sig, o.tok = eng, fn, dma, False, None
        cand = []
        for b in reads:
            if b.w is not None:
                cand.append((b.w, True))
        for b in writes:
            if b.w is not None:
                cand.append((b.w, False))
            for r in b.r:
                cand.append((r, False))
        deps, seen = [], set()
        if dma:
            q = self.dq[eng]
            s = q["next"]
            q["next"] = (s + 1) % self.NDMA
            if q["last"][s] is not None:
                cand.append((q["last"][s], True))
            q["cnt"][s] += 16
            o.tok = (("d", eng, s), q["cnt"][s])
            q["last"][s] = o
            self.all_dma.append(o)
        for p, raw in cand:
            if p is o or id(p) in seen:
                continue
            if (not dma) and (not p.dma) and p.eng == eng:
                if eng == "pe":
                    continue
            seen.add(id(p))
            deps.append(p)
            if not p.dma:
                p.sig = True
        o.deps = deps
        for b in reads:
            if dma:
                b.r.append(o)
            else:
                b.r = [x for x in b.r if x.dma or x.eng != eng] + [o]
        for b in writes:
            b.w = o
            b.r = []
        self.ops[eng].append(o)
        return o

    def emit(self, stack):
        nc = self.nc
        keys = []
        for e in ENGS:
            cnt, gen, used = 0, 0, False
            for o in self.ops[e]:
                if o.dma or not o.sig:
                    continue
                cnt += 1
                if cnt > self.ROLL:
                    gen += 1
                    cnt = 1
                o.tok = (("c", e, gen), cnt)
                used = True
            if used:
                for g in range(gen + 1):
                    keys.append(("c", e, g))
            for s in range(self.NDMA):
                if self.dq[e]["cnt"][s]:
                    keys.append(("d", e, s))
        sems = {k: stack.enter_context(nc.semaphore("s_%s_%s_%d" % k)) for k in keys}
        final = {}
        for o in self.all_dma:
            final[o.tok[0]] = max(final.get(o.tok[0], 0), o.tok[1])

        def mk(e):
            def body(engine):
                waited = {}
                for o in self.ops[e]:
                    for p in o.deps:
                        k, v = p.tok
                        if waited.get(k, 0) < v:
                            engine.wait_ge(sems[k], v)
                            waited[k] = v
                    ins = o.fn(engine)
                    if o.dma:
                        ins.then_inc(sems[o.tok[0]], 16)
                    elif o.sig:
                        ins.then_inc(sems[o.tok[0]], 1)
                if e == "sp":
                    for k, v in final.items():
                        if waited.get(k, 0) < v:
                            engine.wait_ge(sems[k], v)
            return body

        with nc.Block() as block:
            block.tensor(mk("pe"))
            block.scalar(mk("act"))
            block.vector(mk("dve"))
            block.gpsimd(mk("pool"))
            block.sync(mk("sp"))


D_MODEL = 1024
SEQ = 2048
N_META = 16
L = SEQ + N_META
NT = 17
DEPTH = 4
EPS = 1e-6
TP = [(0, 16)] + [(16 + 128 * (t - 1), 128) for t in range(1, NT)]
GR = [(0, 400, [0, 1, 2, 3]), (400, 512, [4, 5, 6, 7]), (912, 512, [8, 9, 10, 11]),
      (1424, 512, [12, 13, 14, 15]), (1936, 128, [16])]
TG = {}
for _g, (_p, _w, _ts) in enumerate(GR):
    for _t in _ts:
        TG[_t] = _g


def build_program(n_layers=DEPTH, n_seq=2, dbg=False):
    nc = bass.Bass("TRN2", target_bir_lowering=False)

    def din(name, shape, dt=F32):
        return nc.dram_tensor(name, list(shape), dt, kind="ExternalInput").ap()

    x_d = din("x", [n_seq, SEQ, D_MODEL])
    meta_d = din("meta", [N_META, D_MODEL])
    wm_d = din("wm", [DEPTH, 4, D_MODEL, 1280])
    wd_d = din("wd", [DEPTH, 8, D_MODEL, 512])
    wg_d = din("wg", [DEPTH, D_MODEL, 8])
    wo_d = din("wo", [DEPTH, 2048, D_MODEL])
    grow_d = din("grow", [DEPTH, 4, D_MODEL])
    convw_d = din("convw", [DEPTH, 128, 16, 4])
    convb_d = din("convb", [DEPTH, 128, 16])
    bg_d = din("bg", [DEPTH, 4, 2])
    lam_d = din("lam", [DEPTH, 1, 256])
    identb_d = din("identb", [128, 128], BF16)
    identf_d = din("identf", [128, 128])
    mask_d = din("mask", [128, 128], BF16)
    sel_d = din("sel", [4, 512])
    out_d = nc.dram_tensor("out", [n_seq, SEQ, D_MODEL], F32, kind="ExternalOutput").ap()
    xres_d = nc.dram_tensor("xres", [L, D_MODEL], F32).ap()
    hcat_d = nc.dram_tensor("hcat", [L, 2048], BF16).ap()

    with contextlib.ExitStack() as st:
        P = Prog(nc)

        def T(name, shape, dt):
            return st.enter_context(nc.sbuf_tensor("sb_" + name, list(shape), dt))

        hT = T("hT", [128, 8, L], BF16)
        Wb = [T("W0", [128, 8, 1280], BF16), T("W1", [128, 8, 1280], BF16)]
        raw = T("raw", [128, 3 + L + 1], F32)
        acc = T("acc", [128, L + 4], F32)
        mqs = [T("mq", [128, 2, L], BF16), T("mq1", [128, 2, L], BF16)]
        mks = [T("mk", [128, 2, L], BF16), T("mk1", [128, 2, L], BF16)]
        mq, mk = mqs[0], mks[0]
        dq = T("dq", [128, L], BF16)
        dk = T("dk", [128, L], BF16)
        dv = T("dv", [128, NT, 130], BF16)
        CTs = [T("CTa", [128, 2, 257], F32), T("CTb", [128, 2, 257], F32)]
        CTdb = T("CTdb", [128, 2, 257], BF16)
        grow = T("grow", [128, 4, D_MODEL], F32)
        identb = T("identb", [128, 128], BF16)
        identf = T("identf", [128, 128], F32)
        maskb = T("maskb", [128, 128], BF16)
        sel = T("sel", [4, 512], F32)
        convw = T("convw", [128, 16, 4], F32)
        convb = T("convb", [128, 16], F32)
        bg = T("bg", [4, 4], F32)
        lamt = T("lamt", [128, 256], F32)
        lams = T("lams", [128, 8], F32)
        gatesT = T("gatesT", [128, NT, 12], F32)
        decs = T("decs", [4, 2 * NT], F32)
        decb = T("decb", [128, 4, 2 * NT], F32)
        ssq = T("ssq", [128, NT, 12], F32)
        rsd = T("rsd", [128, 16], F32)
        ss = T("ss", [128, 8], F32)
        vext = [T("vext%d" % i_, [128, 258], BF16) for i_ in range(3)]
        th = [T("th%d" % i_, [128, 512], F32) for i_ in range(3)]
        zs = [T("zs%d" % i_, [128, 384], F32) for i_ in range(3)]
        kw = [T("kw%d" % i_, [128, 256], BF16) for i_ in range(3)]
        SwT = [T("SwT%d" % i_, [128, 128], BF16) for i_ in range(3)]
        hg = [T("hg0", [128, 384], F32), T("hg1", [128, 384], F32)]
        hout = [T("hout0", [128, 384], BF16), T("hout1", [128, 384], BF16)]
        sm = [T("sm0", [128, 8], F32), T("sm1", [128, 8], F32)]
        PT = [T("PT%d" % i_, [128, 512], BF16) for i_ in range(6)]
        a0 = [T("a00", [128, 384], F32), T("a01", [128, 384], F32)]
        dummy = T("dummy", [1, 8], F32)
        ps = [st.enter_context(nc.psum_tensor("ps%d" % i, [128, 512], F32)) for i in range(8)]
        psb = [p[:].bitcast(BF16) for p in ps]
        pb = [Buf("ps%d" % i) for i in range(8)]

        rawb = raw[:].bitcast(BF16)
        accb = acc[:].bitcast(BF16)
        hc = [rawb[:, 0:2048], rawb[:, 2048:4096]]
        hcT = accb[:, 0:2048].rearrange("p (f j) -> p f j", j=128)
        hb = accb[:, 2048:3072]
        junkb = accb[:, 3072:4096]
        mqf = mq[:].rearrange("p c l -> p (c l)").bitcast(F32)
        mkf = mk[:].rearrange("p c l -> p (c l)").bitcast(F32)
        xo = [mqf[:, 0:1024], mqf[:, 1024:2048]]
        xn = [mkf[:, 0:1024], mkf[:, 1024:2048]]
        T1 = raw[0:4, 4:4 + L]
        T2 = acc[0:4, 0:L]
        T3 = mqf[0:4, 0:L]

        b_hT = [Buf("hT%d" % t) for t in range(NT)]
        b_W = [Buf("W0"), Buf("W1")]
        b_raw, b_acc = Buf("raw"), Buf("acc")
        b_mqs = [[Buf("mq%d_%d" % (p_, g)) for g in range(5)] for p_ in range(2)]
        b_mks = [[Buf("mk%d_%d" % (p_, g)) for g in range(5)] for p_ in range(2)]
        b_mq, b_mk = b_mqs[0], b_mks[0]
        b_dq = [Buf("dq%d" % g) for g in range(5)]
        b_dk = [Buf("dk%d" % g) for g in range(5)]
        b_dv = [Buf("dv%d" % t) for t in range(NT)]
        b_CTs = [Buf(), Buf()]
        b_CTdb = Buf()
        b_grow = [Buf("g%d" % i) for i in range(4)]
        b_const = Buf("const")
        b_lp = Buf("layerparams")
        b_lam = Buf("lam")
        b_gT, b_decs, b_decb = Buf(), Buf(), Buf()
        b_ssq = [Buf("ssq%d" % t) for t in range(NT)]
        b_rsd, b_ss = Buf(), Buf()
        b_vext, b_th, b_zs, b_kw, b_SwT, b_hg, b_hout, b_sm = ([Buf(), Buf(), Buf()] for _ in range(8))
        b_PT = [Buf() for _ in range(6)]
        b_a0 = [Buf(), Buf()]
        b_hc, b_xo, b_xn = [Buf(), Buf()], [Buf(), Buf()], [Buf(), Buf()]
        b_hcT, b_hb, b_junk = Buf(), Buf(), Buf()
        b_dummy = Buf()
        b_xres = [Buf("xres%d" % t) for t in range(NT)]
        b_hcat = [Buf("hcat%d" % t) for t in range(NT)]
        b_out = Buf("out")

        def MM(out, lhsT, rhs, start, stop, R, W):
            P.add("pe", lambda e: e.matmul(out, lhsT, rhs, start=start, stop=stop), R, W)

        def TR(out, in_, ident, R, W):
            P.add("pe", lambda e: e.transpose(out, in_, ident), R, W)

        def ACT(out, in_, func, R, W, **kw_):
            P.add("act", lambda e: e.activation(out, in_, func, **kw_), R, W)

        def TS(out, in0, s1, s2, op0, op1, R, W, eng="dve"):
            if op1 is None:
                P.add(eng, lambda e: e.tensor_scalar(out, in0, s1, None, op0), R, W)
            else:
                P.add(eng, lambda e: e.tensor_scalar(out, in0, s1, s2, op0, op1), R, W)

        def TT(out, in0, in1, op, R, W, eng="dve"):
            P.add(eng, lambda e: e.tensor_tensor(out, in0, in1, op), R, W)

        def STT(out, in0, sc, in1, op0, op1, R, W):
            P.add("dve", lambda e: e.scalar_tensor_tensor(out, in0, sc, in1, op0, op1), R, W)

        def CP(out, in_, R, W, eng="dve"):
            if eng == "act":
                P.add("act", lambda e: e.copy(out, in_), R, W)
            else:
                P.add(eng, lambda e: e.tensor_copy(out, in_), R, W)

        def MS(ap, val, W, eng="pool"):
            P.add(eng, lambda e: e.memset(ap, val), [], W)

        def DMA(q, out, in_, R, W):
            P.add(q, lambda e: e.dma_start(out=out, in_=in_), R, W, dma=True)

        def FENCE(bufs):
            P.add("pool", lambda e: e.memset(dummy[0:1, 0:1], 0.0), [], list(bufs) + [b_dummy])

        MUL, ADD, SUB, MAX = ALU.mult, ALU.add, ALU.subtract, ALU.max

        DMA("sp", identb[:], identb_d, [], [b_const])
        DMA("sp", identf[:], identf_d, [], [b_const])
        DMA("sp", maskb[:], mask_d, [], [b_const])
        DMA("sp", sel[:], sel_d, [], [b_const])
        for i in range(3):
            MS(vext[i][:, 256:258], 1.0, [b_vext[i]])
        for t in range(NT):
            MS(dv[:, t, 128:130], 1.0, [b_dv[t]])
        MS(raw[:, 0:3], 0.0, [b_raw])

        M_BUFS = [b_raw, b_acc] + b_mqs[0] + b_mks[0] + b_mqs[1] + b_mks[1]
        O_BUFS = b_hc + b_xo + b_xn + [b_hcT, b_hb, b_junk]

        wslot = [0]

        def load_weights(src, ncols, krows=8):
            i = wslot[0]
            wslot[0] ^= 1
            for kc in range(krows):
                DMA("pool", Wb[i][:, kc, 0:ncols], src[kc * 128:(kc + 1) * 128, :], [], [b_W[i]])
            return Wb[i], b_W[i]

        def norm_tile(t, xt, bx, bank):
            pos0, n = TP[t]
            ACT(junkb[:n, :], xt, AF.Square, [bx], [b_junk, b_ss], accum_out=ss[:n, 0:1])
            ACT(ss[:n, 1:2], ss[:n, 0:1], AF.Ln, [b_ss], [b_ss], scale=1.0 / D_MODEL, bias=EPS)
            ACT(ss[:n, 2:3], ss[:n, 1:2], AF.Exp, [b_ss], [b_ss], scale=-0.5)
            STT(hb[:n, :], xt, ss[:n, 2:3], grow[:n, 0, :], MUL, MUL, [bx, b_ss, b_grow[0]], [b_hb])
            for kc in range(8):
                TR(psb[bank][:, kc * 128:kc * 128 + n], hb[:n, kc * 128:(kc + 1) * 128], identb[:n, :n],
                   [b_hb, b_const], [pb[bank]])
            src = psb[bank][:, 0:1024].rearrange("p (k j) -> p k j", j=128)[:, :, 0:n]
            CP(hT[:, :, pos0:pos0 + n], src, [pb[bank]], [b_hT[t]], eng="act")

        def load_gpre(l):
            DMA("sp", grow[:, 0, :], grow_d[l, 0:1, :].partition_broadcast(128), [], [b_grow[0]])

        def load_layer_params(l):
            for i in range(1, 4):
                DMA("sp", grow[:, i, :], grow_d[l, i:i + 1, :].partition_broadcast(128), [], [b_grow[i]])
            DMA("sp", convw[:], convw_d[l], [], [b_lp])
            DMA("sp", convb[:], convb_d[l], [], [b_lp])
            DMA("sp", bg[:, 0:2], bg_d[l], [], [b_lp])
            DMA("sp", lamt[:], lam_d[l].partition_broadcast(128), [], [b_lam])
            TS(bg[:, 2:3], bg[:, 1:2], -1.0, None, MUL, None, [b_lp], [b_lp])
            TS(grow[:, 2, :], grow[:, 2, :], 0.25, None, MUL, None, [b_grow[2]], [b_grow[2]], eng="pool")
            lam_init = 0.8 - 0.6 * math.exp(-0.3 * l)
            TT(lamt[:, 0:64], lamt[:, 0:64], lamt[:, 64:128], MUL, [b_lam], [b_lam])
            TT(lamt[:, 128:192], lamt[:, 128:192], lamt[:, 192:256], MUL, [b_lam], [b_lam])
            P.add("dve", lambda e: e.reduce_sum(lams[:, 0:1], lamt[:, 0:64], axis=AX.X), [b_lam], [b_lam])
            P.add("dve", lambda e: e.reduce_sum(lams[:, 1:2], lamt[:, 128:192], axis=AX.X), [b_lam], [b_lam])
            ACT(lams[:, 2:4], lams[:, 0:2], AF.Exp, [b_lam], [b_lam])
            TT(lams[:, 4:5], lams[:, 2:3], lams[:, 3:4], SUB, [b_lam], [b_lam])
            TS(lams[:, 4:5], lams[:, 4:5], lam_init, None, ADD, None, [b_lam], [b_lam])
            TS(lams[:, 5:6], lams[:, 4:5], -1.0, None, MUL, None, [b_lam], [b_lam])
            return lam_init

        def gates_phase(l):
            W, bW = load_weights(wg_d[l], 8)
            allm = M_BUFS
            for g, (p0, w, ts) in enumerate(GR):
                bi, bf_ = (0, 1) if g % 2 == 0 else (2, 3)
                hr = [b_hT[t] for t in ts]
                for kc in range(8):
                    MM(ps[bi][0:4, 0:w], W[:, kc, 0:4], hT[:, kc, p0:p0 + w], kc == 0, kc == 7, hr + [bW], [pb[bi]])
                for kc in range(8):
                    MM(ps[bf_][0:4, 0:w], W[:, kc, 4:8], hT[:, kc, p0:p0 + w], kc == 0, kc == 7, hr + [bW], [pb[bf_]])
                TS(T1[:, p0:p0 + w], ps[bi][0:4, 0:w], bg[:, 0:1], None, ADD, None, [pb[bi], b_lp], allm)
                ACT(T2[:, p0:p0 + w], ps[bf_][0:4, 0:w], AF.Exp, [pb[bf_], b_lp], allm, scale=-1.0, bias=bg[:, 2:3])
            ACT(T2, T2, AF.Ln, allm, allm, bias=1.0)
            P.add("dve", lambda e: e.tensor_tensor_scan(T3, T2, T2, 0.0, ADD, MAX), allm, allm)
            TT(T1, T1, T3, ADD, allm, allm)
            P.add("dve", lambda e: e.tensor_tensor_scan(T2, T1, T1, 0.0, MAX, MAX), allm, allm)
            ge = T2[:, 15:L:128]
            TS(decs[:, 0:1], T2[:, 15:16], -1.0, None, MUL, None, allm, [b_decs])
            TT(decs[:, 1:NT], T2[:, 15:L - 128:128], T2[:, 143:L:128], SUB, allm, [b_decs])
            ACT(decs[:, 0:NT], decs[:, 0:NT], AF.Exp, [b_decs], [b_decs])
            TS(decs[:, NT:2 * NT], decs[:, 0:NT], 1.0 / 16, None, MUL, None, [b_decs], [b_decs])
            gl = T2[:, 143:L:128].unsqueeze(2).to_broadcast([4, 16, 128])
            for Tx in (T1, T3):
                TT(Tx[:, 16:L].rearrange("p (t j) -> p t j", j=128), Tx[:, 16:L].rearrange("p (t j) -> p t j", j=128),
                   gl, SUB, allm, allm)
                TS(Tx[:, 0:16], Tx[:, 0:16], T2[:, 15:16], None, SUB, None, allm, allm)
                ACT(Tx, Tx, AF.Exp, allm, allm)
            for h in range(4):
                MM(ps[4][:, h * 2 * NT:(h + 1) * 2 * NT], sel[:, h * 128:(h + 1) * 128], decs[:, :], True, True,
                   [b_const, b_decs], [pb[4]])
            CP(decb[:].rearrange("p h c -> p (h c)"), ps[4][:, 0:8 * NT], [pb[4]], [b_decb])
            for t in range(NT):
                p0, n = TP[t]
                TR(ps[5][:n, t * 8:t * 8 + 4], T1[:, p0:p0 + n], identf[0:4, 0:4], allm + [b_const], [pb[5]])
                TR(ps[5][:n, t * 8 + 4:t * 8 + 8], T3[:, p0:p0 + n], identf[0:4, 0:4], allm + [b_const], [pb[5]])
            CP(gatesT[0:16, 0, 0:8], ps[5][0:16, 0:8], [pb[5]], [b_gT])
            CP(gatesT[:, 1:NT, 0:8], ps[5][:, 8:8 * NT].rearrange("p (t c) -> p t c", c=8), [pb[5]], [b_gT])
            TS(gatesT[0:16, 0, 8:12], gatesT[0:16, 0, 0:4], 1.0 / 16, None, MUL, None, [b_gT], [b_gT])
            TS(gatesT[:, 1:NT, 8:12], gatesT[:, 1:NT, 0:4], 1.0 / 16, None, MUL, None, [b_gT], [b_gT])

        def mlstm_prep_gen(l, h, W, bW, banks):
            hp = h % 2
            MS(raw[:, 0:3], 0.0, [b_raw])
            bi = 0
            for c in range(4):
                dst, bdst = (mqs[hp], b_mqs[hp]) if c < 2 else (mks[hp], b_mks[hp])
                cc = (0 if c < 2 else 8) + h * 2 + (c % 2)
                for g, (p0, w, ts) in enumerate(GR):
                    bk = banks[bi % len(banks)]
                    bi += 1
                    hr_ = [b_hT[t] for t in ts]
                    for kc in range(8):
                        MM(ps[bk][:, 0:w], W[:, kc, c * 128:(c + 1) * 128], hT[:, kc, p0:p0 + w], kc == 0, kc == 7,
                           hr_ + [bW], [pb[bk]])
                    CP(raw[:, 3 + p0:3 + p0 + w], ps[bk][:, 0:w], [pb[bk]], [b_raw], eng="act")
                    yield
                TS(acc[:, 0:L], raw[:, 3:3 + L], convw[:, cc, 3:4], None, MUL, None, [b_raw, b_lp], [b_acc])
                yield
                for j in (2, 1, 0):
                    STT(acc[:, 0:L], raw[:, j:j + L], convw[:, cc, j:j + 1], acc[:, 0:L], MUL, ADD,
                        [b_raw, b_acc, b_lp], [b_acc])
                    yield
                ACT(dst[:, c % 2, :], acc[:, 0:L], AF.Silu, [b_acc, b_lp], bdst, bias=convb[:, cc:cc + 1])
                yield

        def mlstm_head(l, h, W, bW, hook=None):
            hp = h % 2
            mq, mk, b_mq, b_mk = mqs[hp], mks[hp], b_mqs[hp], b_mks[hp]
            MS(CTs[0][:], 0.0, [b_CTs[0]])

            def A_pe(t):
                p0, n = TP[t]
                g = TG[t]
                for kc in range(8):
                    MM(ps[0][:n, 0:256], hT[:, kc, p0:p0 + n], W[:, kc, 512:768], kc == 0, kc == 7, [b_hT[t], bW], [pb[0]])
                for kc in range(8):
                    MM(ps[1][:n, 0:512], hT[:, kc, p0:p0 + n], W[:, kc, 768:1280], kc == 0, kc == 7, [b_hT[t], bW], [pb[1]])
                for c in range(2):
                    TR(psb[2][:n, c * 128:(c + 1) * 128], mk[:, c, p0:p0 + n], identb[:, :], [b_mk[g], b_const], [pb[2]])
                for c in range(2):
                    MM(ps[3][:n, 0:n], mk[:, c, p0:p0 + n], mq[:, c, p0:p0 + n], c == 0, c == 1, [b_mk[g], b_mq[g]], [pb[3]])

            def A_other(t):
                p0, n = TP[t]
                i = t % 3
                CP(vext[i][:n, 0:256], ps[0][:n, 0:256], [pb[0]], [b_vext[i]], eng="act")
                ACT(th[i][:n, :], ps[1][:n, :], AF.Tanh, [pb[1]], [b_th[i]], scale=0.5)
                TS(kw[i][:n, :], psb[2][:n, 0:256], gatesT[:n, t, h:h + 1], None, MUL, None, [pb[2], b_gT], [b_kw[i]])
                STT(SwT[i][:n, :n], ps[3][:n, 0:n], gatesT[:n, t, 8 + h:9 + h], maskb[:n, :n], MUL, MUL,
                    [pb[3], b_gT, b_const], [b_SwT[i]])
                STT(zs[i][:n, 0:256], th[i][:n, 256:512], 1.0, ps[1][:n, 256:512], ADD, MUL, [pb[1], b_th[i]], [b_zs[i]])
                TT(zs[i][:n, 0:256], zs[i][:n, 0:256], grow[:n, 2, h * 256:(h + 1) * 256], MUL, [b_zs[i], b_grow[2]], [b_zs[i]])

            def CTDB(t):
                CTo, bCTo = CTs[t % 2], b_CTs[t % 2]
                TS(CTdb[:], CTo[:], decb[:, h, NT + t:NT + t + 1], None, MUL, None, [bCTo, b_decb], [b_CTdb])

            def B_pe(t):
                p0, n = TP[t]
                g = TG[t]
                ia = t % 3
                for c in range(2):
                    MM(ps[5 + c][:, 0:257], kw[ia][:n, c * 128:(c + 1) * 128], vext[ia][:n, 0:257], True, True,
                       [b_kw[ia], b_vext[ia]], [pb[5 + c]])
                MM(ps[4][:n, 0:257], SwT[ia][:n, :n], vext[ia][:n, 0:257], True, False, [b_SwT[ia], b_vext[ia]], [pb[4]])
                for c in range(2):
                    MM(ps[4][:n, 0:257], mq[:, c, p0:p0 + n], CTdb[:, c, :], False, c == 1, [b_mq[g], b_CTdb], [pb[4]])

            def B_rest(t):
                p0, n = TP[t]
                i = t % 2
                CTo, bCTo = CTs[t % 2], b_CTs[t % 2]
                CTn, bCTn = CTs[(t + 1) % 2], b_CTs[(t + 1) % 2]
                for c in range(2):
                    STT(CTn[:, c, :], CTo[:, c, :], decb[:, h, t:t + 1], ps[5 + c][:, 0:257], MUL, ADD,
                        [pb[5 + c], bCTo, b_decb], [bCTn])
                TT(sm[i][:n, 0:1], ps[4][:n, 256:257], gatesT[:n, t, 4 + h:5 + h], MAX, [pb[4], b_gT], [b_sm[i]])
                STT(sm[i][:n, 0:1], ps[4][:n, 256:257], -1.0, sm[i][:n, 0:1], MUL, MAX, [pb[4], b_sm[i]], [b_sm[i]])
                P.add("dve", lambda e, o_=sm[i][:n, 1:2], i_=sm[i][:n, 0:1]: e.reciprocal(o_, i_), [b_sm[i]], [b_sm[i]])
                ACT(hrw[i][:n, :], ps[4][:n, 0:256], AF.Copy, [pb[4], b_sm[i]], [b_hr[i]], scale=sm[i][:n, 1:2])
                if t + 1 < NT:
                    CTDB(t + 1)

            def B2(t):
                p0, n = TP[t]
                i = t % 2
                ia = t % 3
                STT(hg[i][:n, 0:256], th[ia][:n, 0:256], 1.0, hrw[i][:n, :], ADD, MUL, [b_th[ia], b_hr[i]], [b_hg[i]])
                ACT(junkh[i][:n, :], hg[i][:n, 0:256], AF.Square, [b_hg[i]], [b_junkh[i], b_ssq[t]], accum_out=ssq[:n, t, h:h + 1])
                TT(hout[i][:n, 0:256], hg[i][:n, 0:256], zs[ia][:n, 0:256], MUL, [b_hg[i], b_zs[ia]], [b_hout[i]])
                DMA("sp", hcat_d[p0:p0 + n, h * 256:(h + 1) * 256], hout[i][:n, 0:256], [b_hout[i]], [b_hcat[t]])

            CTDB(0)
            A_pe(0)
            A_other(0)
            A_pe(1)
            A_other(1)
            for t in range(2, NT):
                B_pe(t - 2)
                A_pe(t)
                B_rest(t - 2)
                A_other(t)
                B2(t - 2)
                if hook is not None:
                    hook(t)
            for t in (NT - 2, NT - 1):
                B_pe(t)
                B_rest(t)
                B2(t)

        junkh = [T("junkh0", [128, 256], BF16), T("junkh1", [128, 256], BF16)]
        b_junkh = [Buf(), Buf()]
        hrw = [T("hr0", [128, 256], F32), T("hr1", [128, 256], F32)]
        b_hr = [Buf(), Buf()]

        def diff_head(l, h, W, bW, lam_init):
            scale = 64 ** -0.5
            bankrot = [0]
            for c, (dst, bdst) in enumerate(((dq, b_dq), (dk, b_dk))):
                for g, (p0, w, ts) in enumerate(GR):
                    bk = bankrot[0]
                    bankrot[0] ^= 1
                    hr = [b_hT[t] for t in ts]
                    for kc in range(8):
                        MM(ps[bk][:, 0:w], W[:, kc, c * 128:(c + 1) * 128], hT[:, kc, p0:p0 + w], kc == 0, kc == 7,
                           hr + [bW], [pb[bk]])
                    CP(dst[:, p0:p0 + w], ps[bk][:, 0:w], [pb[bk]], [bdst[g]], eng=("act" if g % 2 else "dve"))
            for t in range(NT):
                p0, n = TP[t]
                bk = t % 2
                for kc in range(8):
                    MM(ps[bk][:n, 0:128], hT[:, kc, p0:p0 + n], W[:, kc, 256:384], kc == 0, kc == 7, [b_hT[t], bW], [pb[bk]])
                CP(dv[:n, t, 0:128], ps[bk][:n, 0:128], [pb[bk]], [b_dv[t]], eng=("act" if t % 2 else "dve"))
            QG = [[0]] + [list(range(a_, min(a_ + 3, NT))) for a_ in range(1, NT, 3)]
            items = []
            for gq, Q in enumerate(QG):
                items.append(("z", gq, None, False))
                for j in range(0, Q[-1] + 1):
                    items.append(("a", gq, j, j == Q[-1]))
            SBP = [(0, 1), (2, 3)]
            cfac = 0.5 * (1.0 - lam_init)

            def S(k):
                ty, gq, j, last = items[k]
                Q = QG[gq]
                nq = TP[Q[0]][1]
                sb = SBP[k % 2]
                if ty == "z":
                    for slot, t in enumerate(Q):
                        p0, n = TP[t]
                        for kc in range(8):
                            MM(ps[sb[0]][:n, slot * 128:slot * 128 + 128], hT[:, kc, p0:p0 + n], W[:, kc, 384:512], kc == 0, kc == 7,
                               [b_hT[t], bW], [pb[sb[0]]])
                    return
                qs = [t for t in Q if t >= j]
                ps0 = TP[qs[0]][0]
                wq = sum(TP[t][1] for t in qs)
                kp0, kn = TP[j]
                for c in range(2):
                    cs = slice(64 * c, 64 * c + 64)
                    MM(ps[sb[c]][:kn, 0:wq], dk[cs, kp0:kp0 + kn], dq[cs, ps0:ps0 + wq], True, True,
                       [b_dk[TG[j]]] + [b_dq[TG[t]] for t in qs], [pb[sb[c]]])

            def E(k):
                ty, gq, j, last = items[k]
                Q = QG[gq]
                nq = TP[Q[0]][1]
                gp = gq % 2
                sb = SBP[k % 2]
                if ty == "z":
                    wz = len(Q) * 128
                    ACT(th[gp][:nq, 0:wz], ps[sb[0]][:nq, 0:wz], AF.Tanh, [pb[sb[0]]], [b_th[gp]], scale=0.5)
                    TS(th[gp][:nq, 0:wz], th[gp][:nq, 0:wz], cfac, cfac, MUL, ADD, [b_th[gp]], [b_th[gp]])
                    TT(zs[gp][:nq, 0:wz], ps[sb[0]][:nq, 0:wz], th[gp][:nq, 0:wz], MUL, [pb[sb[0]], b_th[gp]], [b_zs[gp]])
                    zv = zs[gp][:nq, 0:wz].rearrange("p (s d) -> p s d", d=128)
                    TT(zv, zv, grow[:nq, 3, h * 128:(h + 1) * 128].unsqueeze(1).to_broadcast([nq, len(Q), 128]), MUL,
                       [b_zs[gp], b_grow[3]], [b_zs[gp]])
                    return
                qs = [t for t in Q if t >= j]
                wq = sum(TP[t][1] for t in qs)
                kp0, kn = TP[j]
                for c in range(2):
                    pi = (k % 3) * 2 + c
                    ACT(PT[pi][:kn, 0:wq], ps[sb[c]][:kn, 0:wq], AF.Exp, [pb[sb[c]]], [b_PT[pi]], scale=scale)
                    if j >= Q[0]:
                        TT(PT[pi][:kn, 0:kn], PT[pi][:kn, 0:kn], maskb[:kn, :kn], MUL, [b_PT[pi], b_const], [b_PT[pi]], eng="pool")

            def V(k):
                ty, gq, j, last = items[k]
                if ty == "z":
                    return
                Q = QG[gq]
                nq = TP[Q[0]][1]
                nS = len(Q)
                gp = gq % 2
                pa = [4, 5] if gp == 0 else [6, 7]
                qs = [t for t in Q if t >= j]
                kp0, kn = TP[j]
                for bi_, t in enumerate(qs):
                    slot = t - Q[0]
                    n = TP[t][1]
                    for c in range(2):
                        pi = (k % 3) * 2 + c
                        P.add("pe", lambda e, o_=ps[pa[c]][:n, slot * 129:slot * 129 + 129], l_=PT[pi][:kn, bi_ * 128:bi_ * 128 + n],
                              r_=dv[:kn, j, 0:129], st_=(j == 0 and slot == 0), sp_=(j == t):
                              e.matmul(o_, l_, r_, start=st_, stop=sp_, skip_group_check=True),
                              [b_PT[pi], b_dv[j]], [pb[pa[c]]])
                if not last:
                    return
                A0 = ps[pa[0]][:nq, 0:nS * 129].rearrange("p (s d) -> p s d", d=129)
                A1 = ps[pa[1]][:nq, 0:nS * 129].rearrange("p (s d) -> p s d", d=129)
                r0 = sm[gp][:nq, 0:nS]
                r1 = sm[gp][:nq, 4:4 + nS]
                P.add("dve", lambda e: e.reciprocal(r0.unsqueeze(2), A0[:, :, 128:129]), [pb[pa[0]]], [b_sm[gp]])
                P.add("dve", lambda e: e.reciprocal(r1.unsqueeze(2), A1[:, :, 128:129]), [pb[pa[1]]], [b_sm[gp]])
                TS(r1, r1, lams[:nq, 5:6], None, MUL, None, [b_sm[gp], b_lam], [b_sm[gp]])
                a0v = a0[gp][:nq, 0:nS * 128].rearrange("p (s d) -> p s d", d=128)
                hgv = hg[gp][:nq, 0:nS * 128].rearrange("p (s d) -> p s d", d=128)
                TT(a0v, A0[:, :, 0:128], r0.unsqueeze(2).to_broadcast([nq, nS, 128]), MUL, [pb[pa[0]], b_sm[gp]], [b_a0[gp]])
                TT(hgv, A1[:, :, 0:128], r1.unsqueeze(2).to_broadcast([nq, nS, 128]), MUL, [pb[pa[1]], b_sm[gp]], [b_hg[gp]])
                TT(hgv, hgv, a0v, ADD, [b_hg[gp], b_a0[gp]], [b_hg[gp]])
                for slot, t in enumerate(Q):
                    ACT(junkh[gp][:nq, 0:128], hg[gp][:nq, slot * 128:(slot + 1) * 128], AF.Square, [b_hg[gp]],
                        [b_junkh[gp], b_ssq[t]], accum_out=ssq[:nq, t, 4 + h:5 + h])
                TT(hout[gp][:nq, 0:nS * 128], hg[gp][:nq, 0:nS * 128], zs[gp][:nq, 0:nS * 128], MUL, [b_hg[gp], b_zs[gp]], [b_hout[gp]])
                p0 = TP[Q[0]][0]
                cols = slice(1024 + h * 128, 1024 + (h + 1) * 128)
                if nS == 1:
                    DMA("sp", hcat_d[p0:p0 + nq, cols], hout[gp][:nq, 0:128], [b_hout[gp]], [b_hcat[t] for t in Q])
                else:
                    DMA("sp", hcat_d[p0:p0 + nS * 128, cols].rearrange("(s p) d -> p s d", p=128),
                        hout[gp][:nq, 0:nS * 128].rearrange("p (s d) -> p s d", d=128), [b_hout[gp]], [b_hcat[t] for t in Q])

            NI = len(items)
            for k in range(NI):
                S(k)
                E(k)
                if k >= 1:
                    V(k - 1)
            V(NI - 1)

        def out_phase(l, s, last):
            FENCE(M_BUFS + O_BUFS)
            WA, bWA = load_weights(wo_d[l, 0:1024, :], 1024)
            WB, bWB = load_weights(wo_d[l, 1024:2048, :], 1024)
            if not last:
                load_gpre(l + 1)
            def O1(t):
                p0, n = TP[t]
                i = t % 2
                DMA("sp", hc[i][:n, :], hcat_d[p0:p0 + n, :], [b_hcat[t]], [b_hc[i]])
                if l == 0:
                    src = meta_d if t == 0 else x_d[s, p0 - 16:p0 - 16 + n, :]
                    DMA("sp", xo[i][:n, :], src, [], [b_xo[i]])
                else:
                    DMA("sp", xo[i][:n, :], xres_d[p0:p0 + n, :], [b_xres[t]], [b_xo[i]])
                ACT(rsd[:n, 0:4], ssq[:n, t, 0:4], AF.Ln, [b_ssq[t]], [b_rsd], scale=1.0 / 1024, bias=EPS)
                ACT(rsd[:n, 4:12], ssq[:n, t, 4:12], AF.Ln, [b_ssq[t]], [b_rsd], scale=1.0 / 128, bias=EPS)
                ACT(rsd[:n, 0:12], rsd[:n, 0:12], AF.Exp, [b_rsd], [b_rsd], scale=-0.5)
                TT(hc[i][:n, 0:1024].rearrange("p (h d) -> p h d", d=256), hc[i][:n, 0:1024].rearrange("p (h d) -> p h d", d=256),
                   rsd[:n, 0:4].unsqueeze(2).to_broadcast([n, 4, 256]), MUL, [b_hc[i], b_rsd], [b_hc[i]])
                TT(hc[i][:n, 1024:2048].rearrange("p (h d) -> p h d", d=128), hc[i][:n, 1024:2048].rearrange("p (h d) -> p h d", d=128),
                   rsd[:n, 4:12].unsqueeze(2).to_broadcast([n, 8, 128]), MUL, [b_hc[i], b_rsd], [b_hc[i]])
                for half in range(2):
                    for f in range(8):
                        fc = half * 8 + f
                        TR(psb[half][:, f * 128:f * 128 + n], hc[i][:n, fc * 128:(fc + 1) * 128], identb[:n, :n],
                           [b_hc[i], b_const], [pb[half]])
                    src = psb[half][:, 0:1024].rearrange("p (k j) -> p k j", j=128)[:, :, 0:n]
                    CP(hcT[:, half * 8:(half + 1) * 8, 0:n], src, [pb[half]], [b_hcT], eng=("act" if half else "dve"))
                ya = [2, 3] if i == 0 else [4, 5]
                for half in range(2):
                    for fc in range(16):
                        Wx, bWx = (WA, bWA) if fc < 8 else (WB, bWB)
                        MM(ps[ya[half]][:n, :], hcT[:, fc, 0:n], Wx[:, fc % 8, half * 512:(half + 1) * 512], fc == 0, fc == 15,
                           [b_hcT, bWx], [pb[ya[half]]])

            def O2(t):
                p0, n = TP[t]
                i = t % 2
                ya = [2, 3] if i == 0 else [4, 5]
                for half in range(2):
                    ACT(junkb[:n, 0:512], ps[ya[half]][:n, :], AF.Square, [pb[ya[half]]], [b_junk, b_ss],
                        accum_out=ss[:n, 4 + half:5 + half])
                TT(ss[:n, 6:7], ss[:n, 4:5], ss[:n, 5:6], ADD, [b_ss], [b_ss])
                ACT(ss[:n, 6:7], ss[:n, 6:7], AF.Ln, [b_ss], [b_ss], scale=1.0 / D_MODEL, bias=EPS)
                ACT(ss[:n, 7:8], ss[:n, 6:7], AF.Exp, [b_ss], [b_ss], scale=-0.5)
                for half in range(2):
                    hs = slice(half * 512, (half + 1) * 512)
                    STT(xn[i][:n, hs], ps[ya[half]][:n, :], ss[:n, 7:8], grow[:n, 1, hs], MUL, MUL,
                        [pb[ya[half]], b_ss, b_grow[1]], [b_xn[i]])
                    TT(xn[i][:n, hs], xn[i][:n, hs], xo[i][:n, hs], ADD, [b_xn[i], b_xo[i]], [b_xn[i]])
                if last:
                    if t > 0:
                        DMA("sp", out_d[s, p0 - 16:p0 - 16 + n, :], xn[i][:n, :], [b_xn[i]], [b_out])
                else:
                    DMA("sp", xres_d[p0:p0 + n, :], xn[i][:n, :], [b_xn[i]], [b_xres[t]])
                    norm_tile(t, xn[i][:n, :], b_xn[i], 6 + i)

            O1(0)
            for t in range(1, NT):
                O1(t)
                O2(t - 1)
            O2(NT - 1)
            FENCE(M_BUFS + O_BUFS)

        for s in range(n_seq):
            load_gpre(0)
            FENCE(M_BUFS + O_BUFS)
            for t in range(NT):
                p0, n = TP[t]
                i = t % 2
                src = meta_d if t == 0 else x_d[s, p0 - 16:p0 - 16 + n, :]
                DMA("sp", xo[i][:n, :], src, [], [b_xo[i]])
                norm_tile(t, xo[i][:n, :], b_xo[i], 6 + i)
            FENCE(M_BUFS + O_BUFS)
            for l in range(n_layers):
                lam_init = load_layer_params(l)
                gates_phase(l)
                nxt = load_weights(wm_d[l, 0], 1280)
                for _ in mlstm_prep_gen(l, 0, nxt[0], nxt[1], [0, 1]):
                    pass
                for h in range(4):
                    W, bW = nxt
                    nxt = load_weights(wm_d[l, h + 1], 1280) if h < 3 else load_weights(wd_d[l, 0], 512)
                    hook = None
                    gen = None
                    if h < 3:
                        gen = mlstm_prep_gen(l, h + 1, nxt[0], nxt[1], [7])

                        def hook(t, gen=gen):
                            if t >= 7:
                                for _ in range(4):
                                    next(gen, None)
                    mlstm_head(l, h, W, bW, hook)
                    if gen is not None:
                        for _ in gen:
                            pass
                for h in range(8):
                    W, bW = nxt
                    if h < 7:
                        nxt = load_weights(wd_d[l, h + 1], 512)
                    diff_head(l, h, W, bW, lam_init)
                out_phase(l, s, l == n_layers - 1)
        if os.environ.get('MK_SBUF'):
            print('SBUF remaining', nc.sbuf_bytes_remaining)
        P.emit(st)
    return nc


_CACHE = {}


def _host_layout(inp):
    bf = ml_dtypes.bfloat16
    w_in = np.asarray(inp["w_in"], dtype=np.float32)
    D = 1024
    sec = lambda k: w_in[:, :, k * D:(k + 1) * D] if k < 5 else None
    qm, km, vm, om, zm = (w_in[:, :, k * D:(k + 1) * D] for k in range(5))
    wg = np.ascontiguousarray(w_in[:, :, 5 * D:5 * D + 8])
    off = 5 * D + 8
    qd, kd, vd, zd = (w_in[:, :, off + k * D:off + (k + 1) * D] for k in range(4))
    wm = np.empty((DEPTH, 4, D, 1280), np.float32)
    for h in range(4):
        hs = slice(h * 256, (h + 1) * 256)
        wm[:, h] = np.concatenate([qm[:, :, hs], km[:, :, hs], vm[:, :, hs], om[:, :, hs], zm[:, :, hs]], axis=-1)
    wd = np.empty((DEPTH, 8, D, 512), np.float32)
    for h in range(8):
        hs = slice(h * 128, (h + 1) * 128)
        wd[:, h] = np.concatenate([qd[:, :, hs], kd[:, :, hs], vd[:, :, hs], zd[:, :, hs]], axis=-1)
    grow = np.stack([inp["pre_norm_g"], inp["post_norm_g"], inp["mlstm_norm_g"], inp["diff_norm_g"]], axis=1).astype(np.float32)
    convw = np.ascontiguousarray(np.asarray(inp["conv_w"], np.float32).transpose(0, 2, 1).reshape(DEPTH, 16, 128, 4).transpose(0, 2, 1, 3))
    convb = np.ascontiguousarray(np.asarray(inp["conv_b"], np.float32).reshape(DEPTH, 16, 128).transpose(0, 2, 1))
    bgt = np.ascontiguousarray(np.asarray(inp["b_gates"], np.float32).reshape(DEPTH, 2, 4).transpose(0, 2, 1))
    lam = np.concatenate([inp["lambda_q1"], inp["lambda_k1"], inp["lambda_q2"], inp["lambda_k2"]], axis=-1).astype(np.float32)[:, None, :]
    sel = np.zeros((4, 512), np.float32)
    for h in range(4):
        sel[h, h * 128:(h + 1) * 128] = 1.0
    mask = np.triu(np.ones((128, 128), np.float32)).astype(bf)
    common = {
        "meta": np.ascontiguousarray(inp["meta_tokens"], dtype=np.float32),
        "wm": wm, "wd": wd, "wg": wg, "wo": np.ascontiguousarray(inp["w_out"], dtype=np.float32),
        "grow": np.ascontiguousarray(grow), "convw": convw, "convb": convb, "bg": bgt, "lam": np.ascontiguousarray(lam),
        "identb": np.eye(128, dtype=np.float32).astype(bf), "identf": np.eye(128, dtype=np.float32),
        "mask": mask, "sel": sel,
    }
    return common


def kernel(**inputs):
    n_layers = int(os.environ.get("MK_LAYERS", DEPTH))
    key = ("nc", n_layers)
    if key not in _CACHE:
        _CACHE[key] = build_program(n_layers=n_layers, n_seq=2)
    nc = _CACHE[key]
    common = _host_layout(inputs)
    x = np.asarray(inputs["x"], dtype=np.float32)
    in_maps = []
    for c in range(8):
        m = dict(common)
        m["x"] = np.ascontiguousarray(x[2 * c:2 * c + 2])
        in_maps.append(m)
    res = run_bass_kernel_spmd(nc, in_maps, core_ids=list(range(8)))
    out = np.concatenate([r["out"] for r in res.results], axis=0)
    return out.astype(np.float32)
```

```python
import contextlib
import math
import os
import numpy as np
import ml_dtypes
import concourse.bass as bass
import concourse.mybir as mybir
from concourse.bass_utils import run_bass_kernel_spmd

F32 = mybir.dt.float32
BF16 = mybir.dt.bfloat16
AF = mybir.ActivationFunctionType
ALU = mybir.AluOpType
AX = mybir.AxisListType

ENGS = ["pe", "act", "dve", "pool", "sp"]


class Buf:
    __slots__ = ("n", "w", "r")

    def __init__(self, n=""):
        self.n = n
        self.w = None
        self.r = []


class Op:
    __slots__ = ("eng", "fn", "deps", "sig", "tok", "dma")


class Prog:
    NDMA = 8
    ROLL = 30000

    def __init__(self, nc):
        self.nc = nc
        self.ops = {e: [] for e in ENGS}
        self.dq = {e: {"next": 0, "last": [None] * self.NDMA, "cnt": [0] * self.NDMA} for e in ENGS}
        self.all_dma = []

    def add(self, eng, fn, reads=(), writes=(), dma=False):
        o = Op()
        o.eng, o.fn, o.dma, o.sig, o.tok = eng, fn, dma, False, None
        cand = []
        for b in reads:
            if b.w is not None:
                cand.append((b.w, True))
        for b in writes:
            if b.w is not None:
                cand.append((b.w, False))
            for r in b.r:
                cand.append((r, False))
        deps, seen = [], set()
        if dma:
            q = self.dq[eng]
            s = q["next"]
            q["next"] = (s + 1) % self.NDMA
            if q["last"][s] is not None:
                cand.append((q["last"][s], True))
            q["cnt"][s] += 16
            o.tok = (("d", eng, s), q["cnt"][s])
            q["last"][s] = o
            self.all_dma.append(o)
        for p, raw in cand:
            if p is o or id(p) in seen:
                continue
            if (not dma) and (not p.dma) and p.eng == eng:
                if eng == "pe":
                    continue
            seen.add(id(p))
            deps.append(p)
            if not p.dma:
                p.sig = True
        o.deps = deps
        for b in reads:
            if dma:
                b.r.append(o)
            else:
                b.r = [x for x in b.r if x.dma or x.eng != eng] + [o]
        for b in writes:
            b.w = o
            b.r = []
        self.ops[eng].append(o)
        return o

    def emit(self, stack):
        nc = self.nc
        keys = []
        for e in ENGS:
            cnt, gen, used = 0, 0, False
            for o in self.ops[e]:
                if o.dma or not o.sig:
                    continue
                cnt += 1
                if cnt > self.ROLL:
                    gen += 1
                    cnt = 1
                o.tok = (("c", e, gen), cnt)
                used = True
            if used:
                for g in range(gen + 1):
                    keys.append(("c", e, g))
            for s in range(self.NDMA):
                if self.dq[e]["cnt"][s]:
                    keys.append(("d", e, s))
        sems = {k: stack.enter_context(nc.semaphore("s_%s_%s_%d" % k)) for k in keys}
        final = {}
        for o in self.all_dma:
            final[o.tok[0]] = max(final.get(o.tok[0], 0), o.tok[1])

        def mk(e):
            def body(engine):
                waited = {}
                for o in self.ops[e]:
                    for p in o.deps:
                        k, v = p.tok
                        if waited.get(k, 0) < v:
                            engine.wait_ge(sems[k], v)
                            waited[k] = v
                    ins = o.fn(engine)
                    if o.dma:
                        ins.then_inc(sems[o.tok[0]], 16)
                    elif o.sig:
                        ins.then_inc(sems[o.tok[0]], 1)
                if e == "sp":
                    for k, v in final.items():
                        if waited.get(k, 0) < v:
                            engine.wait_ge(sems[k], v)
            return body

        with nc.Block() as block:
            block.tensor(mk("pe"))
            block.scalar(mk("act"))
            block.vector(mk("dve"))
            block.gpsimd(mk("pool"))
            block.sync(mk("sp"))


D_MODEL = 1024
SEQ = 2048
N_META = 16
L = SEQ + N_META
NT = 17
DEPTH = 4
EPS = 1e-6
TP = [(0, 16)] + [(16 + 128 * (t - 1), 128) for t in range(1, NT)]
GR = [(0, 400, [0, 1, 2, 3]), (400, 512, [4, 5, 6, 7]), (912, 512, [8, 9, 10, 11]),
      (1424, 512, [12, 13, 14, 15]), (1936, 128, [16])]
TG = {}
for _g, (_p, _w, _ts) in enumerate(GR):
    for _t in _ts:
        TG[_t] = _g


def build_program(n_layers=DEPTH, n_seq=2, dbg=False):
    nc = bass.Bass("TRN2", target_bir_lowering=False)

    def din(name, shape, dt=F32):
        return nc.dram_tensor(name, list(shape), dt, kind="ExternalInput").ap()

    x_d = din("x", [n_seq, SEQ, D_MODEL])
    meta_d = din("meta", [N_META, D_MODEL])
    wm_d = din("wm", [DEPTH, 4, D_MODEL, 1280])
    wd_d = din("wd", [DEPTH, 8, D_MODEL, 512])
    wg_d = din("wg", [DEPTH, D_MODEL, 8])
    wo_d = din("wo", [DEPTH, 2048, D_MODEL])
    grow_d = din("grow", [DEPTH, 4, D_MODEL])
    convw_d = din("convw", [DEPTH, 128, 16, 4])
    convb_d = din("convb", [DEPTH, 128, 16])
    bg_d = din("bg", [DEPTH, 4, 2])
    lam_d = din("lam", [DEPTH, 1, 256])
    identb_d = din("identb", [128, 128], BF16)
    identf_d = din("identf", [128, 128])
    mask_d = din("mask", [128, 128], BF16)
    sel_d = din("sel", [4, 512])
    out_d = nc.dram_tensor("out", [n_seq, SEQ, D_MODEL], F32, kind="ExternalOutput").ap()
    xres_d = nc.dram_tensor("xres", [L, D_MODEL], F32).ap()
    hcat_d = nc.dram_tensor("hcat", [L, 2048], BF16).ap()

    with contextlib.ExitStack() as st:
        P = Prog(nc)

        def T(name, shape, dt):
            return st.enter_context(nc.sbuf_tensor("sb_" + name, list(shape), dt))

        hT = T("hT", [128, 8, L], BF16)
        Wb = [T("W0", [128, 8, 1280], BF16), T("W1", [128, 8, 1280], BF16)]
        raw = T("raw", [128, 3 + L + 1], F32)
        acc = T("acc", [128, L + 4], F32)
        mqs = [T("mq", [128, 2, L], BF16), T("mq1", [128, 2, L], BF16)]
        mks = [T("mk", [128, 2, L], BF16), T("mk1", [128, 2, L], BF16)]
        mq, mk = mqs[0], mks[0]
        dq = T("dq", [128, L], BF16)
        dk = T("dk", [128, L], BF16)
        dv = T("dv", [128, NT, 130], BF16)
        CTs = [T("CTa", [128, 2, 257], F32), T("CTb", [128, 2, 257], F32)]
        CTdb = T("CTdb", [128, 2, 257], BF16)
        grow = T("grow", [128, 4, D_MODEL], F32)
        identb = T("identb", [128, 128], BF16)
        identf = T("identf", [128, 128], F32)
        maskb = T("maskb", [128, 128], BF16)
        sel = T("sel", [4, 512], F32)
        convw = T("convw", [128, 16, 4], F32)
        convb = T("convb", [128, 16], F32)
        bg = T("bg", [4, 4], F32)
        lamt = T("lamt", [128, 256], F32)
        lams = T("lams", [128, 8], F32)
        gatesT = T("gatesT", [128, NT, 12], F32)
        decs = T("decs", [4, 2 * NT], F32)
        decb = T("decb", [128, 4, 2 * NT], F32)
        ssq = T("ssq", [128, NT, 12], F32)
        rsd = T("rsd", [128, 16], F32)
        ss = T("ss", [128, 8], F32)
        vext = [T("vext%d" % i_, [128, 258], BF16) for i_ in range(3)]
        th = [T("th%d" % i_, [128, 512], F32) for i_ in range(3)]
        zs = [T("zs%d" % i_, [128, 384], F32) for i_ in range(3)]
        kw = [T("kw%d" % i_, [128, 256], BF16) for i_ in range(3)]
        SwT = [T("SwT%d" % i_, [128, 128], BF16) for i_ in range(3)]
        hg = [T("hg0", [128, 384], F32), T("hg1", [128, 384], F32)]
        hout = [T("hout0", [128, 384], BF16), T("hout1", [128, 384], BF16)]
        sm = [T("sm0", [128, 8], F32), T("sm1", [128, 8], F32)]
        PT = [T("PT%d" % i_, [128, 512], BF16) for i_ in range(6)]
        a0 = [T("a00", [128, 384], F32), T("a01", [128, 384], F32)]
        dummy = T("dummy", [1, 8], F32)
        ps = [st.enter_context(nc.psum_tensor("ps%d" % i, [128, 512], F32)) for i in range(8)]
        psb = [p[:].bitcast(BF16) for p in ps]
        pb = [Buf("ps%d" % i) for i in range(8)]

        rawb = raw[:].bitcast(BF16)
        accb = acc[:].bitcast(BF16)
        hc = [rawb[:, 0:2048], rawb[:, 2048:4096]]
        hcT = accb[:, 0:2048].rearrange("p (f j) -> p f j", j=128)
        hb = accb[:, 2048:3072]
        junkb = accb[:, 3072:4096]
        mqf = mq[:].rearrange("p c l -> p (c l)").bitcast(F32)
        mkf = mk[:].rearrange("p c l -> p (c l)").bitcast(F32)
        xo = [mqf[:, 0:1024], mqf[:, 1024:2048]]
        xn = [mkf[:, 0:1024], mkf[:, 1024:2048]]
        T1 = raw[0:4, 4:4 + L]
        T2 = acc[0:4, 0:L]
        T3 = mqf[0:4, 0:L]

        b_hT = [Buf("hT%d" % t) for t in range(NT)]
        b_W = [Buf("W0"), Buf("W1")]
        b_raw, b_acc = Buf("raw"), Buf("acc")
        b_mqs = [[Buf("mq%d_%d" % (p_, g)) for g in range(5)] for p_ in range(2)]
        b_mks = [[Buf("mk%d_%d" % (p_, g)) for g in range(5)] for p_ in range(2)]
        b_mq, b_mk = b_mqs[0], b_mks[0]
        b_dq = [Buf("dq%d" % g) for g in range(5)]
        b_dk = [Buf("dk%d" % g) for g in range(5)]
        b_dv = [Buf("dv%d" % t) for t in range(NT)]
        b_CTs = [Buf(), Buf()]
        b_CTdb = Buf()
        b_grow = [Buf("g%d" % i) for i in range(4)]
        b_const = Buf("const")
        b_lp = Buf("layerparams")
        b_lam = Buf("lam")
        b_gT, b_decs, b_decb = Buf(), Buf(), Buf()
        b_ssq = [Buf("ssq%d" % t) for t in range(NT)]
        b_rsd, b_ss = Buf(), Buf()
        b_vext, b_th, b_zs, b_kw, b_SwT, b_hg, b_hout, b_sm = ([Buf(), Buf(), Buf()] for _ in range(8))
        b_PT = [Buf() for _ in range(6)]
        b_a0 = [Buf(), Buf()]
        b_hc, b_xo, b_xn = [Buf(), Buf()], [Buf(), Buf()], [Buf(), Buf()]
        b_hcT, b_hb, b_junk = Buf(), Buf(), Buf()
        b_dummy = Buf()
        b_xres = [Buf("xres%d" % t) for t in range(NT)]
        b_hcat = [Buf("hcat%d" % t) for t in range(NT)]
        b_out = Buf("out")

        def MM(out, lhsT, rhs, start, stop, R, W):
            P.add("pe", lambda e: e.matmul(out, lhsT, rhs, start=start, stop=stop), R, W)

        def TR(out, in_, ident, R, W):
            P.add("pe", lambda e: e.transpose(out, in_, ident), R, W)

        def ACT(out, in_, func, R, W, **kw_):
            P.add("act", lambda e: e.activation(out, in_, func, **kw_), R, W)

        def TS(out, in0, s1, s2, op0, op1, R, W, eng="dve"):
            if op1 is None:
                P.add(eng, lambda e: e.tensor_scalar(out, in0, s1, None, op0), R, W)
            else:
                P.add(eng, lambda e: e.tensor_scalar(out, in0, s1, s2, op0, op1), R, W)

        def TT(out, in0, in1, op, R, W, eng="dve"):
            P.add(eng, lambda e: e.tensor_tensor(out, in0, in1, op), R, W)

        def STT(out, in0, sc, in1, op0, op1, R, W):
            P.add("dve", lambda e: e.scalar_tensor_tensor(out, in0, sc, in1, op0, op1), R, W)

        def CP(out, in_, R, W, eng="dve"):
            if eng == "act":
                P.add("act", lambda e: e.copy(out, in_), R, W)
            else:
                P.add(eng, lambda e: e.tensor_copy(out, in_), R, W)

        def MS(ap, val, W, eng="pool"):
            P.add(eng, lambda e: e.memset(ap, val), [], W)

        def DMA(q, out, in_, R, W):
            P.add(q, lambda e: e.dma_start(out=out, in_=in_), R, W, dma=True)

        def FENCE(bufs):
            P.add("pool", lambda e: e.memset(dummy[0:1, 0:1], 0.0), [], list(bufs) + [b_dummy])

        MUL, ADD, SUB, MAX = ALU.mult, ALU.add, ALU.subtract, ALU.max

        DMA("sp", identb[:], identb_d, [], [b_const])
        DMA("sp", identf[:], identf_d, [], [b_const])
        DMA("sp", maskb[:], mask_d, [], [b_const])
        DMA("sp", sel[:], sel_d, [], [b_const])
        for i in range(3):
            MS(vext[i][:, 256:258], 1.0, [b_vext[i]])
        for t in range(NT):
            MS(dv[:, t, 128:130], 1.0, [b_dv[t]])
        MS(raw[:, 0:3], 0.0, [b_raw])

        M_BUFS = [b_raw, b_acc] + b_mqs[0] + b_mks[0] + b_mqs[1] + b_mks[1]
        O_BUFS = b_hc + b_xo + b_xn + [b_hcT, b_hb, b_junk]

        wslot = [0]

        def load_weights(src, ncols, krows=8):
            i = wslot[0]
            wslot[0] ^= 1
            for kc in range(krows):
                DMA("pool", Wb[i][:, kc, 0:ncols], src[kc * 128:(kc + 1) * 128, :], [], [b_W[i]])
            return Wb[i], b_W[i]

        def norm_tile(t, xt, bx, bank):
            pos0, n = TP[t]
            ACT(junkb[:n, :], xt, AF.Square, [bx], [b_junk, b_ss], accum_out=ss[:n, 0:1])
            ACT(ss[:n, 1:2], ss[:n, 0:1], AF.Ln, [b_ss], [b_ss], scale=1.0 / D_MODEL, bias=EPS)
            ACT(ss[:n, 2:3], ss[:n, 1:2], AF.Exp, [b_ss], [b_ss], scale=-0.5)
            STT(hb[:n, :], xt, ss[:n, 2:3], grow[:n, 0, :], MUL, MUL, [bx, b_ss, b_grow[0]], [b_hb])
            for kc in range(8):
                TR(psb[bank][:, kc * 128:kc * 128 + n], hb[:n, kc * 128:(kc + 1) * 128], identb[:n, :n],
                   [b_hb, b_const], [pb[bank]])
            src = psb[bank][:, 0:1024].rearrange("p (k j) -> p k j", j=128)[:, :, 0:n]
            CP(hT[:, :, pos0:pos0 + n], src, [pb[bank]], [b_hT[t]], eng="act")

        def load_gpre(l):
            DMA("sp", grow[:, 0, :], grow_d[l, 0:1, :].partition_broadcast(128), [], [b_grow[0]])

        def load_layer_params(l):
            for i in range(1, 4):
                DMA("sp", grow[:, i, :], grow_d[l, i:i + 1, :].partition_broadcast(128), [], [b_grow[i]])
            DMA("sp", convw[:], convw_d[l], [], [b_lp])
            DMA("sp", convb[:], convb_d[l], [], [b_lp])
            DMA("sp", bg[:, 0:2], bg_d[l], [], [b_lp])
            DMA("sp", lamt[:], lam_d[l].partition_broadcast(128), [], [b_lam])
            TS(bg[:, 2:3], bg[:, 1:2], -1.0, None, MUL, None, [b_lp], [b_lp])
            TS(grow[:, 2, :], grow[:, 2, :], 0.25, None, MUL, None, [b_grow[2]], [b_grow[2]], eng="pool")
            lam_init = 0.8 - 0.6 * math.exp(-0.3 * l)
            TT(lamt[:, 0:64], lamt[:, 0:64], lamt[:, 64:128], MUL, [b_lam], [b_lam])
            TT(lamt[:, 128:192], lamt[:, 128:192], lamt[:, 192:256], MUL, [b_lam], [b_lam])
            P.add("dve", lambda e: e.reduce_sum(lams[:, 0:1], lamt[:, 0:64], axis=AX.X), [b_lam], [b_lam])
            P.add("dve", lambda e: e.reduce_sum(lams[:, 1:2], lamt[:, 128:192], axis=AX.X), [b_lam], [b_lam])
            ACT(lams[:, 2:4], lams[:, 0:2], AF.Exp, [b_lam], [b_lam])
            TT(lams[:, 4:5], lams[:, 2:3], lams[:, 3:4], SUB, [b_lam], [b_lam])
            TS(lams[:, 4:5], lams[:, 4:5], lam_init, None, ADD, None, [b_lam], [b_lam])
            TS(lams[:, 5:6], lams[:, 4:5], -1.0, None, MUL, None, [b_lam], [b_lam])
            return lam_init

        def gates_phase(l):
            W, bW = load_weights(wg_d[l], 8)
            allm = M_BUFS
            for g, (p0, w, ts) in enumerate(GR):
                bi, bf_ = (0, 1) if g % 2 == 0 else (2, 3)
                hr = [b_hT[t] for t in ts]
                for kc in range(8):
                    MM(ps[bi][0:4, 0:w], W[:, kc, 0:4], hT[:, kc, p0:p0 + w], kc == 0, kc == 7, hr + [bW], [pb[bi]])
                for kc in range(8):
                    MM(ps[bf_][0:4, 0:w], W[:, kc, 4:8], hT[:, kc, p0:p0 + w], kc == 0, kc == 7, hr + [bW], [pb[bf_]])
                TS(T1[:, p0:p0 + w], ps[bi][0:4, 0:w], bg[:, 0:1], None, ADD, None, [pb[bi], b_lp], allm)
                ACT(T2[:, p0:p0 + w], ps[bf_][0:4, 0:w], AF.Exp, [pb[bf_], b_lp], allm, scale=-1.0, bias=bg[:, 2:3])
            ACT(T2, T2, AF.Ln, allm, allm, bias=1.0)
            P.add("dve", lambda e: e.tensor_tensor_scan(T3, T2, T2, 0.0, ADD, MAX), allm, allm)
            TT(T1, T1, T3, ADD, allm, allm)
            P.add("dve", lambda e: e.tensor_tensor_scan(T2, T1, T1, 0.0, MAX, MAX), allm, allm)
            ge = T2[:, 15:L:128]
            TS(decs[:, 0:1], T2[:, 15:16], -1.0, None, MUL, None, allm, [b_decs])
            TT(decs[:, 1:NT], T2[:, 15:L - 128:128], T2[:, 143:L:128], SUB, allm, [b_decs])
            ACT(decs[:, 0:NT], decs[:, 0:NT], AF.Exp, [b_decs], [b_decs])
            TS(decs[:, NT:2 * NT], decs[:, 0:NT], 1.0 / 16, None, MUL, None, [b_decs], [b_decs])
            gl = T2[:, 143:L:128].unsqueeze(2).to_broadcast([4, 16, 128])
            for Tx in (T1, T3):
                TT(Tx[:, 16:L].rearrange("p (t j) -> p t j", j=128), Tx[:, 16:L].rearrange("p (t j) -> p t j", j=128),
                   gl, SUB, allm, allm)
                TS(Tx[:, 0:16], Tx[:, 0:16], T2[:, 15:16], None, SUB, None, allm, allm)
                ACT(Tx, Tx, AF.Exp, allm, allm)
            for h in range(4):
                MM(ps[4][:, h * 2 * NT:(h + 1) * 2 * NT], sel[:, h * 128:(h + 1) * 128], decs[:, :], True, True,
                   [b_const, b_decs], [pb[4]])
            CP(decb[:].rearrange("p h c -> p (h c)"), ps[4][:, 0:8 * NT], [pb[4]], [b_decb])
            for t in range(NT):
                p0, n = TP[t]
                TR(ps[5][:n, t * 8:t * 8 + 4], T1[:, p0:p0 + n], identf[0:4, 0:4], allm + [b_const], [pb[5]])
                TR(ps[5][:n, t * 8 + 4:t * 8 + 8], T3[:, p0:p0 + n], identf[0:4, 0:4], allm + [b_const], [pb[5]])
            CP(gatesT[0:16, 0, 0:8], ps[5][0:16, 0:8], [pb[5]], [b_gT])
            CP(gatesT[:, 1:NT, 0:8], ps[5][:, 8:8 * NT].rearrange("p (t c) -> p t c", c=8), [pb[5]], [b_gT])
            TS(gatesT[0:16, 0, 8:12], gatesT[0:16, 0, 0:4], 1.0 / 16, None, MUL, None, [b_gT], [b_gT])
            TS(gatesT[:, 1:NT, 8:12], gatesT[:, 1:NT, 0:4], 1.0 / 16, None, MUL, None, [b_gT], [b_gT])

        def mlstm_prep_gen(l, h, W, bW, banks):
            hp = h % 2
            MS(raw[:, 0:3], 0.0, [b_raw])
            bi = 0
            for c in range(4):
                dst, bdst = (mqs[hp], b_mqs[hp]) if c < 2 else (mks[hp], b_mks[hp])
                cc = (0 if c < 2 else 8) + h * 2 + (c % 2)
                for g, (p0, w, ts) in enumerate(GR):
                    bk = banks[bi % len(banks)]
                    bi += 1
                    hr_ = [b_hT[t] for t in ts]
                    for kc in range(8):
                        MM(ps[bk][:, 0:w], W[:, kc, c * 128:(c + 1) * 128], hT[:, kc, p0:p0 + w], kc == 0, kc == 7,
                           hr_ + [bW], [pb[bk]])
                    CP(raw[:, 3 + p0:3 + p0 + w], ps[bk][:, 0:w], [pb[bk]], [b_raw], eng="act")
                    yield
                TS(acc[:, 0:L], raw[:, 3:3 + L], convw[:, cc, 3:4], None, MUL, None, [b_raw, b_lp], [b_acc])
                yield
                for j in (2, 1, 0):
                    STT(acc[:, 0:L], raw[:, j:j + L], convw[:, cc, j:j + 1], acc[:, 0:L], MUL, ADD,
                        [b_raw, b_acc, b_lp], [b_acc])
                    yield
                ACT(dst[:, c % 2, :], acc[:, 0:L], AF.Silu, [b_acc, b_lp], bdst, bias=convb[:, cc:cc + 1])
                yield

        def mlstm_head(l, h, W, bW, hook=None):
            hp = h % 2
            mq, mk, b_mq, b_mk = mqs[hp], mks[hp], b_mqs[hp], b_mks[hp]
            MS(CTs[0][:], 0.0, [b_CTs[0]])

            def A_pe(t):
                p0, n = TP[t]
                g = TG[t]
                for kc in range(8):
                    MM(ps[0][:n, 0:256], hT[:, kc, p0:p0 + n], W[:, kc, 512:768], kc == 0, kc == 7, [b_hT[t], bW], [pb[0]])
                for kc in range(8):
                    MM(ps[1][:n, 0:512], hT[:, kc, p0:p0 + n], W[:, kc, 768:1280], kc == 0, kc == 7, [b_hT[t], bW], [pb[1]])
                for c in range(2):
                    TR(psb[2][:n, c * 128:(c + 1) * 128], mk[:, c, p0:p0 + n], identb[:, :], [b_mk[g], b_const], [pb[2]])
                for c in range(2):
                    MM(ps[2][:n, 128:128 + n], mk[:, c, p0:p0 + n], mq[:, c, p0:p0 + n], c == 0, c == 1, [b_mk[g], b_mq[g]], [pb[2]])

            def A_other(t):
                p0, n = TP[t]
                i = t % 3
                CP(vext[i][:n, 0:256], ps[0][:n, 0:256], [pb[0]], [b_vext[i]], eng="act")
                ACT(th[i][:n, :], ps[1][:n, :], AF.Tanh, [pb[1]], [b_th[i]], scale=0.5)
                TS(kw[i][:n, :], psb[2][:n, 0:256], gatesT[:n, t, h:h + 1], None, MUL, None, [pb[2], b_gT], [b_kw[i]])
                STT(SwT[i][:n, :n], ps[2][:n, 128:128 + n], gatesT[:n, t, 8 + h:9 + h], maskb[:n, :n], MUL, MUL,
                    [pb[2], b_gT, b_const], [b_SwT[i]])
                STT(zs[i][:n, 0:256], th[i][:n, 256:512], 1.0, ps[1][:n, 256:512], ADD, MUL, [pb[1], b_th[i]], [b_zs[i]])
                TT(zs[i][:n, 0:256], zs[i][:n, 0:256], grow[:n, 2, h * 256:(h + 1) * 256], MUL, [b_zs[i], b_grow[2]], [b_zs[i]])

            def CTDB(t):
                CTo, bCTo = CTs[t % 2], b_CTs[t % 2]
                TS(CTdb[:], CTo[:], decb[:, h, NT + t:NT + t + 1], None, MUL, None, [bCTo, b_decb], [b_CTdb])

            def B_pe(t):
                p0, n = TP[t]
                g = TG[t]
                ia = t % 3
                for c in range(2):
                    MM(ps[5 + c][:, 0:257], kw[ia][:n, c * 128:(c + 1) * 128], vext[ia][:n, 0:257], True, True,
                       [b_kw[ia], b_vext[ia]], [pb[5 + c]])
                MM(ps[4][:n, 0:257], SwT[ia][:n, :n], vext[ia][:n, 0:257], True, False, [b_SwT[ia], b_vext[ia]], [pb[4]])
                for c in range(2):
                    MM(ps[4][:n, 0:257], mq[:, c, p0:p0 + n], CTdb[:, c, :], False, c == 1, [b_mq[g], b_CTdb], [pb[4]])

            def B_rest(t):
                p0, n = TP[t]
                i = t % 2
                CTo, bCTo = CTs[t % 2], b_CTs[t % 2]
                CTn, bCTn = CTs[(t + 1) % 2], b_CTs[(t + 1) % 2]
                for c in range(2):
                    STT(CTn[:, c, :], CTo[:, c, :], decb[:, h, t:t + 1], ps[5 + c][:, 0:257], MUL, ADD,
                        [pb[5 + c], bCTo, b_decb], [bCTn])
                TT(sm[i][:n, 0:1], ps[4][:n, 256:257], gatesT[:n, t, 4 + h:5 + h], MAX, [pb[4], b_gT], [b_sm[i]])
                STT(sm[i][:n, 0:1], ps[4][:n, 256:257], -1.0, sm[i][:n, 0:1], MUL, MAX, [pb[4], b_sm[i]], [b_sm[i]])
                P.add("dve", lambda e, o_=sm[i][:n, 1:2], i_=sm[i][:n, 0:1]: e.reciprocal(o_, i_), [b_sm[i]], [b_sm[i]])
                ACT(hrw[i][:n, :], ps[4][:n, 0:256], AF.Copy, [pb[4], b_sm[i]], [b_hr[i]], scale=sm[i][:n, 1:2])
                if t + 1 < NT:
                    CTDB(t + 1)

            def B2(t):
                p0, n = TP[t]
                i = t % 2
                ia = t % 3
                STT(hg[i][:n, 0:256], th[ia][:n, 0:256], 1.0, hrw[i][:n, :], ADD, MUL, [b_th[ia], b_hr[i]], [b_hg[i]])
                ACT(junkh[i][:n, :], hg[i][:n, 0:256], AF.Square, [b_hg[i]], [b_junkh[i], b_ssq[t]], accum_out=ssq[:n, t, h:h + 1])
                TT(hout[i][:n, 0:256], hg[i][:n, 0:256], zs[ia][:n, 0:256], MUL, [b_hg[i], b_zs[ia]], [b_hout[i]])
                DMA("sp", hcat_d[p0:p0 + n, h * 256:(h + 1) * 256], hout[i][:n, 0:256], [b_hout[i]], [b_hcat[t]])

            CTDB(0)
            A_pe(0)
            A_other(0)
            A_pe(1)
            A_other(1)
            for t in range(2, NT):
                B_pe(t - 2)
                A_pe(t)
                B_rest(t - 2)
                A_other(t)
                B2(t - 2)
                if hook is not None:
                    hook(t)
            for t in (NT - 2, NT - 1):
                B_pe(t)
                B_rest(t)
                B2(t)

        junkh = [T("junkh0", [128, 256], BF16), T("junkh1", [128, 256], BF16)]
        b_junkh = [Buf(), Buf()]
        hrw = [T("hr0", [128, 256], F32), T("hr1", [128, 256], F32)]
        b_hr = [Buf(), Buf()]

        def diff_head(l, h, W, bW, lam_init):
            scale = 64 ** -0.5
            bankrot = [0]
            for c, (dst, bdst) in enumerate(((dq, b_dq), (dk, b_dk))):
                for g, (p0, w, ts) in enumerate(GR):
                    bk = bankrot[0]
                    bankrot[0] ^= 1
                    hr = [b_hT[t] for t in ts]
                    for kc in range(8):
                        MM(ps[bk][:, 0:w], W[:, kc, c * 128:(c + 1) * 128], hT[:, kc, p0:p0 + w], kc == 0, kc == 7,
                           hr + [bW], [pb[bk]])
                    CP(dst[:, p0:p0 + w], ps[bk][:, 0:w], [pb[bk]], [bdst[g]], eng=("act" if g % 2 else "dve"))
            for t in range(NT):
                p0, n = TP[t]
                bk = t % 2
                for kc in range(8):
                    MM(ps[bk][:n, 0:128], hT[:, kc, p0:p0 + n], W[:, kc, 256:384], kc == 0, kc == 7, [b_hT[t], bW], [pb[bk]])
                CP(dv[:n, t, 0:128], ps[bk][:n, 0:128], [pb[bk]], [b_dv[t]], eng=("act" if t % 2 else "dve"))
            QG = [[0]] + [list(range(a_, min(a_ + 3, NT))) for a_ in range(1, NT, 3)]
            items = []
            for gq, Q in enumerate(QG):
                items.append(("z", gq, None, False))
                for j in range(0, Q[-1] + 1):
                    items.append(("a", gq, j, j == Q[-1]))
            SBP = [(0, 1), (2, 3)]
            cfac = 0.5 * (1.0 - lam_init)

            def S(k):
                ty, gq, j, last = items[k]
                Q = QG[gq]
                nq = TP[Q[0]][1]
                sb = SBP[k % 2]
                if ty == "z":
                    for slot, t in enumerate(Q):
                        p0, n = TP[t]
                        for kc in range(8):
                            MM(ps[sb[0]][:n, slot * 128:slot * 128 + 128], hT[:, kc, p0:p0 + n], W[:, kc, 384:512], kc == 0, kc == 7,
                               [b_hT[t], bW], [pb[sb[0]]])
                    return
                qs = [t for t in Q if t >= j]
                ps0 = TP[qs[0]][0]
                wq = sum(TP[t][1] for t in qs)
                kp0, kn = TP[j]
                for c in range(2):
                    cs = slice(64 * c, 64 * c + 64)
                    MM(ps[sb[c]][:kn, 0:wq], dk[cs, kp0:kp0 + kn], dq[cs, ps0:ps0 + wq], True, True,
                       [b_dk[TG[j]]] + [b_dq[TG[t]] for t in qs], [pb[sb[c]]])

            def E(k):
                ty, gq, j, last = items[k]
                Q = QG[gq]
                nq = TP[Q[0]][1]
                gp = gq % 2
                sb = SBP[k % 2]
                if ty == "z":
                    wz = len(Q) * 128
                    ACT(th[gp][:nq, 0:wz], ps[sb[0]][:nq, 0:wz], AF.Tanh, [pb[sb[0]]], [b_th[gp]], scale=0.5)
                    TS(th[gp][:nq, 0:wz], th[gp][:nq, 0:wz], cfac, cfac, MUL, ADD, [b_th[gp]], [b_th[gp]])
                    TT(zs[gp][:nq, 0:wz], ps[sb[0]][:nq, 0:wz], th[gp][:nq, 0:wz], MUL, [pb[sb[0]], b_th[gp]], [b_zs[gp]])
                    zv = zs[gp][:nq, 0:wz].rearrange("p (s d) -> p s d", d=128)
                    TT(zv, zv, grow[:nq, 3, h * 128:(h + 1) * 128].unsqueeze(1).to_broadcast([nq, len(Q), 128]), MUL,
                       [b_zs[gp], b_grow[3]], [b_zs[gp]])
                    return
                qs = [t for t in Q if t >= j]
                wq = sum(TP[t][1] for t in qs)
                kp0, kn = TP[j]
                for c in range(2):
                    pi = (k % 3) * 2 + c
                    ACT(PT[pi][:kn, 0:wq], ps[sb[c]][:kn, 0:wq], AF.Exp, [pb[sb[c]]], [b_PT[pi]], scale=scale)
                    if j >= Q[0]:
                        TT(PT[pi][:kn, 0:kn], PT[pi][:kn, 0:kn], maskb[:kn, :kn], MUL, [b_PT[pi], b_const], [b_PT[pi]], eng="pool")

            def V(k):
                ty, gq, j, last = items[k]
                if ty == "z":
                    return
                Q = QG[gq]
                nq = TP[Q[0]][1]
                nS = len(Q)
                gp = gq % 2
                pa = [4, 5] if gp == 0 else [6, 7]
                qs = [t for t in Q if t >= j]
                kp0, kn = TP[j]
                for bi_, t in enumerate(qs):
                    slot = t - Q[0]
                    n = TP[t][1]
                    for c in range(2):
                        pi = (k % 3) * 2 + c
                        P.add("pe", lambda e, o_=ps[pa[c]][:n, slot * 129:slot * 129 + 129], l_=PT[pi][:kn, bi_ * 128:bi_ * 128 + n],
                              r_=dv[:kn, j, 0:129], st_=(j == 0 and slot == 0), sp_=(j == t):
                              e.matmul(o_, l_, r_, start=st_, stop=sp_, skip_group_check=True),
                              [b_PT[pi], b_dv[j]], [pb[pa[c]]])
                if not last:
                    return
                A0 = ps[pa[0]][:nq, 0:nS * 129].rearrange("p (s d) -> p s d", d=129)
                A1 = ps[pa[1]][:nq, 0:nS * 129].rearrange("p (s d) -> p s d", d=129)
                r0 = sm[gp][:nq, 0:nS]
                r1 = sm[gp][:nq, 4:4 + nS]
                P.add("dve", lambda e: e.reciprocal(r0.unsqueeze(2), A0[:, :, 128:129]), [pb[pa[0]]], [b_sm[gp]])
                P.add("dve", lambda e: e.reciprocal(r1.unsqueeze(2), A1[:, :, 128:129]), [pb[pa[1]]], [b_sm[gp]])
                TS(r1, r1, lams[:nq, 5:6], None, MUL, None, [b_sm[gp], b_lam], [b_sm[gp]])
                a0v = a0[gp][:nq, 0:nS * 128].rearrange("p (s d) -> p s d", d=128)
                hgv = hg[gp][:nq, 0:nS * 128].rearrange("p (s d) -> p s d", d=128)
                TT(a0v, A0[:, :, 0:128], r0.unsqueeze(2).to_broadcast([nq, nS, 128]), MUL, [pb[pa[0]], b_sm[gp]], [b_a0[gp]])
                TT(hgv, A1[:, :, 0:128], r1.unsqueeze(2).to_broadcast([nq, nS, 128]), MUL, [pb[pa[1]], b_sm[gp]], [b_hg[gp]])
                TT(hgv, hgv, a0v, ADD, [b_hg[gp], b_a0[gp]], [b_hg[gp]])
                for slot, t in enumerate(Q):
                    ACT(junkh[gp][:nq, 0:128], hg[gp][:nq, slot * 128:(slot + 1) * 128], AF.Square, [b_hg[gp]],
                        [b_junkh[gp], b_ssq[t]], accum_out=ssq[:nq, t, 4 + h:5 + h])
                TT(hout[gp][:nq, 0:nS * 128], hg[gp][:nq, 0:nS * 128], zs[gp][:nq, 0:nS * 128], MUL, [b_hg[gp], b_zs[gp]], [b_hout[gp]])
                p0 = TP[Q[0]][0]
                cols = slice(1024 + h * 128, 1024 + (h + 1) * 128)
                if nS == 1:
                    DMA("sp", hcat_d[p0:p0 + nq, cols], hout[gp][:nq, 0:128], [b_hout[gp]], [b_hcat[t] for t in Q])
                else:
                    DMA("sp", hcat_d[p0:p0 + nS * 128, cols].rearrange("(s p) d -> p s d", p=128),
                        hout[gp][:nq, 0:nS * 128].rearrange("p (s d) -> p s d", d=128), [b_hout[gp]], [b_hcat[t] for t in Q])

            NI = len(items)
            for k in range(NI):
                S(k)
                E(k)
                if k >= 1:
                    V(k - 1)
            V(NI - 1)

        def out_phase(l, s, last):
            FENCE(M_BUFS + O_BUFS)
            WA, bWA = load_weights(wo_d[l, 0:1024, :], 1024)
            WB, bWB = load_weights(wo_d[l, 1024:2048, :], 1024)
            if not last:
                load_gpre(l + 1)
            def O1(t):
                p0, n = TP[t]
                i = t % 2
                DMA("sp", hc[i][:n, :], hcat_d[p0:p0 + n, :], [b_hcat[t]], [b_hc[i]])
                if l == 0:
                    src = meta_d if t == 0 else x_d[s, p0 - 16:p0 - 16 + n, :]
                    DMA("sp", xo[i][:n, :], src, [], [b_xo[i]])
                else:
                    DMA("sp", xo[i][:n, :], xres_d[p0:p0 + n, :], [b_xres[t]], [b_xo[i]])
                ACT(rsd[:n, 0:4], ssq[:n, t, 0:4], AF.Ln, [b_ssq[t]], [b_rsd], scale=1.0 / 1024, bias=EPS)
                ACT(rsd[:n, 4:12], ssq[:n, t, 4:12], AF.Ln, [b_ssq[t]], [b_rsd], scale=1.0 / 128, bias=EPS)
                ACT(rsd[:n, 0:12], rsd[:n, 0:12], AF.Exp, [b_rsd], [b_rsd], scale=-0.5)
                TT(hc[i][:n, 0:1024].rearrange("p (h d) -> p h d", d=256), hc[i][:n, 0:1024].rearrange("p (h d) -> p h d", d=256),
                   rsd[:n, 0:4].unsqueeze(2).to_broadcast([n, 4, 256]), MUL, [b_hc[i], b_rsd], [b_hc[i]])
                TT(hc[i][:n, 1024:2048].rearrange("p (h d) -> p h d", d=128), hc[i][:n, 1024:2048].rearrange("p (h d) -> p h d", d=128),
                   rsd[:n, 4:12].unsqueeze(2).to_broadcast([n, 8, 128]), MUL, [b_hc[i], b_rsd], [b_hc[i]])
                for half in range(2):
                    for f in range(8):
                        fc = half * 8 + f
                        TR(psb[half][:, f * 128:f * 128 + n], hc[i][:n, fc * 128:(fc + 1) * 128], identb[:n, :n],
                           [b_hc[i], b_const], [pb[half]])
                    src = psb[half][:, 0:1024].rearrange("p (k j) -> p k j", j=128)[:, :, 0:n]
                    CP(hcT[:, half * 8:(half + 1) * 8, 0:n], src, [pb[half]], [b_hcT], eng=("act" if half else "dve"))
                ya = [2, 3] if i == 0 else [4, 5]
                for half in range(2):
                    for fc in range(16):
                        Wx, bWx = (WA, bWA) if fc < 8 else (WB, bWB)
                        MM(ps[ya[half]][:n, :], hcT[:, fc, 0:n], Wx[:, fc % 8, half * 512:(half + 1) * 512], fc == 0, fc == 15,
                           [b_hcT, bWx], [pb[ya[half]]])

            def O2(t):
                p0, n = TP[t]
                i = t % 2
                ya = [2, 3] if i == 0 else [4, 5]
                for half in range(2):
                    ACT(junkb[:n, 0:512], ps[ya[half]][:n, :], AF.Square, [pb[ya[half]]], [b_junk, b_ss],
                        accum_out=ss[:n, 4 + half:5 + half])
                TT(ss[:n, 6:7], ss[:n, 4:5], ss[:n, 5:6], ADD, [b_ss], [b_ss])
                ACT(ss[:n, 6:7], ss[:n, 6:7], AF.Ln, [b_ss], [b_ss], scale=1.0 / D_MODEL, bias=EPS)
                ACT(ss[:n, 7:8], ss[:n, 6:7], AF.Exp, [b_ss], [b_ss], scale=-0.5)
                for half in range(2):
                    hs = slice(half * 512, (half + 1) * 512)
                    STT(xn[i][:n, hs], ps[ya[half]][:n, :], ss[:n, 7:8], grow[:n, 1, hs], MUL, MUL,
                        [pb[ya[half]], b_ss, b_grow[1]], [b_xn[i]])
                    TT(xn[i][:n, hs], xn[i][:n, hs], xo[i][:n, hs], ADD, [b_xn[i], b_xo[i]], [b_xn[i]])
                if last:
                    if t > 0:
                        DMA("sp", out_d[s, p0 - 16:p0 - 16 + n, :], xn[i][:n, :], [b_xn[i]], [b_out])
                else:
                    DMA("sp", xres_d[p0:p0 + n, :], xn[i][:n, :], [b_xn[i]], [b_xres[t]])
                    norm_tile(t, xn[i][:n, :], b_xn[i], 6 + i)

            O1(0)
            for t in range(1, NT):
                O1(t)
                O2(t - 1)
            O2(NT - 1)
            FENCE(M_BUFS + O_BUFS)

        for s in range(n_seq):
            load_gpre(0)
            FENCE(M_BUFS + O_BUFS)
            for t in range(NT):
                p0, n = TP[t]
                i = t % 2
                src = meta_d if t == 0 else x_d[s, p0 - 16:p0 - 16 + n, :]
                DMA("sp", xo[i][:n, :], src, [], [b_xo[i]])
                norm_tile(t, xo[i][:n, :], b_xo[i], 6 + i)
            FENCE(M_BUFS + O_BUFS)
            for l in range(n_layers):
                lam_init = load_layer_params(l)
                gates_phase(l)
                nxt = load_weights(wm_d[l, 0], 1280)
                for _ in mlstm_prep_gen(l, 0, nxt[0], nxt[1], [0, 1]):
                    pass
                for h in range(4):
                    W, bW = nxt
                    nxt = load_weights(wm_d[l, h + 1], 1280) if h < 3 else load_weights(wd_d[l, 0], 512)
                    hook = None
                    gen = None
                    if h < 3:
                        gen = mlstm_prep_gen(l, h + 1, nxt[0], nxt[1], [3, 7])

                        def hook(t, gen=gen):
                            if t >= 7:
                                for _ in range(4):
                                    next(gen, None)
                    mlstm_head(l, h, W, bW, hook)
                    if gen is not None:
                        for _ in gen:
                            pass
                for h in range(8):
                    W, bW = nxt
                    if h < 7:
                        nxt = load_weights(wd_d[l, h + 1], 512)
                    diff_head(l, h, W, bW, lam_init)
                out_phase(l, s, l == n_layers - 1)
        if os.environ.get('MK_SBUF'):
            print('SBUF remaining', nc.sbuf_bytes_remaining)
        P.emit(st)
    return nc


_CACHE = {}


def _host_layout(inp):
    bf = ml_dtypes.bfloat16
    w_in = np.asarray(inp["w_in"], dtype=np.float32)
    D = 1024
    sec = lambda k: w_in[:, :, k * D:(k + 1) * D] if k < 5 else None
    qm, km, vm, om, zm = (w_in[:, :, k * D:(k + 1) * D] for k in range(5))
    wg = np.ascontiguousarray(w_in[:, :, 5 * D:5 * D + 8])
    off = 5 * D + 8
    qd, kd, vd, zd = (w_in[:, :, off + k * D:off + (k + 1) * D] for k in range(4))
    wm = np.empty((DEPTH, 4, D, 1280), np.float32)
    for h in range(4):
        hs = slice(h * 256, (h + 1) * 256)
        wm[:, h] = np.concatenate([qm[:, :, hs], km[:, :, hs], vm[:, :, hs], om[:, :, hs], zm[:, :, hs]], axis=-1)
    wd = np.empty((DEPTH, 8, D, 512), np.float32)
    for h in range(8):
        hs = slice(h * 128, (h + 1) * 128)
        wd[:, h] = np.concatenate([qd[:, :, hs], kd[:, :, hs], vd[:, :, hs], zd[:, :, hs]], axis=-1)
    grow = np.stack([inp["pre_norm_g"], inp["post_norm_g"], inp["mlstm_norm_g"], inp["diff_norm_g"]], axis=1).astype(np.float32)
    convw = np.ascontiguousarray(np.asarray(inp["conv_w"], np.float32).transpose(0, 2, 1).reshape(DEPTH, 16, 128, 4).transpose(0, 2, 1, 3))
    convb = np.ascontiguousarray(np.asarray(inp["conv_b"], np.float32).reshape(DEPTH, 16, 128).transpose(0, 2, 1))
    bgt = np.ascontiguousarray(np.asarray(inp["b_gates"], np.float32).reshape(DEPTH, 2, 4).transpose(0, 2, 1))
    lam = np.concatenate([inp["lambda_q1"], inp["lambda_k1"], inp["lambda_q2"], inp["lambda_k2"]], axis=-1).astype(np.float32)[:, None, :]
    sel = np.zeros((4, 512), np.float32)
    for h in range(4):
        sel[h, h * 128:(h + 1) * 128] = 1.0
    mask = np.triu(np.ones((128, 128), np.float32)).astype(bf)
    common = {
        "meta": np.ascontiguousarray(inp["meta_tokens"], dtype=np.float32),
        "wm": wm, "wd": wd, "wg": wg, "wo": np.ascontiguousarray(inp["w_out"], dtype=np.float32),
        "grow": np.ascontiguousarray(grow), "convw": convw, "convb": convb, "bg": bgt, "lam": np.ascontiguousarray(lam),
        "identb": np.eye(128, dtype=np.float32).astype(bf), "identf": np.eye(128, dtype=np.float32),
        "mask": mask, "sel": sel,
    }
    return common


def kernel(**inputs):
    n_layers = int(os.environ.get("MK_LAYERS", DEPTH))
    key = ("nc", n_layers)
    if key not in _CACHE:
        _CACHE[key] = build_program(n_layers=n_layers, n_seq=2)
    nc = _CACHE[key]
    common = _host_layout(inputs)
    x = np.asarray(inputs["x"], dtype=np.float32)
    in_maps = []
    for c in range(8):
        m = dict(common)
        m["x"] = np.ascontiguousarray(x[2 * c:2 * c + 2])
        in_maps.append(m)
    res = run_bass_kernel_spmd(nc, in_maps, core_ids=list(range(8)))
    out = np.concatenate([r["out"] for r in res.results], axis=0)
    return out.astype(np.float32)
```

```python
import contextlib
import math
import os
import numpy as np
import ml_dtypes
import concourse.bass as bass
import concourse.mybir as mybir
from concourse.bass_utils import run_bass_kernel_spmd

F32 = mybir.dt.float32
BF16 = mybir.dt.bfloat16
AF = mybir.ActivationFunctionType
ALU = mybir.AluOpType
AX = mybir.AxisListType

ENGS = ["pe", "act", "dve", "pool", "sp"]


class Buf:
    __slots__ = ("n", "w", "r")

    def __init__(self, n=""):
        self.n = n
        self.w = None
        self.r = []


class Op:
    __slots__ = ("eng", "fn", "deps", "sig", "tok", "dma")


class Prog:
    NDMA = 8
    ROLL = 30000

    def __init__(self, nc):
        self.nc = nc
        self.ops = {e: [] for e in ENGS}
        self.dq = {e: {"next": 0, "last": [None] * self.NDMA, "cnt": [0] * self.NDMA} for e in ENGS}
        self.all_dma = []

    def add(self, eng, fn, reads=(), writes=(), dma=False):
        o = Op()
        o.eng, o.fn, o.dma, o.sig, o.tok = eng, fn, dma, False, None
        cand = []
        for b in reads:
            if b.w is not None:
                cand.append((b.w, True))
        for b in writes:
            if b.w is not None:
                cand.append((b.w, False))
            for r in b.r:
                cand.append((r, False))
        deps, seen = [], set()
        if dma:
            q = self.dq[eng]
            s = q["next"]
            q["next"] = (s + 1) % self.NDMA
            if q["last"][s] is not None:
                cand.append((q["last"][s], True))
            q["cnt"][s] += 16
            o.tok = (("d", eng, s), q["cnt"][s])
            q["last"][s] = o
            self.all_dma.append(o)
        for p, raw in cand:
            if p is o or id(p) in seen:
                continue
            if (not dma) and (not p.dma) and p.eng == eng:
                if eng == "pe":
                    continue
            seen.add(id(p))
            deps.append(p)
            if not p.dma:
                p.sig = True
        o.deps = deps
        for b in reads:
            if dma:
                b.r.append(o)
            else:
                b.r = [x for x in b.r if x.dma or x.eng != eng] + [o]
        for b in writes:
            b.w = o
            b.r = []
        self.ops[eng].append(o)
        return o

    def emit(self, stack):
        nc = self.nc
        keys = []
        for e in ENGS:
            cnt, gen, used = 0, 0, False
            for o in self.ops[e]:
                if o.dma or not o.sig:
                    continue
                cnt += 1
                if cnt > self.ROLL:
                    gen += 1
                    cnt = 1
                o.tok = (("c", e, gen), cnt)
                used = True
            if used:
                for g in range(gen + 1):
                    keys.append(("c", e, g))
            for s in range(self.NDMA):
                if self.dq[e]["cnt"][s]:
                    keys.append(("d", e, s))
        sems = {k: stack.enter_context(nc.semaphore("s_%s_%s_%d" % k)) for k in keys}
        final = {}
        for o in self.all_dma:
            final[o.tok[0]] = max(final.get(o.tok[0], 0), o.tok[1])

        def mk(e):
            def body(engine):
                waited = {}
                for o in self.ops[e]:
                    for p in o.deps:
                        k, v = p.tok
                        if waited.get(k, 0) < v:
                            engine.wait_ge(sems[k], v)
                            waited[k] = v
                    ins = o.fn(engine)
                    if o.dma:
                        ins.then_inc(sems[o.tok[0]], 16)
                    elif o.sig:
                        ins.then_inc(sems[o.tok[0]], 1)
                if e == "sp":
                    for k, v in final.items():
                        if waited.get(k, 0) < v:
                            engine.wait_ge(sems[k], v)
            return body

        with nc.Block() as block:
            block.tensor(mk("pe"))
            block.scalar(mk("act"))
            block.vector(mk("dve"))
            block.gpsimd(mk("pool"))
            block.sync(mk("sp"))


D_MODEL = 1024
SEQ = 2048
N_META = 16
L = SEQ + N_META
NT = 17
DEPTH = 4
EPS = 1e-6
TP = [(0, 16)] + [(16 + 128 * (t - 1), 128) for t in range(1, NT)]
GR = [(0, 400, [0, 1, 2, 3]), (400, 512, [4, 5, 6, 7]), (912, 512, [8, 9, 10, 11]),
      (1424, 512, [12, 13, 14, 15]), (1936, 128, [16])]
TG = {}
for _g, (_p, _w, _ts) in enumerate(GR):
    for _t in _ts:
        TG[_t] = _g


def build_program(n_layers=DEPTH, n_seq=2, dbg=False):
    nc = bass.Bass("TRN2", target_bir_lowering=False)

    def din(name, shape, dt=F32):
        return nc.dram_tensor(name, list(shape), dt, kind="ExternalInput").ap()

    x_d = din("x", [n_seq, SEQ, D_MODEL])
    meta_d = din("meta", [N_META, D_MODEL])
    wm_d = din("wm", [DEPTH, 4, D_MODEL, 1280])
    wd_d = din("wd", [DEPTH, 8, D_MODEL, 512])
    wg_d = din("wg", [DEPTH, D_MODEL, 8])
    wo_d = din("wo", [DEPTH, 2048, D_MODEL])
    grow_d = din("grow", [DEPTH, 4, D_MODEL])
    convw_d = din("convw", [DEPTH, 128, 16, 4])
    convb_d = din("convb", [DEPTH, 128, 16])
    bg_d = din("bg", [DEPTH, 4, 2])
    lam_d = din("lam", [DEPTH, 1, 256])
    identb_d = din("identb", [128, 128], BF16)
    identf_d = din("identf", [128, 128])
    mask_d = din("mask", [128, 128], BF16)
    sel_d = din("sel", [4, 512])
    out_d = nc.dram_tensor("out", [n_seq, SEQ, D_MODEL], F32, kind="ExternalOutput").ap()
    xres_d = nc.dram_tensor("xres", [L, D_MODEL], F32).ap()
    hcat_d = nc.dram_tensor("hcat", [L, 2048], BF16).ap()

    with contextlib.ExitStack() as st:
        P = Prog(nc)

        def T(name, shape, dt):
            return st.enter_context(nc.sbuf_tensor("sb_" + name, list(shape), dt))

        hT = T("hT", [128, 8, L], BF16)
        Wb = [T("W0", [128, 8, 1280], BF16), T("W1", [128, 8, 1280], BF16)]
        raw = T("raw", [128, 3 + L + 1], F32)
        acc = T("acc", [128, L + 4], F32)
        mqs = [T("mq", [128, 2, L], BF16), T("mq1", [128, 2, L], BF16)]
        mks = [T("mk", [128, 2, L], BF16), T("mk1", [128, 2, L], BF16)]
        mq, mk = mqs[0], mks[0]
        dq = T("dq", [128, L], BF16)
        dk = T("dk", [128, L], BF16)
        dv = T("dv", [128, NT, 130], BF16)
        CTs = [T("CTa", [128, 2, 257], F32), T("CTb", [128, 2, 257], F32)]
        CTdb = T("CTdb", [128, 2, 257], BF16)
        grow = T("grow", [128, 4, D_MODEL], F32)
        identb = T("identb", [128, 128], BF16)
        identf = T("identf", [128, 128], F32)
        maskb = T("maskb", [128, 128], BF16)
        sel = T("sel", [4, 512], F32)
        convw = T("convw", [128, 16, 4], F32)
        convb = T("convb", [128, 16], F32)
        bg = T("bg", [4, 4], F32)
        lamt = T("lamt", [128, 256], F32)
        lams = T("lams", [128, 8], F32)
        gatesT = T("gatesT", [128, NT, 12], F32)
        decs = T("decs", [4, 2 * NT], F32)
        decb = T("decb", [128, 4, 2 * NT], F32)
        ssq = T("ssq", [128, NT, 12], F32)
        rsd = T("rsd", [128, 16], F32)
        ss = T("ss", [128, 8], F32)
        vext = [T("vext%d" % i_, [128, 258], BF16) for i_ in range(3)]
        th = [T("th%d" % i_, [128, 512], F32) for i_ in range(3)]
        zs = [T("zs%d" % i_, [128, 384], F32) for i_ in range(3)]
        kw = [T("kw%d" % i_, [128, 256], BF16) for i_ in range(3)]
        SwT = [T("SwT%d" % i_, [128, 128], BF16) for i_ in range(3)]
        hg = [T("hg0", [128, 384], F32), T("hg1", [128, 384], F32)]
        hout = [T("hout0", [128, 384], BF16), T("hout1", [128, 384], BF16)]
        sm = [T("sm0", [128, 8], F32), T("sm1", [128, 8], F32)]
        PT = [T("PT%d" % i_, [128, 512], BF16) for i_ in range(6)]
        a0 = [T("a00", [128, 384], F32), T("a01", [128, 384], F32)]
        dummy = T("dummy", [1, 8], F32)
        ps = [st.enter_context(nc.psum_tensor("ps%d" % i, [128, 512], F32)) for i in range(8)]
        psb = [p[:].bitcast(BF16) for p in ps]
        pb = [Buf("ps%d" % i) for i in range(8)]

        rawb = raw[:].bitcast(BF16)
        accb = acc[:].bitcast(BF16)
        hc = [rawb[:, 0:2048], rawb[:, 2048:4096]]
        hcT0 = accb[:, 0:2048].rearrange("p (f j) -> p f j", j=128)
        hcT1 = mqs[1][:].rearrange("p c l -> p (c l)")[:, 0:2048].rearrange("p (f j) -> p f j", j=128)
        hcTs = [hcT0, hcT1]
        hb = accb[:, 2048:3072]
        junkb = accb[:, 3072:4096]
        mqf = mq[:].rearrange("p c l -> p (c l)").bitcast(F32)
        mkf = mk[:].rearrange("p c l -> p (c l)").bitcast(F32)
        xo = [mqf[:, 0:1024], mqf[:, 1024:2048]]
        xn = [mkf[:, 0:1024], mkf[:, 1024:2048]]
        T1 = raw[0:4, 4:4 + L]
        T2 = acc[0:4, 0:L]
        T3 = mqf[0:4, 0:L]

        b_hT = [Buf("hT%d" % t) for t in range(NT)]
        b_W = [Buf("W0"), Buf("W1")]
        b_raw, b_acc = Buf("raw"), Buf("acc")
        b_mqs = [[Buf("mq%d_%d" % (p_, g)) for g in range(5)] for p_ in range(2)]
        b_mks = [[Buf("mk%d_%d" % (p_, g)) for g in range(5)] for p_ in range(2)]
        b_mq, b_mk = b_mqs[0], b_mks[0]
        b_dq = [Buf("dq%d" % g) for g in range(5)]
        b_dk = [Buf("dk%d" % g) for g in range(5)]
        b_dv = [Buf("dv%d" % t) for t in range(NT)]
        b_CTs = [Buf(), Buf()]
        b_CTdb = Buf()
        b_grow = [Buf("g%d" % i) for i in range(4)]
        b_const = Buf("const")
        b_lp = Buf("layerparams")
        b_lam = Buf("lam")
        b_gT, b_decs, b_decb = Buf(), Buf(), Buf()
        b_ssq = [Buf("ssq%d" % t) for t in range(NT)]
        b_rsd, b_ss = Buf(), Buf()
        b_vext, b_th, b_zs, b_kw, b_SwT, b_hg, b_hout, b_sm = ([Buf(), Buf(), Buf()] for _ in range(8))
        b_PT = [Buf() for _ in range(6)]
        b_a0 = [Buf(), Buf()]
        b_hc, b_xo, b_xn = [Buf(), Buf()], [Buf(), Buf()], [Buf(), Buf()]
        b_hcTs = [Buf(), Buf()]
        b_hb, b_junk = Buf(), Buf()
        b_dummy = Buf()
        b_xres = [Buf("xres%d" % t) for t in range(NT)]
        b_hcat = [Buf("hcat%d" % t) for t in range(NT)]
        b_out = Buf("out")

        def MM(out, lhsT, rhs, start, stop, R, W):
            P.add("pe", lambda e: e.matmul(out, lhsT, rhs, start=start, stop=stop), R, W)

        def TR(out, in_, ident, R, W):
            P.add("pe", lambda e: e.transpose(out, in_, ident), R, W)

        def ACT(out, in_, func, R, W, **kw_):
            P.add("act", lambda e: e.activation(out, in_, func, **kw_), R, W)

        def TS(out, in0, s1, s2, op0, op1, R, W, eng="dve"):
            if op1 is None:
                P.add(eng, lambda e: e.tensor_scalar(out, in0, s1, None, op0), R, W)
            else:
                P.add(eng, lambda e: e.tensor_scalar(out, in0, s1, s2, op0, op1), R, W)

        def TT(out, in0, in1, op, R, W, eng="dve"):
            P.add(eng, lambda e: e.tensor_tensor(out, in0, in1, op), R, W)

        def STT(out, in0, sc, in1, op0, op1, R, W):
            P.add("dve", lambda e: e.scalar_tensor_tensor(out, in0, sc, in1, op0, op1), R, W)

        def CP(out, in_, R, W, eng="dve"):
            if eng == "act":
                P.add("act", lambda e: e.copy(out, in_), R, W)
            else:
                P.add(eng, lambda e: e.tensor_copy(out, in_), R, W)

        def MS(ap, val, W, eng="pool"):
            P.add(eng, lambda e: e.memset(ap, val), [], W)

        def DMA(q, out, in_, R, W):
            P.add(q, lambda e: e.dma_start(out=out, in_=in_), R, W, dma=True)

        def FENCE(bufs):
            P.add("pool", lambda e: e.memset(dummy[0:1, 0:1], 0.0), [], list(bufs) + [b_dummy])

        MUL, ADD, SUB, MAX = ALU.mult, ALU.add, ALU.subtract, ALU.max

        DMA("sp", identb[:], identb_d, [], [b_const])
        DMA("sp", identf[:], identf_d, [], [b_const])
        DMA("sp", maskb[:], mask_d, [], [b_const])
        DMA("sp", sel[:], sel_d, [], [b_const])
        for i in range(3):
            MS(vext[i][:, 256:258], 1.0, [b_vext[i]])
        for t in range(NT):
            MS(dv[:, t, 128:130], 1.0, [b_dv[t]])
        MS(raw[:, 0:3], 0.0, [b_raw])

        M_BUFS = [b_raw, b_acc] + b_mqs[0] + b_mks[0] + b_mqs[1] + b_mks[1]
        O_BUFS = b_hc + b_xo + b_xn + b_hcTs + [b_hb, b_junk]

        wslot = [0]

        def load_weights(src, ncols, krows=8):
            i = wslot[0]
            wslot[0] ^= 1
            for kc in range(krows):
                DMA("pool", Wb[i][:, kc, 0:ncols], src[kc * 128:(kc + 1) * 128, :], [], [b_W[i]])
            return Wb[i], b_W[i]

        def norm_tile(t, xt, bx, bank):
            pos0, n = TP[t]
            ACT(junkb[:n, :], xt, AF.Square, [bx], [b_junk, b_ss], accum_out=ss[:n, 0:1])
            ACT(ss[:n, 1:2], ss[:n, 0:1], AF.Ln, [b_ss], [b_ss], scale=1.0 / D_MODEL, bias=EPS)
            ACT(ss[:n, 2:3], ss[:n, 1:2], AF.Exp, [b_ss], [b_ss], scale=-0.5)
            STT(hb[:n, :], xt, ss[:n, 2:3], grow[:n, 0, :], MUL, MUL, [bx, b_ss, b_grow[0]], [b_hb])
            for kc in range(8):
                TR(psb[bank][:, kc * 128:kc * 128 + n], hb[:n, kc * 128:(kc + 1) * 128], identb[:n, :n],
                   [b_hb, b_const], [pb[bank]])
            src = psb[bank][:, 0:1024].rearrange("p (k j) -> p k j", j=128)[:, :, 0:n]
            CP(hT[:, :, pos0:pos0 + n], src, [pb[bank]], [b_hT[t]], eng="act")

        def load_gpre(l):
            DMA("sp", grow[:, 0, :], grow_d[l, 0:1, :].partition_broadcast(128), [], [b_grow[0]])

        def load_layer_params(l):
            for i in range(1, 4):
                DMA("sp", grow[:, i, :], grow_d[l, i:i + 1, :].partition_broadcast(128), [], [b_grow[i]])
            DMA("sp", convw[:], convw_d[l], [], [b_lp])
            DMA("sp", convb[:], convb_d[l], [], [b_lp])
            DMA("sp", bg[:, 0:2], bg_d[l], [], [b_lp])
            DMA("sp", lamt[:], lam_d[l].partition_broadcast(128), [], [b_lam])
            TS(bg[:, 2:3], bg[:, 1:2], -1.0, None, MUL, None, [b_lp], [b_lp])
            TS(grow[:, 2, :], grow[:, 2, :], 0.25, None, MUL, None, [b_grow[2]], [b_grow[2]], eng="pool")
            lam_init = 0.8 - 0.6 * math.exp(-0.3 * l)
            TT(lamt[:, 0:64], lamt[:, 0:64], lamt[:, 64:128], MUL, [b_lam], [b_lam])
            TT(lamt[:, 128:192], lamt[:, 128:192], lamt[:, 192:256], MUL, [b_lam], [b_lam])
            P.add("dve", lambda e: e.reduce_sum(lams[:, 0:1], lamt[:, 0:64], axis=AX.X), [b_lam], [b_lam])
            P.add("dve", lambda e: e.reduce_sum(lams[:, 1:2], lamt[:, 128:192], axis=AX.X), [b_lam], [b_lam])
            ACT(lams[:, 2:4], lams[:, 0:2], AF.Exp, [b_lam], [b_lam])
            TT(lams[:, 4:5], lams[:, 2:3], lams[:, 3:4], SUB, [b_lam], [b_lam])
            TS(lams[:, 4:5], lams[:, 4:5], lam_init, None, ADD, None, [b_lam], [b_lam])
            TS(lams[:, 5:6], lams[:, 4:5], -1.0, None, MUL, None, [b_lam], [b_lam])
            return lam_init

        def gates_phase(l):
            W, bW = load_weights(wg_d[l], 8)
            allm = M_BUFS
            for g, (p0, w, ts) in enumerate(GR):
                bi, bf_ = (0, 1) if g % 2 == 0 else (2, 3)
                hr = [b_hT[t] for t in ts]
                for kc in range(8):
                    MM(ps[bi][0:4, 0:w], W[:, kc, 0:4], hT[:, kc, p0:p0 + w], kc == 0, kc == 7, hr + [bW], [pb[bi]])
                for kc in range(8):
                    MM(ps[bf_][0:4, 0:w], W[:, kc, 4:8], hT[:, kc, p0:p0 + w], kc == 0, kc == 7, hr + [bW], [pb[bf_]])
                TS(T1[:, p0:p0 + w], ps[bi][0:4, 0:w], bg[:, 0:1], None, ADD, None, [pb[bi], b_lp], allm)
                ACT(T2[:, p0:p0 + w], ps[bf_][0:4, 0:w], AF.Exp, [pb[bf_], b_lp], allm, scale=-1.0, bias=bg[:, 2:3])
            ACT(T2, T2, AF.Ln, allm, allm, bias=1.0)
            P.add("dve", lambda e: e.tensor_tensor_scan(T3, T2, T2, 0.0, ADD, MAX), allm, allm)
            TT(T1, T1, T3, ADD, allm, allm)
            P.add("dve", lambda e: e.tensor_tensor_scan(T2, T1, T1, 0.0, MAX, MAX), allm, allm)
            ge = T2[:, 15:L:128]
            TS(decs[:, 0:1], T2[:, 15:16], -1.0, None, MUL, None, allm, [b_decs])
            TT(decs[:, 1:NT], T2[:, 15:L - 128:128], T2[:, 143:L:128], SUB, allm, [b_decs])
            ACT(decs[:, 0:NT], decs[:, 0:NT], AF.Exp, [b_decs], [b_decs])
            TS(decs[:, NT:2 * NT], decs[:, 0:NT], 1.0 / 16, None, MUL, None, [b_decs], [b_decs])
            gl = T2[:, 143:L:128].unsqueeze(2).to_broadcast([4, 16, 128])
            for Tx in (T1, T3):
                TT(Tx[:, 16:L].rearrange("p (t j) -> p t j", j=128), Tx[:, 16:L].rearrange("p (t j) -> p t j", j=128),
                   gl, SUB, allm, allm)
                TS(Tx[:, 0:16], Tx[:, 0:16], T2[:, 15:16], None, SUB, None, allm, allm)
                ACT(Tx, Tx, AF.Exp, allm, allm)
            for h in range(4):
                MM(ps[4][:, h * 2 * NT:(h + 1) * 2 * NT], sel[:, h * 128:(h + 1) * 128], decs[:, :], True, True,
                   [b_const, b_decs], [pb[4]])
            CP(decb[:].rearrange("p h c -> p (h c)"), ps[4][:, 0:8 * NT], [pb[4]], [b_decb])
            for t in range(NT):
                p0, n = TP[t]
                TR(ps[5][:n, t * 8:t * 8 + 4], T1[:, p0:p0 + n], identf[0:4, 0:4], allm + [b_const], [pb[5]])
                TR(ps[5][:n, t * 8 + 4:t * 8 + 8], T3[:, p0:p0 + n], identf[0:4, 0:4], allm + [b_const], [pb[5]])
            CP(gatesT[0:16, 0, 0:8], ps[5][0:16, 0:8], [pb[5]], [b_gT])
            CP(gatesT[:, 1:NT, 0:8], ps[5][:, 8:8 * NT].rearrange("p (t c) -> p t c", c=8), [pb[5]], [b_gT])
            TS(gatesT[0:16, 0, 8:12], gatesT[0:16, 0, 0:4], 1.0 / 16, None, MUL, None, [b_gT], [b_gT])
            TS(gatesT[:, 1:NT, 8:12], gatesT[:, 1:NT, 0:4], 1.0 / 16, None, MUL, None, [b_gT], [b_gT])

        def mlstm_prep_gen(l, h, W, bW, banks):
            hp = h % 2
            MS(raw[:, 0:3], 0.0, [b_raw])
            bi = 0
            for c in range(4):
                dst, bdst = (mqs[hp], b_mqs[hp]) if c < 2 else (mks[hp], b_mks[hp])
                cc = (0 if c < 2 else 8) + h * 2 + (c % 2)
                for g, (p0, w, ts) in enumerate(GR):
                    bk = banks[bi % len(banks)]
                    bi += 1
                    hr_ = [b_hT[t] for t in ts]
                    for kc in range(8):
                        MM(ps[bk][:, 0:w], W[:, kc, c * 128:(c + 1) * 128], hT[:, kc, p0:p0 + w], kc == 0, kc == 7,
                           hr_ + [bW], [pb[bk]])
                    CP(raw[:, 3 + p0:3 + p0 + w], ps[bk][:, 0:w], [pb[bk]], [b_raw], eng="act")
                    yield
                TS(acc[:, 0:L], raw[:, 3:3 + L], convw[:, cc, 3:4], None, MUL, None, [b_raw, b_lp], [b_acc])
                yield
                for j in (2, 1, 0):
                    STT(acc[:, 0:L], raw[:, j:j + L], convw[:, cc, j:j + 1], acc[:, 0:L], MUL, ADD,
                        [b_raw, b_acc, b_lp], [b_acc])
                    yield
                ACT(dst[:, c % 2, :], acc[:, 0:L], AF.Silu, [b_acc, b_lp], bdst, bias=convb[:, cc:cc + 1])
                yield

        def mlstm_head(l, h, W, bW, hook=None):
            hp = h % 2
            mq, mk, b_mq, b_mk = mqs[hp], mks[hp], b_mqs[hp], b_mks[hp]
            MS(CTs[0][:], 0.0, [b_CTs[0]])

            def A_pe(t):
                p0, n = TP[t]
                g = TG[t]
                for kc in range(8):
                    MM(ps[0][:n, 0:256], hT[:, kc, p0:p0 + n], W[:, kc, 512:768], kc == 0, kc == 7, [b_hT[t], bW], [pb[0]])
                for kc in range(8):
                    MM(ps[1][:n, 0:512], hT[:, kc, p0:p0 + n], W[:, kc, 768:1280], kc == 0, kc == 7, [b_hT[t], bW], [pb[1]])
                for c in range(2):
                    TR(psb[2][:n, c * 128:(c + 1) * 128], mk[:, c, p0:p0 + n], identb[:, :], [b_mk[g], b_const], [pb[2]])
                for c in range(2):
                    MM(ps[2][:n, 128:128 + n], mk[:, c, p0:p0 + n], mq[:, c, p0:p0 + n], c == 0, c == 1, [b_mk[g], b_mq[g]], [pb[2]])

            def A_other(t):
                p0, n = TP[t]
                i = t % 3
                CP(vext[i][:n, 0:256], ps[0][:n, 0:256], [pb[0]], [b_vext[i]], eng="act")
                ACT(th[i][:n, :], ps[1][:n, :], AF.Tanh, [pb[1]], [b_th[i]], scale=0.5)
                TS(kw[i][:n, :], psb[2][:n, 0:256], gatesT[:n, t, h:h + 1], None, MUL, None, [pb[2], b_gT], [b_kw[i]])
                STT(SwT[i][:n, :n], ps[2][:n, 128:128 + n], gatesT[:n, t, 8 + h:9 + h], maskb[:n, :n], MUL, MUL,
                    [pb[2], b_gT, b_const], [b_SwT[i]])
                STT(zs[i][:n, 0:256], th[i][:n, 256:512], 1.0, ps[1][:n, 256:512], ADD, MUL, [pb[1], b_th[i]], [b_zs[i]])
                TT(zs[i][:n, 0:256], zs[i][:n, 0:256], grow[:n, 2, h * 256:(h + 1) * 256], MUL, [b_zs[i], b_grow[2]], [b_zs[i]])

            def CTDB(t):
                CTo, bCTo = CTs[t % 2], b_CTs[t % 2]
                TS(CTdb[:], CTo[:], decb[:, h, NT + t:NT + t + 1], None, MUL, None, [bCTo, b_decb], [b_CTdb])

            def B_pe(t):
                p0, n = TP[t]
                g = TG[t]
                ia = t % 3
                for c in range(2):
                    MM(ps[5 + c][:, 0:257], kw[ia][:n, c * 128:(c + 1) * 128], vext[ia][:n, 0:257], True, True,
                       [b_kw[ia], b_vext[ia]], [pb[5 + c]])
                MM(ps[4][:n, 0:257], SwT[ia][:n, :n], vext[ia][:n, 0:257], True, False, [b_SwT[ia], b_vext[ia]], [pb[4]])
                for c in range(2):
                    MM(ps[4][:n, 0:257], mq[:, c, p0:p0 + n], CTdb[:, c, :], False, c == 1, [b_mq[g], b_CTdb], [pb[4]])

            def B_rest(t):
                p0, n = TP[t]
                i = t % 2
                CTo, bCTo = CTs[t % 2], b_CTs[t % 2]
                CTn, bCTn = CTs[(t + 1) % 2], b_CTs[(t + 1) % 2]
                for c in range(2):
                    STT(CTn[:, c, :], CTo[:, c, :], decb[:, h, t:t + 1], ps[5 + c][:, 0:257], MUL, ADD,
                        [pb[5 + c], bCTo, b_decb], [bCTn])
                TT(sm[i][:n, 0:1], ps[4][:n, 256:257], gatesT[:n, t, 4 + h:5 + h], MAX, [pb[4], b_gT], [b_sm[i]])
                STT(sm[i][:n, 0:1], ps[4][:n, 256:257], -1.0, sm[i][:n, 0:1], MUL, MAX, [pb[4], b_sm[i]], [b_sm[i]])
                P.add("dve", lambda e, o_=sm[i][:n, 1:2], i_=sm[i][:n, 0:1]: e.reciprocal(o_, i_), [b_sm[i]], [b_sm[i]])
                ACT(hrw[i][:n, :], ps[4][:n, 0:256], AF.Copy, [pb[4], b_sm[i]], [b_hr[i]], scale=sm[i][:n, 1:2])
                if t + 1 < NT:
                    CTDB(t + 1)

            def B2(t):
                p0, n = TP[t]
                i = t % 2
                ia = t % 3
                STT(hg[i][:n, 0:256], th[ia][:n, 0:256], 1.0, hrw[i][:n, :], ADD, MUL, [b_th[ia], b_hr[i]], [b_hg[i]])
                ACT(junkh[i][:n, :], hg[i][:n, 0:256], AF.Square, [b_hg[i]], [b_junkh[i], b_ssq[t]], accum_out=ssq[:n, t, h:h + 1])
                TT(hout[i][:n, 0:256], hg[i][:n, 0:256], zs[ia][:n, 0:256], MUL, [b_hg[i], b_zs[ia]], [b_hout[i]])
                DMA("sp", hcat_d[p0:p0 + n, h * 256:(h + 1) * 256], hout[i][:n, 0:256], [b_hout[i]], [b_hcat[t]])

            CTDB(0)
            A_pe(0)
            A_other(0)
            A_pe(1)
            A_other(1)
            for t in range(2, NT):
                B_pe(t - 2)
                A_pe(t)
                B_rest(t - 2)
                A_other(t)
                B2(t - 2)
                if hook is not None:
                    hook(t)
            for t in (NT - 2, NT - 1):
                B_pe(t)
                B_rest(t)
                B2(t)

        junkh = [T("junkh0", [128, 256], BF16), T("junkh1", [128, 256], BF16)]
        b_junkh = [Buf(), Buf()]
        hrw = [T("hr0", [128, 256], F32), T("hr1", [128, 256], F32)]
        b_hr = [Buf(), Buf()]

        def diff_head(l, h, W, bW, lam_init):
            scale = 64 ** -0.5
            bankrot = [0]
            for c, (dst, bdst) in enumerate(((dq, b_dq), (dk, b_dk))):
                for g, (p0, w, ts) in enumerate(GR):
                    bk = bankrot[0]
                    bankrot[0] ^= 1
                    hr = [b_hT[t] for t in ts]
                    for kc in range(8):
                        MM(ps[bk][:, 0:w], W[:, kc, c * 128:(c + 1) * 128], hT[:, kc, p0:p0 + w], kc == 0, kc == 7,
                           hr + [bW], [pb[bk]])
                    CP(dst[:, p0:p0 + w], ps[bk][:, 0:w], [pb[bk]], [bdst[g]], eng=("act" if g % 2 else "dve"))
            for t in range(NT):
                p0, n = TP[t]
                bk = t % 2
                for kc in range(8):
                    MM(ps[bk][:n, 0:128], hT[:, kc, p0:p0 + n], W[:, kc, 256:384], kc == 0, kc == 7, [b_hT[t], bW], [pb[bk]])
                CP(dv[:n, t, 0:128], ps[bk][:n, 0:128], [pb[bk]], [b_dv[t]], eng=("act" if t % 2 else "dve"))
            QG = [[0]] + [list(range(a_, min(a_ + 3, NT))) for a_ in range(1, NT, 3)]
            items = []
            for gq, Q in enumerate(QG):
                items.append(("z", gq, None, False))
                for j in range(0, Q[-1] + 1):
                    items.append(("a", gq, j, j == Q[-1]))
            SBP = [(0, 1), (2, 3)]
            cfac = 0.5 * (1.0 - lam_init)

            def S(k):
                ty, gq, j, last = items[k]
                Q = QG[gq]
                nq = TP[Q[0]][1]
                sb = SBP[k % 2]
                if ty == "z":
                    for slot, t in enumerate(Q):
                        p0, n = TP[t]
                        for kc in range(8):
                            MM(ps[sb[0]][:n, slot * 128:slot * 128 + 128], hT[:, kc, p0:p0 + n], W[:, kc, 384:512], kc == 0, kc == 7,
                               [b_hT[t], bW], [pb[sb[0]]])
                    return
                qs = [t for t in Q if t >= j]
                ps0 = TP[qs[0]][0]
                wq = sum(TP[t][1] for t in qs)
                kp0, kn = TP[j]
                for c in range(2):
                    cs = slice(64 * c, 64 * c + 64)
                    MM(ps[sb[c]][:kn, 0:wq], dk[cs, kp0:kp0 + kn], dq[cs, ps0:ps0 + wq], True, True,
                       [b_dk[TG[j]]] + [b_dq[TG[t]] for t in qs], [pb[sb[c]]])

            def E(k):
                ty, gq, j, last = items[k]
                Q = QG[gq]
                nq = TP[Q[0]][1]
                gp = gq % 2
                sb = SBP[k % 2]
                if ty == "z":
                    wz = len(Q) * 128
                    ACT(th[gp][:nq, 0:wz], ps[sb[0]][:nq, 0:wz], AF.Tanh, [pb[sb[0]]], [b_th[gp]], scale=0.5)
                    TS(th[gp][:nq, 0:wz], th[gp][:nq, 0:wz], cfac, cfac, MUL, ADD, [b_th[gp]], [b_th[gp]])
                    TT(zs[gp][:nq, 0:wz], ps[sb[0]][:nq, 0:wz], th[gp][:nq, 0:wz], MUL, [pb[sb[0]], b_th[gp]], [b_zs[gp]])
                    zv = zs[gp][:nq, 0:wz].rearrange("p (s d) -> p s d", d=128)
                    TT(zv, zv, grow[:nq, 3, h * 128:(h + 1) * 128].unsqueeze(1).to_broadcast([nq, len(Q), 128]), MUL,
                       [b_zs[gp], b_grow[3]], [b_zs[gp]])
                    return
                qs = [t for t in Q if t >= j]
                wq = sum(TP[t][1] for t in qs)
                kp0, kn = TP[j]
                for c in range(2):
                    pi = (k % 3) * 2 + c
                    ACT(PT[pi][:kn, 0:wq], ps[sb[c]][:kn, 0:wq], AF.Exp, [pb[sb[c]]], [b_PT[pi]], scale=scale)
                    if j >= Q[0]:
                        TT(PT[pi][:kn, 0:kn], PT[pi][:kn, 0:kn], maskb[:kn, :kn], MUL, [b_PT[pi], b_const], [b_PT[pi]], eng="pool")

            def V(k):
                ty, gq, j, last = items[k]
                if ty == "z":
                    return
                Q = QG[gq]
                nq = TP[Q[0]][1]
                nS = len(Q)
                gp = gq % 2
                pa = [4, 5] if gp == 0 else [6, 7]
                qs = [t for t in Q if t >= j]
                kp0, kn = TP[j]
                for bi_, t in enumerate(qs):
                    slot = t - Q[0]
                    n = TP[t][1]
                    for c in range(2):
                        pi = (k % 3) * 2 + c
                        P.add("pe", lambda e, o_=ps[pa[c]][:n, slot * 129:slot * 129 + 129], l_=PT[pi][:kn, bi_ * 128:bi_ * 128 + n],
                              r_=dv[:kn, j, 0:129], st_=(j == 0 and slot == 0), sp_=(j == t):
                              e.matmul(o_, l_, r_, start=st_, stop=sp_, skip_group_check=True),
                              [b_PT[pi], b_dv[j]], [pb[pa[c]]])
                if not last:
                    return
                A0 = ps[pa[0]][:nq, 0:nS * 129].rearrange("p (s d) -> p s d", d=129)
                A1 = ps[pa[1]][:nq, 0:nS * 129].rearrange("p (s d) -> p s d", d=129)
                r0 = sm[gp][:nq, 0:nS]
                r1 = sm[gp][:nq, 4:4 + nS]
                P.add("dve", lambda e: e.reciprocal(r0.unsqueeze(2), A0[:, :, 128:129]), [pb[pa[0]]], [b_sm[gp]])
                P.add("dve", lambda e: e.reciprocal(r1.unsqueeze(2), A1[:, :, 128:129]), [pb[pa[1]]], [b_sm[gp]])
                TS(r1, r1, lams[:nq, 5:6], None, MUL, None, [b_sm[gp], b_lam], [b_sm[gp]])
                a0v = a0[gp][:nq, 0:nS * 128].rearrange("p (s d) -> p s d", d=128)
                hgv = hg[gp][:nq, 0:nS * 128].rearrange("p (s d) -> p s d", d=128)
                TT(a0v, A0[:, :, 0:128], r0.unsqueeze(2).to_broadcast([nq, nS, 128]), MUL, [pb[pa[0]], b_sm[gp]], [b_a0[gp]])
                TT(hgv, A1[:, :, 0:128], r1.unsqueeze(2).to_broadcast([nq, nS, 128]), MUL, [pb[pa[1]], b_sm[gp]], [b_hg[gp]])
                TT(hgv, hgv, a0v, ADD, [b_hg[gp], b_a0[gp]], [b_hg[gp]])
                for slot, t in enumerate(Q):
                    ACT(junkh[gp][:nq, 0:128], hg[gp][:nq, slot * 128:(slot + 1) * 128], AF.Square, [b_hg[gp]],
                        [b_junkh[gp], b_ssq[t]], accum_out=ssq[:nq, t, 4 + h:5 + h])
                TT(hout[gp][:nq, 0:nS * 128], hg[gp][:nq, 0:nS * 128], zs[gp][:nq, 0:nS * 128], MUL, [b_hg[gp], b_zs[gp]], [b_hout[gp]])
                p0 = TP[Q[0]][0]
                cols = slice(1024 + h * 128, 1024 + (h + 1) * 128)
                if nS == 1:
                    DMA("sp", hcat_d[p0:p0 + nq, cols], hout[gp][:nq, 0:128], [b_hout[gp]], [b_hcat[t] for t in Q])
                else:
                    DMA("sp", hcat_d[p0:p0 + nS * 128, cols].rearrange("(s p) d -> p s d", p=128),
                        hout[gp][:nq, 0:nS * 128].rearrange("p (s d) -> p s d", d=128), [b_hout[gp]], [b_hcat[t] for t in Q])

            NI = len(items)
            for k in range(NI):
                S(k)
                E(k)
                if k >= 1:
                    V(k - 1)
            V(NI - 1)

        def out_phase(l, s, last):
            FENCE(M_BUFS + O_BUFS)
            WA, bWA = load_weights(wo_d[l, 0:1024, :], 1024)
            WB, bWB = load_weights(wo_d[l, 1024:2048, :], 1024)
            if not last:
                load_gpre(l + 1)
            def O1a(t):
                p0, n = TP[t]
                i = t % 2
                DMA("sp", hc[i][:n, :], hcat_d[p0:p0 + n, :], [b_hcat[t]], [b_hc[i]])
                ACT(rsd[:n, 0:4], ssq[:n, t, 0:4], AF.Ln, [b_ssq[t]], [b_rsd], scale=1.0 / 1024, bias=EPS)
                ACT(rsd[:n, 4:12], ssq[:n, t, 4:12], AF.Ln, [b_ssq[t]], [b_rsd], scale=1.0 / 128, bias=EPS)
                ACT(rsd[:n, 0:12], rsd[:n, 0:12], AF.Exp, [b_rsd], [b_rsd], scale=-0.5)
                TT(hc[i][:n, 0:1024].rearrange("p (h d) -> p h d", d=256), hc[i][:n, 0:1024].rearrange("p (h d) -> p h d", d=256),
                   rsd[:n, 0:4].unsqueeze(2).to_broadcast([n, 4, 256]), MUL, [b_hc[i], b_rsd], [b_hc[i]])
                TT(hc[i][:n, 1024:2048].rearrange("p (h d) -> p h d", d=128), hc[i][:n, 1024:2048].rearrange("p (h d) -> p h d", d=128),
                   rsd[:n, 4:12].unsqueeze(2).to_broadcast([n, 8, 128]), MUL, [b_hc[i], b_rsd], [b_hc[i]])
                for half in range(2):
                    for f in range(8):
                        fc = half * 8 + f
                        TR(psb[half][:, f * 128:f * 128 + n], hc[i][:n, fc * 128:(fc + 1) * 128], identb[:n, :n],
                           [b_hc[i], b_const], [pb[half]])
                    src = psb[half][:, 0:1024].rearrange("p (k j) -> p k j", j=128)[:, :, 0:n]
                    CP(hcTs[i][:, half * 8:(half + 1) * 8, 0:n], src, [pb[half]], [b_hcTs[i]], eng=("act" if half else "dve"))

            def O1b(t):
                p0, n = TP[t]
                i = t % 2
                if l == 0:
                    src = meta_d if t == 0 else x_d[s, p0 - 16:p0 - 16 + n, :]
                    DMA("sp", xo[i][:n, :], src, [], [b_xo[i]])
                else:
                    DMA("sp", xo[i][:n, :], xres_d[p0:p0 + n, :], [b_xres[t]], [b_xo[i]])
                ya = [2, 3] if i == 0 else [4, 5]
                for half in range(2):
                    for fc in range(16):
                        Wx, bWx = (WA, bWA) if fc < 8 else (WB, bWB)
                        MM(ps[ya[half]][:n, :], hcTs[i][:, fc, 0:n], Wx[:, fc % 8, half * 512:(half + 1) * 512], fc == 0, fc == 15,
                           [b_hcTs[i], bWx], [pb[ya[half]]])

            def O2(t):
                p0, n = TP[t]
                i = t % 2
                ya = [2, 3] if i == 0 else [4, 5]
                for half in range(2):
                    ACT(junkb[:n, 0:512], ps[ya[half]][:n, :], AF.Square, [pb[ya[half]]], [b_junk, b_ss],
                        accum_out=ss[:n, 4 + half:5 + half])
                TT(ss[:n, 6:7], ss[:n, 4:5], ss[:n, 5:6], ADD, [b_ss], [b_ss])
                ACT(ss[:n, 6:7], ss[:n, 6:7], AF.Ln, [b_ss], [b_ss], scale=1.0 / D_MODEL, bias=EPS)
                ACT(ss[:n, 7:8], ss[:n, 6:7], AF.Exp, [b_ss], [b_ss], scale=-0.5)
                for half in range(2):
                    hs = slice(half * 512, (half + 1) * 512)
                    STT(xn[i][:n, hs], ps[ya[half]][:n, :], ss[:n, 7:8], grow[:n, 1, hs], MUL, MUL,
                        [pb[ya[half]], b_ss, b_grow[1]], [b_xn[i]])
                    TT(xn[i][:n, hs], xn[i][:n, hs], xo[i][:n, hs], ADD, [b_xn[i], b_xo[i]], [b_xn[i]])
                if last:
                    if t > 0:
                        DMA("sp", out_d[s, p0 - 16:p0 - 16 + n, :], xn[i][:n, :], [b_xn[i]], [b_out])
                else:
                    DMA("sp", xres_d[p0:p0 + n, :], xn[i][:n, :], [b_xn[i]], [b_xres[t]])
                    norm_tile(t, xn[i][:n, :], b_xn[i], 6 + i)

            O1a(0)
            for t in range(NT):
                if t + 1 < NT:
                    O1a(t + 1)
                O1b(t)
                if t >= 1:
                    O2(t - 1)
            O2(NT - 1)
            FENCE(M_BUFS + O_BUFS)

        for s in range(n_seq):
            load_gpre(0)
            FENCE(M_BUFS + O_BUFS)
            for t in range(NT):
                p0, n = TP[t]
                i = t % 2
                src = meta_d if t == 0 else x_d[s, p0 - 16:p0 - 16 + n, :]
                DMA("sp", xo[i][:n, :], src, [], [b_xo[i]])
                norm_tile(t, xo[i][:n, :], b_xo[i], 6 + i)
            FENCE(M_BUFS + O_BUFS)
            for l in range(n_layers):
                lam_init = load_layer_params(l)
                gates_phase(l)
                nxt = load_weights(wm_d[l, 0], 1280)
                for _ in mlstm_prep_gen(l, 0, nxt[0], nxt[1], [0, 1]):
                    pass
                for h in range(4):
                    W, bW = nxt
                    nxt = load_weights(wm_d[l, h + 1], 1280) if h < 3 else load_weights(wd_d[l, 0], 512)
                    hook = None
                    gen = None
                    if h < 3:
                        gen = mlstm_prep_gen(l, h + 1, nxt[0], nxt[1], [3, 7])

                        def hook(t, gen=gen):
                            if t >= 7:
                                for _ in range(4):
                                    next(gen, None)
                    mlstm_head(l, h, W, bW, hook)
                    if gen is not None:
                        for _ in gen:
                            pass
                for h in range(8):
                    W, bW = nxt
                    if h < 7:
                        nxt = load_weights(wd_d[l, h + 1], 512)
                    diff_head(l, h, W, bW, lam_init)
                out_phase(l, s, l == n_layers - 1)
        if os.environ.get('MK_SBUF'):
            print('SBUF remaining', nc.sbuf_bytes_remaining)
        P.emit(st)
    return nc


_CACHE = {}


def _host_layout(inp):
    bf = ml_dtypes.bfloat16
    w_in = np.asarray(inp["w_in"], dtype=np.float32)
    D = 1024
    sec = lambda k: w_in[:, :, k * D:(k + 1) * D] if k < 5 else None
    qm, km, vm, om, zm = (w_in[:, :, k * D:(k + 1) * D] for k in range(5))
    wg = np.ascontiguousarray(w_in[:, :, 5 * D:5 * D + 8])
    off = 5 * D + 8
    qd, kd, vd, zd = (w_in[:, :, off + k * D:off + (k + 1) * D] for k in range(4))
    wm = np.empty((DEPTH, 4, D, 1280), np.float32)
    for h in range(4):
        hs = slice(h * 256, (h + 1) * 256)
        wm[:, h] = np.concatenate([qm[:, :, hs], km[:, :, hs], vm[:, :, hs], om[:, :, hs], zm[:, :, hs]], axis=-1)
    wd = np.empty((DEPTH, 8, D, 512), np.float32)
    for h in range(8):
        hs = slice(h * 128, (h + 1) * 128)
        wd[:, h] = np.concatenate([qd[:, :, hs], kd[:, :, hs], vd[:, :, hs], zd[:, :, hs]], axis=-1)
    grow = np.stack([inp["pre_norm_g"], inp["post_norm_g"], inp["mlstm_norm_g"], inp["diff_norm_g"]], axis=1).astype(np.float32)
    convw = np.ascontiguousarray(np.asarray(inp["conv_w"], np.float32).transpose(0, 2, 1).reshape(DEPTH, 16, 128, 4).transpose(0, 2, 1, 3))
    convb = np.ascontiguousarray(np.asarray(inp["conv_b"], np.float32).reshape(DEPTH, 16, 128).transpose(0, 2, 1))
    bgt = np.ascontiguousarray(np.asarray(inp["b_gates"], np.float32).reshape(DEPTH, 2, 4).transpose(0, 2, 1))
    lam = np.concatenate([inp["lambda_q1"], inp["lambda_k1"], inp["lambda_q2"], inp["lambda_k2"]], axis=-1).astype(np.float32)[:, None, :]
    sel = np.zeros((4, 512), np.float32)
    for h in range(4):
        sel[h, h * 128:(h + 1) * 128] = 1.0
    mask = np.triu(np.ones((128, 128), np.float32)).astype(bf)
    common = {
        "meta": np.ascontiguousarray(inp["meta_tokens"], dtype=np.float32),
        "wm": wm, "wd": wd, "wg": wg, "wo": np.ascontiguousarray(inp["w_out"], dtype=np.float32),
        "grow": np.ascontiguousarray(grow), "convw": convw, "convb": convb, "bg": bgt, "lam": np.ascontiguousarray(lam),
        "identb": np.eye(128, dtype=np.float32).astype(bf), "identf": np.eye(128, dtype=np.float32),
        "mask": mask, "sel": sel,
    }
    return common


def kernel(**inputs):
    n_layers = int(os.environ.get("MK_LAYERS", DEPTH))
    key = ("nc", n_layers)
    if key not in _CACHE:
        _CACHE[key] = build_program(n_layers=n_layers, n_seq=2)
    nc = _CACHE[key]
    common = _host_layout(inputs)
    x = np.asarray(inputs["x"], dtype=np.float32)
    in_maps = []
    for c in range(8):
        m = dict(common)
        m["x"] = np.ascontiguousarray(x[2 * c:2 * c + 2])
        in_maps.append(m)
    res = run_bass_kernel_spmd(nc, in_maps, core_ids=list(range(8)))
    out = np.concatenate([r["out"] for r in res.results], axis=0)
    return out.astype(np.float32)
```

```python
import contextlib
import math
import os
import numpy as np
import ml_dtypes
import concourse.bass as bass
import concourse.mybir as mybir
from concourse.bass_utils import run_bass_kernel_spmd

F32 = mybir.dt.float32
BF16 = mybir.dt.bfloat16
AF = mybir.ActivationFunctionType
ALU = mybir.AluOpType
AX = mybir.AxisListType

ENGS = ["pe", "act", "dve", "pool", "sp"]


class Buf:
    __slots__ = ("n", "w", "r")

    def __init__(self, n=""):
        self.n = n
        self.w = None
        self.r = []


class Op:
    __slots__ = ("eng", "fn", "deps", "sig", "tok", "dma")


class Prog:
    NDMA = 8
    ROLL = 30000

    def __init__(self, nc):
        self.nc = nc
        self.ops = {e: [] for e in ENGS}
        self.dq = {e: {"next": 0, "last": [None] * self.NDMA, "cnt": [0] * self.NDMA} for e in ENGS}
        self.all_dma = []

    def add(self, eng, fn, reads=(), writes=(), dma=False):
        o = Op()
        o.eng, o.fn, o.dma, o.sig, o.tok = eng, fn, dma, False, None
        cand = []
        for b in reads:
            if b.w is not None:
                cand.append((b.w, True))
        for b in writes:
            if b.w is not None:
                cand.append((b.w, False))
            for r in b.r:
                cand.append((r, False))
        deps, seen = [], set()
        if dma:
            q = self.dq[eng]
            s = q["next"]
            q["next"] = (s + 1) % self.NDMA
            if q["last"][s] is not None:
                cand.append((q["last"][s], True))
            q["cnt"][s] += 16
            o.tok = (("d", eng, s), q["cnt"][s])
            q["last"][s] = o
            self.all_dma.append(o)
        for p, raw in cand:
            if p is o or id(p) in seen:
                continue
            if (not dma) and (not p.dma) and p.eng == eng:
                if eng == "pe":
                    continue
            seen.add(id(p))
            deps.append(p)
            if not p.dma:
                p.sig = True
        o.deps = deps
        for b in reads:
            if dma:
                b.r.append(o)
            else:
                b.r = [x for x in b.r if x.dma or x.eng != eng] + [o]
        for b in writes:
            b.w = o
            b.r = []
        self.ops[eng].append(o)
        return o

    def emit(self, stack):
        nc = self.nc
        keys = []
        for e in ENGS:
            cnt, gen, used = 0, 0, False
            for o in self.ops[e]:
                if o.dma or not o.sig:
                    continue
                cnt += 1
                if cnt > self.ROLL:
                    gen += 1
                    cnt = 1
                o.tok = (("c", e, gen), cnt)
                used = True
            if used:
                for g in range(gen + 1):
                    keys.append(("c", e, g))
            for s in range(self.NDMA):
                if self.dq[e]["cnt"][s]:
                    keys.append(("d", e, s))
        sems = {k: stack.enter_context(nc.semaphore("s_%s_%s_%d" % k)) for k in keys}
        final = {}
        for o in self.all_dma:
            final[o.tok[0]] = max(final.get(o.tok[0], 0), o.tok[1])

        def mk(e):
            def body(engine):
                waited = {}
                for o in self.ops[e]:
                    for p in o.deps:
                        k, v = p.tok
                        if waited.get(k, 0) < v:
                            engine.wait_ge(sems[k], v)
                            waited[k] = v
                    ins = o.fn(engine)
                    if o.dma:
                        ins.then_inc(sems[o.tok[0]], 16)
                    elif o.sig:
                        ins.then_inc(sems[o.tok[0]], 1)
                if e == "sp":
                    for k, v in final.items():
                        if waited.get(k, 0) < v:
                            engine.wait_ge(sems[k], v)
            return body

        with nc.Block() as block:
            block.tensor(mk("pe"))
            block.scalar(mk("act"))
            block.vector(mk("dve"))
            block.gpsimd(mk("pool"))
            block.sync(mk("sp"))


D_MODEL = 1024
SEQ = 2048
N_META = 16
L = SEQ + N_META
NT = 17
DEPTH = 4
EPS = 1e-6
TP = [(0, 16)] + [(16 + 128 * (t - 1), 128) for t in range(1, NT)]
GR = [(0, 400, [0, 1, 2, 3]), (400, 512, [4, 5, 6, 7]), (912, 512, [8, 9, 10, 11]),
      (1424, 512, [12, 13, 14, 15]), (1936, 128, [16])]
TG = {}
for _g, (_p, _w, _ts) in enumerate(GR):
    for _t in _ts:
        TG[_t] = _g


def build_program(n_layers=DEPTH, n_seq=2, dbg=False):
    nc = bass.Bass("TRN2", target_bir_lowering=False)

    def din(name, shape, dt=F32):
        return nc.dram_tensor(name, list(shape), dt, kind="ExternalInput").ap()

    x_d = din("x", [n_seq, SEQ, D_MODEL])
    meta_d = din("meta", [N_META, D_MODEL])
    wm_d = din("wm", [DEPTH, 4, D_MODEL, 1280])
    wd_d = din("wd", [DEPTH, 8, D_MODEL, 512])
    wg_d = din("wg", [DEPTH, D_MODEL, 8])
    wo_d = din("wo", [DEPTH, 2048, D_MODEL])
    grow_d = din("grow", [DEPTH, 4, D_MODEL])
    convw_d = din("convw", [DEPTH, 128, 16, 4])
    convb_d = din("convb", [DEPTH, 128, 16])
    bg_d = din("bg", [DEPTH, 4, 2])
    lam_d = din("lam", [DEPTH, 1, 256])
    identb_d = din("identb", [128, 128], BF16)
    identf_d = din("identf", [128, 128])
    mask_d = din("mask", [128, 128], BF16)
    sel_d = din("sel", [4, 512])
    out_d = nc.dram_tensor("out", [n_seq, SEQ, D_MODEL], F32, kind="ExternalOutput").ap()
    xres_d = nc.dram_tensor("xres", [L, D_MODEL], F32).ap()
    hcat_d = nc.dram_tensor("hcat", [L, 2048], BF16).ap()

    with contextlib.ExitStack() as st:
        P = Prog(nc)

        def T(name, shape, dt):
            return st.enter_context(nc.sbuf_tensor("sb_" + name, list(shape), dt))

        hT = T("hT", [128, 8, L], BF16)
        Wb = [T("W0", [128, 8, 1280], BF16), T("W1", [128, 8, 1280], BF16)]
        raw = T("raw", [128, 3 + L + 1], F32)
        acc = T("acc", [128, L + 4], F32)
        mqs = [T("mq", [128, 2, L], BF16), T("mq1", [128, 2, L], BF16)]
        mks = [T("mk", [128, 2, L], BF16), T("mk1", [128, 2, L], BF16)]
        mq, mk = mqs[0], mks[0]
        dq = T("dq", [128, L], BF16)
        dk = T("dk", [128, L], BF16)
        dv = T("dv", [128, NT, 130], BF16)
        CTs = [T("CTa", [128, 2, 257], F32), T("CTb", [128, 2, 257], F32)]
        CTdb = T("CTdb", [128, 2, 257], BF16)
        grow = T("grow", [128, 4, D_MODEL], F32)
        identb = T("identb", [128, 128], BF16)
        identf = T("identf", [128, 128], F32)
        maskb = T("maskb", [128, 128], BF16)
        sel = T("sel", [4, 512], F32)
        convw = T("convw", [128, 16, 4], F32)
        convb = T("convb", [128, 16], F32)
        bg = T("bg", [4, 4], F32)
        lamt = T("lamt", [128, 256], F32)
        lams = T("lams", [128, 8], F32)
        gatesT = T("gatesT", [128, NT, 12], F32)
        decs = T("decs", [4, 2 * NT], F32)
        decb = T("decb", [128, 4, 2 * NT], F32)
        ssq = T("ssq", [128, NT, 12], F32)
        rsd = T("rsd", [128, 16], F32)
        ss = T("ss", [128, 8], F32)
        vext = [T("vext%d" % i_, [128, 258], BF16) for i_ in range(3)]
        th = [T("th%d" % i_, [128, 512], F32) for i_ in range(3)]
        zs = [T("zs%d" % i_, [128, 384], F32) for i_ in range(3)]
        kw = [T("kw%d" % i_, [128, 256], BF16) for i_ in range(3)]
        SwT = [T("SwT%d" % i_, [128, 128], BF16) for i_ in range(3)]
        hg = [T("hg0", [128, 384], F32), T("hg1", [128, 384], F32)]
        hout = [T("hout0", [128, 384], BF16), T("hout1", [128, 384], BF16)]
        sm = [T("sm0", [128, 8], F32), T("sm1", [128, 8], F32)]
        PT = [T("PT%d" % i_, [128, 512], BF16) for i_ in range(6)]
        a0 = [T("a00", [128, 384], F32), T("a01", [128, 384], F32)]
        dummy = T("dummy", [1, 8], F32)
        ps = [st.enter_context(nc.psum_tensor("ps%d" % i, [128, 512], F32)) for i in range(8)]
        psb = [p[:].bitcast(BF16) for p in ps]
        pb = [Buf("ps%d" % i) for i in range(8)]

        rawb = raw[:].bitcast(BF16)
        accb = acc[:].bitcast(BF16)
        hc = [rawb[:, 0:2048], rawb[:, 2048:4096]]
        hcT0 = accb[:, 0:2048].rearrange("p (f j) -> p f j", j=128)
        hcT1 = mqs[1][:].rearrange("p c l -> p (c l)")[:, 0:2048].rearrange("p (f j) -> p f j", j=128)
        hcTs = [hcT0, hcT1]
        hb = accb[:, 2048:3072]
        junkb = accb[:, 3072:4096]
        mqf = mq[:].rearrange("p c l -> p (c l)").bitcast(F32)
        mkf = mk[:].rearrange("p c l -> p (c l)").bitcast(F32)
        xo = [mqf[:, 0:1024], mqf[:, 1024:2048]]
        xn = [mkf[:, 0:1024], mkf[:, 1024:2048]]
        T1 = raw[0:4, 4:4 + L]
        T2 = acc[0:4, 0:L]
        T3 = mqf[0:4, 0:L]

        b_hT = [Buf("hT%d" % t) for t in range(NT)]
        b_W = [Buf("W0"), Buf("W1")]
        b_raw, b_acc = Buf("raw"), Buf("acc")
        b_mqs = [[Buf("mq%d_%d" % (p_, g)) for g in range(5)] for p_ in range(2)]
        b_mks = [[Buf("mk%d_%d" % (p_, g)) for g in range(5)] for p_ in range(2)]
        b_mq, b_mk = b_mqs[0], b_mks[0]
        b_dq = [Buf("dq%d" % g) for g in range(5)]
        b_dk = [Buf("dk%d" % g) for g in range(5)]
        b_dv = [Buf("dv%d" % t) for t in range(NT)]
        b_CTs = [Buf(), Buf()]
        b_CTdb = Buf()
        b_grow = [Buf("g%d" % i) for i in range(4)]
        b_const = Buf("const")
        b_lp = Buf("layerparams")
        b_lam = Buf("lam")
        b_gT, b_decs, b_decb = Buf(), Buf(), Buf()
        b_ssq = [Buf("ssq%d" % t) for t in range(NT)]
        b_rsd, b_ss = Buf(), Buf()
        b_vext, b_th, b_zs, b_kw, b_SwT, b_hg, b_hout, b_sm = ([Buf(), Buf(), Buf()] for _ in range(8))
        b_PT = [Buf() for _ in range(6)]
        b_a0 = [Buf(), Buf()]
        b_hc, b_xo, b_xn = [Buf(), Buf()], [Buf(), Buf()], [Buf(), Buf()]
        b_hcTs = [Buf(), Buf()]
        b_hb, b_junk = Buf(), Buf()
        b_dummy = Buf()
        b_xres = [Buf("xres%d" % t) for t in range(NT)]
        b_hcat = [Buf("hcat%d" % t) for t in range(NT)]
        b_out = Buf("out")

        def MM(out, lhsT, rhs, start, stop, R, W):
            P.add("pe", lambda e: e.matmul(out, lhsT, rhs, start=start, stop=stop), R, W)

        def TR(out, in_, ident, R, W):
            P.add("pe", lambda e: e.transpose(out, in_, ident), R, W)

        def ACT(out, in_, func, R, W, **kw_):
            P.add("act", lambda e: e.activation(out, in_, func, **kw_), R, W)

        def TS(out, in0, s1, s2, op0, op1, R, W, eng="dve"):
            if op1 is None:
                P.add(eng, lambda e: e.tensor_scalar(out, in0, s1, None, op0), R, W)
            else:
                P.add(eng, lambda e: e.tensor_scalar(out, in0, s1, s2, op0, op1), R, W)

        def TT(out, in0, in1, op, R, W, eng="dve"):
            P.add(eng, lambda e: e.tensor_tensor(out, in0, in1, op), R, W)

        def STT(out, in0, sc, in1, op0, op1, R, W):
            P.add("dve", lambda e: e.scalar_tensor_tensor(out, in0, sc, in1, op0, op1), R, W)

        def CP(out, in_, R, W, eng="dve"):
            if eng == "act":
                P.add("act", lambda e: e.copy(out, in_), R, W)
            else:
                P.add(eng, lambda e: e.tensor_copy(out, in_), R, W)

        def MS(ap, val, W, eng="pool"):
            P.add(eng, lambda e: e.memset(ap, val), [], W)

        def DMA(q, out, in_, R, W):
            P.add(q, lambda e: e.dma_start(out=out, in_=in_), R, W, dma=True)

        def FENCE(bufs):
            P.add("pool", lambda e: e.memset(dummy[0:1, 0:1], 0.0), [], list(bufs) + [b_dummy])

        MUL, ADD, SUB, MAX = ALU.mult, ALU.add, ALU.subtract, ALU.max

        DMA("sp", identb[:], identb_d, [], [b_const])
        DMA("sp", identf[:], identf_d, [], [b_const])
        DMA("sp", maskb[:], mask_d, [], [b_const])
        DMA("sp", sel[:], sel_d, [], [b_const])
        for i in range(3):
            MS(vext[i][:, 256:258], 1.0, [b_vext[i]])
        for t in range(NT):
            MS(dv[:, t, 128:130], 1.0, [b_dv[t]])
        MS(raw[:, 0:3], 0.0, [b_raw])

        M_BUFS = [b_raw, b_acc] + b_mqs[0] + b_mks[0] + b_mqs[1] + b_mks[1]
        O_BUFS = b_hc + b_xo + b_xn + b_hcTs + [b_hb, b_junk]

        wslot = [0]

        def load_weights(src, ncols, krows=8):
            i = wslot[0]
            wslot[0] ^= 1
            for kc in range(krows):
                DMA("pool", Wb[i][:, kc, 0:ncols], src[kc * 128:(kc + 1) * 128, :], [], [b_W[i]])
            return Wb[i], b_W[i]

        def norm_tile(t, xt, bx, bank):
            pos0, n = TP[t]
            ACT(junkb[:n, :], xt, AF.Square, [bx], [b_junk, b_ss], accum_out=ss[:n, 0:1])
            ACT(ss[:n, 1:2], ss[:n, 0:1], AF.Ln, [b_ss], [b_ss], scale=1.0 / D_MODEL, bias=EPS)
            ACT(ss[:n, 2:3], ss[:n, 1:2], AF.Exp, [b_ss], [b_ss], scale=-0.5)
            STT(hb[:n, :], xt, ss[:n, 2:3], grow[:n, 0, :], MUL, MUL, [bx, b_ss, b_grow[0]], [b_hb])
            for kc in range(8):
                TR(psb[bank][:, kc * 128:kc * 128 + n], hb[:n, kc * 128:(kc + 1) * 128], identb[:n, :n],
                   [b_hb, b_const], [pb[bank]])
            src = psb[bank][:, 0:1024].rearrange("p (k j) -> p k j", j=128)[:, :, 0:n]
            CP(hT[:, :, pos0:pos0 + n], src, [pb[bank]], [b_hT[t]], eng="act")

        def load_gpre(l):
            DMA("sp", grow[:, 0, :], grow_d[l, 0:1, :].partition_broadcast(128), [], [b_grow[0]])

        def load_layer_params(l):
            for i in range(1, 4):
                DMA("sp", grow[:, i, :], grow_d[l, i:i + 1, :].partition_broadcast(128), [], [b_grow[i]])
            DMA("sp", convw[:], convw_d[l], [], [b_lp])
            DMA("sp", convb[:], convb_d[l], [], [b_lp])
            DMA("sp", bg[:, 0:2], bg_d[l], [], [b_lp])
            DMA("sp", lamt[:], lam_d[l].partition_broadcast(128), [], [b_lam])
            TS(bg[:, 2:3], bg[:, 1:2], -1.0, None, MUL, None, [b_lp], [b_lp])
            TS(grow[:, 2, :], grow[:, 2, :], 0.25, None, MUL, None, [b_grow[2]], [b_grow[2]], eng="pool")
            lam_init = 0.8 - 0.6 * math.exp(-0.3 * l)
            TT(lamt[:, 0:64], lamt[:, 0:64], lamt[:, 64:128], MUL, [b_lam], [b_lam])
            TT(lamt[:, 128:192], lamt[:, 128:192], lamt[:, 192:256], MUL, [b_lam], [b_lam])
            P.add("dve", lambda e: e.reduce_sum(lams[:, 0:1], lamt[:, 0:64], axis=AX.X), [b_lam], [b_lam])
            P.add("dve", lambda e: e.reduce_sum(lams[:, 1:2], lamt[:, 128:192], axis=AX.X), [b_lam], [b_lam])
            ACT(lams[:, 2:4], lams[:, 0:2], AF.Exp, [b_lam], [b_lam])
            TT(lams[:, 4:5], lams[:, 2:3], lams[:, 3:4], SUB, [b_lam], [b_lam])
            TS(lams[:, 4:5], lams[:, 4:5], lam_init, None, ADD, None, [b_lam], [b_lam])
            TS(lams[:, 5:6], lams[:, 4:5], -1.0, None, MUL, None, [b_lam], [b_lam])
            return lam_init

        def gates_phase(l):
            W, bW = load_weights(wg_d[l], 8)
            allm = M_BUFS
            for g, (p0, w, ts) in enumerate(GR):
                bi, bf_ = (0, 1) if g % 2 == 0 else (2, 3)
                hr = [b_hT[t] for t in ts]
                for kc in range(8):
                    MM(ps[bi][0:4, 0:w], W[:, kc, 0:4], hT[:, kc, p0:p0 + w], kc == 0, kc == 7, hr + [bW], [pb[bi]])
                for kc in range(8):
                    MM(ps[bf_][0:4, 0:w], W[:, kc, 4:8], hT[:, kc, p0:p0 + w], kc == 0, kc == 7, hr + [bW], [pb[bf_]])
                TS(T1[:, p0:p0 + w], ps[bi][0:4, 0:w], bg[:, 0:1], None, ADD, None, [pb[bi], b_lp], allm)
                ACT(T2[:, p0:p0 + w], ps[bf_][0:4, 0:w], AF.Exp, [pb[bf_], b_lp], allm, scale=-1.0, bias=bg[:, 2:3])
            ACT(T2, T2, AF.Ln, allm, allm, bias=1.0)
            P.add("dve", lambda e: e.tensor_tensor_scan(T3, T2, T2, 0.0, ADD, MAX), allm, allm)
            TT(T1, T1, T3, ADD, allm, allm)
            P.add("dve", lambda e: e.tensor_tensor_scan(T2, T1, T1, 0.0, MAX, MAX), allm, allm)
            ge = T2[:, 15:L:128]
            TS(decs[:, 0:1], T2[:, 15:16], -1.0, None, MUL, None, allm, [b_decs])
            TT(decs[:, 1:NT], T2[:, 15:L - 128:128], T2[:, 143:L:128], SUB, allm, [b_decs])
            ACT(decs[:, 0:NT], decs[:, 0:NT], AF.Exp, [b_decs], [b_decs])
            TS(decs[:, NT:2 * NT], decs[:, 0:NT], 1.0 / 16, None, MUL, None, [b_decs], [b_decs])
            gl = T2[:, 143:L:128].unsqueeze(2).to_broadcast([4, 16, 128])
            for Tx in (T1, T3):
                TT(Tx[:, 16:L].rearrange("p (t j) -> p t j", j=128), Tx[:, 16:L].rearrange("p (t j) -> p t j", j=128),
                   gl, SUB, allm, allm)
                TS(Tx[:, 0:16], Tx[:, 0:16], T2[:, 15:16], None, SUB, None, allm, allm)
                ACT(Tx, Tx, AF.Exp, allm, allm)
            for h in range(4):
                MM(ps[4][:, h * 2 * NT:(h + 1) * 2 * NT], sel[:, h * 128:(h + 1) * 128], decs[:, :], True, True,
                   [b_const, b_decs], [pb[4]])
            CP(decb[:].rearrange("p h c -> p (h c)"), ps[4][:, 0:8 * NT], [pb[4]], [b_decb])
            for t in range(NT):
                p0, n = TP[t]
                TR(ps[5][:n, t * 8:t * 8 + 4], T1[:, p0:p0 + n], identf[0:4, 0:4], allm + [b_const], [pb[5]])
                TR(ps[5][:n, t * 8 + 4:t * 8 + 8], T3[:, p0:p0 + n], identf[0:4, 0:4], allm + [b_const], [pb[5]])
            CP(gatesT[0:16, 0, 0:8], ps[5][0:16, 0:8], [pb[5]], [b_gT])
            CP(gatesT[:, 1:NT, 0:8], ps[5][:, 8:8 * NT].rearrange("p (t c) -> p t c", c=8), [pb[5]], [b_gT])
            TS(gatesT[0:16, 0, 8:12], gatesT[0:16, 0, 0:4], 1.0 / 16, None, MUL, None, [b_gT], [b_gT])
            TS(gatesT[:, 1:NT, 8:12], gatesT[:, 1:NT, 0:4], 1.0 / 16, None, MUL, None, [b_gT], [b_gT])

        def mlstm_prep_gen(l, h, W, bW, banks):
            hp = h % 2
            MS(raw[:, 0:3], 0.0, [b_raw])
            bi = 0
            for c in range(4):
                dst, bdst = (mqs[hp], b_mqs[hp]) if c < 2 else (mks[hp], b_mks[hp])
                cc = (0 if c < 2 else 8) + h * 2 + (c % 2)
                for g, (p0, w, ts) in enumerate(GR):
                    bk = banks[bi % len(banks)]
                    bi += 1
                    hr_ = [b_hT[t] for t in ts]
                    for kc in range(8):
                        MM(ps[bk][:, 0:w], W[:, kc, c * 128:(c + 1) * 128], hT[:, kc, p0:p0 + w], kc == 0, kc == 7,
                           hr_ + [bW], [pb[bk]])
                    CP(raw[:, 3 + p0:3 + p0 + w], ps[bk][:, 0:w], [pb[bk]], [b_raw], eng="act")
                    yield
                TS(acc[:, 0:L], raw[:, 3:3 + L], convw[:, cc, 3:4], None, MUL, None, [b_raw, b_lp], [b_acc])
                yield
                for j in (2, 1, 0):
                    STT(acc[:, 0:L], raw[:, j:j + L], convw[:, cc, j:j + 1], acc[:, 0:L], MUL, ADD,
                        [b_raw, b_acc, b_lp], [b_acc])
                    yield
                ACT(dst[:, c % 2, :], acc[:, 0:L], AF.Silu, [b_acc, b_lp], bdst, bias=convb[:, cc:cc + 1])
                yield

        def mlstm_head(l, h, W, bW, hook=None):
            hp = h % 2
            mq, mk, b_mq, b_mk = mqs[hp], mks[hp], b_mqs[hp], b_mks[hp]
            MS(CTs[0][:], 0.0, [b_CTs[0]])

            def A_pe(t):
                p0, n = TP[t]
                g = TG[t]
                for kc in range(8):
                    MM(ps[0][:n, 0:256], hT[:, kc, p0:p0 + n], W[:, kc, 512:768], kc == 0, kc == 7, [b_hT[t], bW], [pb[0]])
                for kc in range(8):
                    MM(ps[1][:n, 0:512], hT[:, kc, p0:p0 + n], W[:, kc, 768:1280], kc == 0, kc == 7, [b_hT[t], bW], [pb[1]])
                for c in range(2):
                    TR(psb[2][:n, c * 128:(c + 1) * 128], mk[:, c, p0:p0 + n], identb[:, :], [b_mk[g], b_const], [pb[2]])
                for c in range(2):
                    MM(ps[2][:n, 128:128 + n], mk[:, c, p0:p0 + n], mq[:, c, p0:p0 + n], c == 0, c == 1, [b_mk[g], b_mq[g]], [pb[2]])

            def A_other(t):
                p0, n = TP[t]
                i = t % 3
                CP(vext[i][:n, 0:256], ps[0][:n, 0:256], [pb[0]], [b_vext[i]], eng="act")
                ACT(th[i][:n, :], ps[1][:n, :], AF.Tanh, [pb[1]], [b_th[i]], scale=0.5)
                TS(kw[i][:n, :], psb[2][:n, 0:256], gatesT[:n, t, h:h + 1], None, MUL, None, [pb[2], b_gT], [b_kw[i]])
                STT(SwT[i][:n, :n], ps[2][:n, 128:128 + n], gatesT[:n, t, 8 + h:9 + h], maskb[:n, :n], MUL, MUL,
                    [pb[2], b_gT, b_const], [b_SwT[i]])
                STT(zs[i][:n, 0:256], th[i][:n, 256:512], 1.0, ps[1][:n, 256:512], ADD, MUL, [pb[1], b_th[i]], [b_zs[i]])
                TT(zs[i][:n, 0:256], zs[i][:n, 0:256], grow[:n, 2, h * 256:(h + 1) * 256], MUL, [b_zs[i], b_grow[2]], [b_zs[i]])

            def CTDB(t):
                CTo, bCTo = CTs[t % 2], b_CTs[t % 2]
                TS(CTdb[:], CTo[:], decb[:, h, NT + t:NT + t + 1], None, MUL, None, [bCTo, b_decb], [b_CTdb])

            def B_pe(t):
                p0, n = TP[t]
                g = TG[t]
                ia = t % 3
                for c in range(2):
                    MM(ps[5 + c][:, 0:257], kw[ia][:n, c * 128:(c + 1) * 128], vext[ia][:n, 0:257], True, True,
                       [b_kw[ia], b_vext[ia]], [pb[5 + c]])
                MM(ps[4][:n, 0:257], SwT[ia][:n, :n], vext[ia][:n, 0:257], True, False, [b_SwT[ia], b_vext[ia]], [pb[4]])
                for c in range(2):
                    MM(ps[4][:n, 0:257], mq[:, c, p0:p0 + n], CTdb[:, c, :], False, c == 1, [b_mq[g], b_CTdb], [pb[4]])

            def B_rest(t):
                p0, n = TP[t]
                i = t % 2
                CTo, bCTo = CTs[t % 2], b_CTs[t % 2]
                CTn, bCTn = CTs[(t + 1) % 2], b_CTs[(t + 1) % 2]
                for c in range(2):
                    STT(CTn[:, c, :], CTo[:, c, :], decb[:, h, t:t + 1], ps[5 + c][:, 0:257], MUL, ADD,
                        [pb[5 + c], bCTo, b_decb], [bCTn])
                TT(sm[i][:n, 0:1], ps[4][:n, 256:257], gatesT[:n, t, 4 + h:5 + h], MAX, [pb[4], b_gT], [b_sm[i]])
                STT(sm[i][:n, 0:1], ps[4][:n, 256:257], -1.0, sm[i][:n, 0:1], MUL, MAX, [pb[4], b_sm[i]], [b_sm[i]])
                P.add("dve", lambda e, o_=sm[i][:n, 1:2], i_=sm[i][:n, 0:1]: e.reciprocal(o_, i_), [b_sm[i]], [b_sm[i]])
                ACT(hrw[i][:n, :], ps[4][:n, 0:256], AF.Copy, [pb[4], b_sm[i]], [b_hr[i]], scale=sm[i][:n, 1:2])
                if t + 1 < NT:
                    CTDB(t + 1)

            def B2(t):
                p0, n = TP[t]
                i = t % 2
                ia = t % 3
                STT(hg[i][:n, 0:256], th[ia][:n, 0:256], 1.0, hrw[i][:n, :], ADD, MUL, [b_th[ia], b_hr[i]], [b_hg[i]])
                ACT(junkh[i][:n, :], hg[i][:n, 0:256], AF.Square, [b_hg[i]], [b_junkh[i], b_ssq[t]], accum_out=ssq[:n, t, h:h + 1])
                TT(hout[i][:n, 0:256], hg[i][:n, 0:256], zs[ia][:n, 0:256], MUL, [b_hg[i], b_zs[ia]], [b_hout[i]])
                DMA("sp", hcat_d[p0:p0 + n, h * 256:(h + 1) * 256], hout[i][:n, 0:256], [b_hout[i]], [b_hcat[t]])

            CTDB(0)
            A_pe(0)
            A_other(0)
            A_pe(1)
            A_other(1)
            for t in range(2, NT):
                B_pe(t - 2)
                A_pe(t)
                B_rest(t - 2)
                A_other(t)
                B2(t - 2)
                if hook is not None:
                    hook(t)
            for t in (NT - 2, NT - 1):
                B_pe(t)
                B_rest(t)
                B2(t)

        junkh = [T("junkh0", [128, 256], BF16), T("junkh1", [128, 256], BF16)]
        b_junkh = [Buf(), Buf()]
        hrw = [T("hr0", [128, 256], F32), T("hr1", [128, 256], F32)]
        b_hr = [Buf(), Buf()]

        def diff_head(l, h, W, bW, lam_init):
            scale = 64 ** -0.5
            bankrot = [0]
            for c, (dst, bdst) in enumerate(((dq, b_dq), (dk, b_dk))):
                for g, (p0, w, ts) in enumerate(GR):
                    bk = bankrot[0]
                    bankrot[0] ^= 1
                    hr = [b_hT[t] for t in ts]
                    for kc in range(8):
                        MM(ps[bk][:, 0:w], W[:, kc, c * 128:(c + 1) * 128], hT[:, kc, p0:p0 + w], kc == 0, kc == 7,
                           hr + [bW], [pb[bk]])
                    CP(dst[:, p0:p0 + w], ps[bk][:, 0:w], [pb[bk]], [bdst[g]], eng=("act" if g % 2 else "dve"))
            for t in range(NT):
                p0, n = TP[t]
                bk = t % 2
                for kc in range(8):
                    MM(ps[bk][:n, 0:128], hT[:, kc, p0:p0 + n], W[:, kc, 256:384], kc == 0, kc == 7, [b_hT[t], bW], [pb[bk]])
                CP(dv[:n, t, 0:128], ps[bk][:n, 0:128], [pb[bk]], [b_dv[t]], eng=("act" if t % 2 else "dve"))
            QG = [[0]] + [list(range(a_, min(a_ + 3, NT))) for a_ in range(1, NT, 3)]
            items = []
            for gq, Q in enumerate(QG):
                items.append(("z", gq, None, False))
                for j in range(0, Q[-1] + 1):
                    items.append(("a", gq, j, j == Q[-1]))
            SBP = [(0, 1), (2, 3)]
            cfac = 0.5 * (1.0 - lam_init)

            def S(k):
                ty, gq, j, last = items[k]
                Q = QG[gq]
                nq = TP[Q[0]][1]
                sb = SBP[k % 2]
                if ty == "z":
                    for slot, t in enumerate(Q):
                        p0, n = TP[t]
                        for kc in range(8):
                            MM(ps[sb[0]][:n, slot * 128:slot * 128 + 128], hT[:, kc, p0:p0 + n], W[:, kc, 384:512], kc == 0, kc == 7,
                               [b_hT[t], bW], [pb[sb[0]]])
                    return
                qs = [t for t in Q if t >= j]
                ps0 = TP[qs[0]][0]
                wq = sum(TP[t][1] for t in qs)
                kp0, kn = TP[j]
                for c in range(2):
                    cs = slice(64 * c, 64 * c + 64)
                    MM(ps[sb[c]][:kn, 0:wq], dk[cs, kp0:kp0 + kn], dq[cs, ps0:ps0 + wq], True, True,
                       [b_dk[TG[j]]] + [b_dq[TG[t]] for t in qs], [pb[sb[c]]])

            def E(k):
                ty, gq, j, last = items[k]
                Q = QG[gq]
                nq = TP[Q[0]][1]
                gp = gq % 2
                sb = SBP[k % 2]
                if ty == "z":
                    wz = len(Q) * 128
                    ACT(th[gp][:nq, 0:wz], ps[sb[0]][:nq, 0:wz], AF.Tanh, [pb[sb[0]]], [b_th[gp]], scale=0.5)
                    TS(th[gp][:nq, 0:wz], th[gp][:nq, 0:wz], cfac, cfac, MUL, ADD, [b_th[gp]], [b_th[gp]])
                    TT(zs[gp][:nq, 0:wz], ps[sb[0]][:nq, 0:wz], th[gp][:nq, 0:wz], MUL, [pb[sb[0]], b_th[gp]], [b_zs[gp]])
                    zv = zs[gp][:nq, 0:wz].rearrange("p (s d) -> p s d", d=128)
                    TT(zv, zv, grow[:nq, 3, h * 128:(h + 1) * 128].unsqueeze(1).to_broadcast([nq, len(Q), 128]), MUL,
                       [b_zs[gp], b_grow[3]], [b_zs[gp]])
                    return
                qs = [t for t in Q if t >= j]
                wq = sum(TP[t][1] for t in qs)
                kp0, kn = TP[j]
                for c in range(2):
                    pi = (k % 3) * 2 + c
                    ACT(PT[pi][:kn, 0:wq], ps[sb[c]][:kn, 0:wq], AF.Exp, [pb[sb[c]]], [b_PT[pi]], scale=scale)
                    if j >= Q[0]:
                        TT(PT[pi][:kn, 0:kn], PT[pi][:kn, 0:kn], maskb[:kn, :kn], MUL, [b_PT[pi], b_const], [b_PT[pi]])

            def V(k):
                ty, gq, j, last = items[k]
                if ty == "z":
                    return
                Q = QG[gq]
                nq = TP[Q[0]][1]
                nS = len(Q)
                gp = gq % 2
                pa = [4, 5] if gp == 0 else [6, 7]
                qs = [t for t in Q if t >= j]
                kp0, kn = TP[j]
                for bi_, t in enumerate(qs):
                    slot = t - Q[0]
                    n = TP[t][1]
                    for c in range(2):
                        pi = (k % 3) * 2 + c
                        P.add("pe", lambda e, o_=ps[pa[c]][:n, slot * 129:slot * 129 + 129], l_=PT[pi][:kn, bi_ * 128:bi_ * 128 + n],
                              r_=dv[:kn, j, 0:129], st_=(j == 0 and slot == 0), sp_=(j == t):
                              e.matmul(o_, l_, r_, start=st_, stop=sp_, skip_group_check=True),
                              [b_PT[pi], b_dv[j]], [pb[pa[c]]])
                if not last:
                    return
                A0 = ps[pa[0]][:nq, 0:nS * 129].rearrange("p (s d) -> p s d", d=129)
                A1 = ps[pa[1]][:nq, 0:nS * 129].rearrange("p (s d) -> p s d", d=129)
                r0 = sm[gp][:nq, 0:nS]
                r1 = sm[gp][:nq, 4:4 + nS]
                P.add("dve", lambda e: e.reciprocal(r0.unsqueeze(2), A0[:, :, 128:129]), [pb[pa[0]]], [b_sm[gp]])
                P.add("dve", lambda e: e.reciprocal(r1.unsqueeze(2), A1[:, :, 128:129]), [pb[pa[1]]], [b_sm[gp]])
                TS(r1, r1, lams[:nq, 5:6], None, MUL, None, [b_sm[gp], b_lam], [b_sm[gp]])
                a0v = a0[gp][:nq, 0:nS * 128].rearrange("p (s d) -> p s d", d=128)
                hgv = hg[gp][:nq, 0:nS * 128].rearrange("p (s d) -> p s d", d=128)
                TT(a0v, A0[:, :, 0:128], r0.unsqueeze(2).to_broadcast([nq, nS, 128]), MUL, [pb[pa[0]], b_sm[gp]], [b_a0[gp]])
                TT(hgv, A1[:, :, 0:128], r1.unsqueeze(2).to_broadcast([nq, nS, 128]), MUL, [pb[pa[1]], b_sm[gp]], [b_hg[gp]])
                TT(hgv, hgv, a0v, ADD, [b_hg[gp], b_a0[gp]], [b_hg[gp]])
                for slot, t in enumerate(Q):
                    ACT(junkh[gp][:nq, 0:128], hg[gp][:nq, slot * 128:(slot + 1) * 128], AF.Square, [b_hg[gp]],
                        [b_junkh[gp], b_ssq[t]], accum_out=ssq[:nq, t, 4 + h:5 + h])
                TT(hout[gp][:nq, 0:nS * 128], hg[gp][:nq, 0:nS * 128], zs[gp][:nq, 0:nS * 128], MUL, [b_hg[gp], b_zs[gp]], [b_hout[gp]])
                p0 = TP[Q[0]][0]
                cols = slice(1024 + h * 128, 1024 + (h + 1) * 128)
                if nS == 1:
                    DMA("sp", hcat_d[p0:p0 + nq, cols], hout[gp][:nq, 0:128], [b_hout[gp]], [b_hcat[t] for t in Q])
                else:
                    DMA("sp", hcat_d[p0:p0 + nS * 128, cols].rearrange("(s p) d -> p s d", p=128),
                        hout[gp][:nq, 0:nS * 128].rearrange("p (s d) -> p s d", d=128), [b_hout[gp]], [b_hcat[t] for t in Q])

            NI = len(items)
            for k in range(NI):
                S(k)
                E(k)
                if k >= 1:
                    V(k - 1)
            V(NI - 1)

        def out_phase(l, s, last):
            FENCE(M_BUFS + O_BUFS)
            WA, bWA = load_weights(wo_d[l, 0:1024, :], 1024)
            WB, bWB = load_weights(wo_d[l, 1024:2048, :], 1024)
            if not last:
                load_gpre(l + 1)
            def O1a(t):
                p0, n = TP[t]
                i = t % 2
                DMA("sp", hc[i][:n, :], hcat_d[p0:p0 + n, :], [b_hcat[t]], [b_hc[i]])
                ACT(rsd[:n, 0:4], ssq[:n, t, 0:4], AF.Ln, [b_ssq[t]], [b_rsd], scale=1.0 / 1024, bias=EPS)
                ACT(rsd[:n, 4:12], ssq[:n, t, 4:12], AF.Ln, [b_ssq[t]], [b_rsd], scale=1.0 / 128, bias=EPS)
                ACT(rsd[:n, 0:12], rsd[:n, 0:12], AF.Exp, [b_rsd], [b_rsd], scale=-0.5)
                TT(hc[i][:n, 0:1024].rearrange("p (h d) -> p h d", d=256), hc[i][:n, 0:1024].rearrange("p (h d) -> p h d", d=256),
                   rsd[:n, 0:4].unsqueeze(2).to_broadcast([n, 4, 256]), MUL, [b_hc[i], b_rsd], [b_hc[i]])
                TT(hc[i][:n, 1024:2048].rearrange("p (h d) -> p h d", d=128), hc[i][:n, 1024:2048].rearrange("p (h d) -> p h d", d=128),
                   rsd[:n, 4:12].unsqueeze(2).to_broadcast([n, 8, 128]), MUL, [b_hc[i], b_rsd], [b_hc[i]])
                for half in range(2):
                    for f in range(8):
                        fc = half * 8 + f
                        TR(psb[half][:, f * 128:f * 128 + n], hc[i][:n, fc * 128:(fc + 1) * 128], identb[:n, :n],
                           [b_hc[i], b_const], [pb[half]])
                    src = psb[half][:, 0:1024].rearrange("p (k j) -> p k j", j=128)[:, :, 0:n]
                    CP(hcTs[i][:, half * 8:(half + 1) * 8, 0:n], src, [pb[half]], [b_hcTs[i]], eng=("act" if half else "dve"))

            def O1b(t):
                p0, n = TP[t]
                i = t % 2
                if l == 0:
                    src = meta_d if t == 0 else x_d[s, p0 - 16:p0 - 16 + n, :]
                    DMA("sp", xo[i][:n, :], src, [], [b_xo[i]])
                else:
                    DMA("sp", xo[i][:n, :], xres_d[p0:p0 + n, :], [b_xres[t]], [b_xo[i]])
                ya = [2, 3] if i == 0 else [4, 5]
                for half in range(2):
                    for fc in range(16):
                        Wx, bWx = (WA, bWA) if fc < 8 else (WB, bWB)
                        MM(ps[ya[half]][:n, :], hcTs[i][:, fc, 0:n], Wx[:, fc % 8, half * 512:(half + 1) * 512], fc == 0, fc == 15,
                           [b_hcTs[i], bWx], [pb[ya[half]]])

            def O2(t):
                p0, n = TP[t]
                i = t % 2
                ya = [2, 3] if i == 0 else [4, 5]
                for half in range(2):
                    ACT(junkb[:n, 0:512], ps[ya[half]][:n, :], AF.Square, [pb[ya[half]]], [b_junk, b_ss],
                        accum_out=ss[:n, 4 + half:5 + half])
                TT(ss[:n, 6:7], ss[:n, 4:5], ss[:n, 5:6], ADD, [b_ss], [b_ss])
                ACT(ss[:n, 6:7], ss[:n, 6:7], AF.Ln, [b_ss], [b_ss], scale=1.0 / D_MODEL, bias=EPS)
                ACT(ss[:n, 7:8], ss[:n, 6:7], AF.Exp, [b_ss], [b_ss], scale=-0.5)
                for half in range(2):
                    hs = slice(half * 512, (half + 1) * 512)
                    STT(xn[i][:n, hs], ps[ya[half]][:n, :], ss[:n, 7:8], grow[:n, 1, hs], MUL, MUL,
                        [pb[ya[half]], b_ss, b_grow[1]], [b_xn[i]])
                    TT(xn[i][:n, hs], xn[i][:n, hs], xo[i][:n, hs], ADD, [b_xn[i], b_xo[i]], [b_xn[i]])
                if last:
                    if t > 0:
                        DMA("sp", out_d[s, p0 - 16:p0 - 16 + n, :], xn[i][:n, :], [b_xn[i]], [b_out])
                else:
                    DMA("sp", xres_d[p0:p0 + n, :], xn[i][:n, :], [b_xn[i]], [b_xres[t]])
                    norm_tile(t, xn[i][:n, :], b_xn[i], 6 + i)

            O1a(0)
            for t in range(NT):
                if t + 1 < NT:
                    O1a(t + 1)
                O1b(t)
                if t >= 1:
                    O2(t - 1)
            O2(NT - 1)
            FENCE(M_BUFS + O_BUFS)

        for s in range(n_seq):
            load_gpre(0)
            FENCE(M_BUFS + O_BUFS)
            for t in range(NT):
                p0, n = TP[t]
                i = t % 2
                src = meta_d if t == 0 else x_d[s, p0 - 16:p0 - 16 + n, :]
                DMA("sp", xo[i][:n, :], src, [], [b_xo[i]])
                norm_tile(t, xo[i][:n, :], b_xo[i], 6 + i)
            FENCE(M_BUFS + O_BUFS)
            for l in range(n_layers):
                lam_init = load_layer_params(l)
                gates_phase(l)
                nxt = load_weights(wm_d[l, 0], 1280)
                for _ in mlstm_prep_gen(l, 0, nxt[0], nxt[1], [0, 1, 3, 7]):
                    pass
                for h in range(4):
                    W, bW = nxt
                    nxt = load_weights(wm_d[l, h + 1], 1280) if h < 3 else load_weights(wd_d[l, 0], 512)
                    hook = None
                    gen = None
                    if h < 3:
                        gen = mlstm_prep_gen(l, h + 1, nxt[0], nxt[1], [3, 7])

                        def hook(t, gen=gen):
                            if t >= 7:
                                for _ in range(4):
                                    next(gen, None)
                    mlstm_head(l, h, W, bW, hook)
                    if gen is not None:
                        for _ in gen:
                            pass
                for h in range(8):
                    W, bW = nxt
                    if h < 7:
                        nxt = load_weights(wd_d[l, h + 1], 512)
                    diff_head(l, h, W, bW, lam_init)
                out_phase(l, s, l == n_layers - 1)
        if os.environ.get('MK_SBUF'):
            print('SBUF remaining', nc.sbuf_bytes_remaining)
        P.emit(st)
    return nc


_CACHE = {}


def _host_layout(inp):
    bf = ml_dtypes.bfloat16
    w_in = np.asarray(inp["w_in"], dtype=np.float32)
    D = 1024
    sec = lambda k: w_in[:, :, k * D:(k + 1) * D] if k < 5 else None
    qm, km, vm, om, zm = (w_in[:, :, k * D:(k + 1) * D] for k in range(5))
    wg = np.ascontiguousarray(w_in[:, :, 5 * D:5 * D + 8])
    off = 5 * D + 8
    qd, kd, vd, zd = (w_in[:, :, off + k * D:off + (k + 1) * D] for k in range(4))
    wm = np.empty((DEPTH, 4, D, 1280), np.float32)
    for h in range(4):
        hs = slice(h * 256, (h + 1) * 256)
        wm[:, h] = np.concatenate([qm[:, :, hs], km[:, :, hs], vm[:, :, hs], om[:, :, hs], zm[:, :, hs]], axis=-1)
    wd = np.empty((DEPTH, 8, D, 512), np.float32)
    for h in range(8):
        hs = slice(h * 128, (h + 1) * 128)
        wd[:, h] = np.concatenate([qd[:, :, hs], kd[:, :, hs], vd[:, :, hs], zd[:, :, hs]], axis=-1)
    grow = np.stack([inp["pre_norm_g"], inp["post_norm_g"], inp["mlstm_norm_g"], inp["diff_norm_g"]], axis=1).astype(np.float32)
    convw = np.ascontiguousarray(np.asarray(inp["conv_w"], np.float32).transpose(0, 2, 1).reshape(DEPTH, 16, 128, 4).transpose(0, 2, 1, 3))
    convb = np.ascontiguousarray(np.asarray(inp["conv_b"], np.float32).reshape(DEPTH, 16, 128).transpose(0, 2, 1))
    bgt = np.ascontiguousarray(np.asarray(inp["b_gates"], np.float32).reshape(DEPTH, 2, 4).transpose(0, 2, 1))
    lam = np.concatenate([inp["lambda_q1"], inp["lambda_k1"], inp["lambda_q2"], inp["lambda_k2"]], axis=-1).astype(np.float32)[:, None, :]
    sel = np.zeros((4, 512), np.float32)
    for h in range(4):
        sel[h, h * 128:(h + 1) * 128] = 1.0
    mask = np.triu(np.ones((128, 128), np.float32)).astype(bf)
    common = {
        "meta": np.ascontiguousarray(inp["meta_tokens"], dtype=np.float32),
        "wm": wm, "wd": wd, "wg": wg, "wo": np.ascontiguousarray(inp["w_out"], dtype=np.float32),
        "grow": np.ascontiguousarray(grow), "convw": convw, "convb": convb, "bg": bgt, "lam": np.ascontiguousarray(lam),
        "identb": np.eye(128, dtype=np.float32).astype(bf), "identf": np.eye(128, dtype=np.float32),
        "mask": mask, "sel": sel,
    }
    return common


def kernel(**inputs):
    n_layers = int(os.environ.get("MK_LAYERS", DEPTH))
    key = ("nc", n_layers)
    if key not in _CACHE:
        _CACHE[key] = build_program(n_layers=n_layers, n_seq=2)
    nc = _CACHE[key]
    common = _host_layout(inputs)
    x = np.asarray(inputs["x"], dtype=np.float32)
    in_maps = []
    for c in range(8):
        m = dict(common)
        m["x"] = np.ascontiguousarray(x[2 * c:2 * c + 2])
        in_maps.append(m)
    res = run_bass_kernel_spmd(nc, in_maps, core_ids=list(range(8)))
    out = np.concatenate([r["out"] for r in res.results], axis=0)
    return out.astype(np.float32)
```

```python
import contextlib
import math
import os
import numpy as np
import ml_dtypes
import concourse.bass as bass
import concourse.mybir as mybir
from concourse.bass_utils import run_bass_kernel_spmd

F32 = mybir.dt.float32
BF16 = mybir.dt.bfloat16
AF = mybir.ActivationFunctionType
ALU = mybir.AluOpType
AX = mybir.AxisListType

ENGS = ["pe", "act", "dve", "pool", "sp"]


class Buf:
    __slots__ = ("n", "w", "r")

    def __init__(self, n=""):
        self.n = n
        self.w = None
        self.r = []


class Op:
    __slots__ = ("eng", "fn", "deps", "sig", "tok", "dma")


class Prog:
    NDMA = 8
    ROLL = 30000

    def __init__(self, nc):
        self.nc = nc
        self.ops = {e: [] for e in ENGS}
        self.dq = {e: {"next": 0, "last": [None] * self.NDMA, "cnt": [0] * self.NDMA} for e in ENGS}
        self.all_dma = []

    def add(self, eng, fn, reads=(), writes=(), dma=False):
        o = Op()
        o.eng, o.fn, o.dma, o.sig, o.tok = eng, fn, dma, False, None
        cand = []
        for b in reads:
            if b.w is not None:
                cand.append((b.w, True))
        for b in writes:
            if b.w is not None:
                cand.append((b.w, False))
            for r in b.r:
                cand.append((r, False))
        deps, seen = [], set()
        if dma:
            q = self.dq[eng]
            s = q["next"]
            q["next"] = (s + 1) % self.NDMA
            if q["last"][s] is not None:
                cand.append((q["last"][s], True))
            q["cnt"][s] += 16
            o.tok = (("d", eng, s), q["cnt"][s])
            q["last"][s] = o
            self.all_dma.append(o)
        for p, raw in cand:
            if p is o or id(p) in seen:
                continue
            if (not dma) and (not p.dma) and p.eng == eng:
                if eng == "pe":
                    continue
            seen.add(id(p))
            deps.append(p)
            if not p.dma:
                p.sig = True
        o.deps = deps
        for b in reads:
            if dma:
                b.r.append(o)
            else:
                b.r = [x for x in b.r if x.dma or x.eng != eng] + [o]
        for b in writes:
            b.w = o
            b.r = []
        self.ops[eng].append(o)
        return o

    def emit(self, stack):
        nc = self.nc
        keys = []
        for e in ENGS:
            cnt, gen, used = 0, 0, False
            for o in self.ops[e]:
                if o.dma or not o.sig:
                    continue
                cnt += 1
                if cnt > self.ROLL:
                    gen += 1
                    cnt = 1
                o.tok = (("c", e, gen), cnt)
                used = True
            if used:
                for g in range(gen + 1):
                    keys.append(("c", e, g))
            for s in range(self.NDMA):
                if self.dq[e]["cnt"][s]:
                    keys.append(("d", e, s))
        sems = {k: stack.enter_context(nc.semaphore("s_%s_%s_%d" % k)) for k in keys}
        final = {}
        for o in self.all_dma:
            final[o.tok[0]] = max(final.get(o.tok[0], 0), o.tok[1])

        def mk(e):
            def body(engine):
                waited = {}
                for o in self.ops[e]:
                    for p in o.deps:
                        k, v = p.tok
                        if waited.get(k, 0) < v:
                            engine.wait_ge(sems[k], v)
                            waited[k] = v
                    ins = o.fn(engine)
                    if o.dma:
                        ins.then_inc(sems[o.tok[0]], 16)
                    elif o.sig:
                        ins.then_inc(sems[o.tok[0]], 1)
                if e == "sp":
                    for k, v in final.items():
                        if waited.get(k, 0) < v:
                            engine.wait_ge(sems[k], v)
            return body

        with nc.Block() as block:
            block.tensor(mk("pe"))
            block.scalar(mk("act"))
            block.vector(mk("dve"))
            block.gpsimd(mk("pool"))
            block.sync(mk("sp"))


D_MODEL = 1024
SEQ = 2048
N_META = 16
L = SEQ + N_META
NT = 17
DEPTH = 4
EPS = 1e-6
TP = [(0, 16)] + [(16 + 128 * (t - 1), 128) for t in range(1, NT)]
GR = [(0, 400, [0, 1, 2, 3]), (400, 512, [4, 5, 6, 7]), (912, 512, [8, 9, 10, 11]),
      (1424, 512, [12, 13, 14, 15]), (1936, 128, [16])]
TG = {}
for _g, (_p, _w, _ts) in enumerate(GR):
    for _t in _ts:
        TG[_t] = _g


def build_program(n_layers=DEPTH, n_seq=2, dbg=False):
    nc = bass.Bass("TRN2", target_bir_lowering=False)

    def din(name, shape, dt=F32):
        return nc.dram_tensor(name, list(shape), dt, kind="ExternalInput").ap()

    x_d = din("x", [n_seq, SEQ, D_MODEL])
    meta_d = din("meta", [N_META, D_MODEL])
    wm_d = din("wm", [DEPTH, 4, D_MODEL, 1280])
    wd_d = din("wd", [DEPTH, 8, D_MODEL, 512])
    wg_d = din("wg", [DEPTH, D_MODEL, 8])
    wo_d = din("wo", [DEPTH, 2048, D_MODEL])
    grow_d = din("grow", [DEPTH, 4, D_MODEL])
    convw_d = din("convw", [DEPTH, 128, 16, 4])
    convb_d = din("convb", [DEPTH, 128, 16])
    bg_d = din("bg", [DEPTH, 4, 2])
    lam_d = din("lam", [DEPTH, 1, 256])
    identb_d = din("identb", [128, 128], BF16)
    identf_d = din("identf", [128, 128])
    mask_d = din("mask", [128, 128], BF16)
    sel_d = din("sel", [4, 512])
    out_d = nc.dram_tensor("out", [n_seq, SEQ, D_MODEL], F32, kind="ExternalOutput").ap()
    xres_d = nc.dram_tensor("xres", [L, D_MODEL], F32).ap()
    hcat_d = nc.dram_tensor("hcat", [L, 2048], BF16).ap()

    with contextlib.ExitStack() as st:
        P = Prog(nc)

        def T(name, shape, dt):
            return st.enter_context(nc.sbuf_tensor("sb_" + name, list(shape), dt))

        hT = T("hT", [128, 8, L], BF16)
        Wb = [T("W0", [128, 8, 1280], BF16), T("W1", [128, 8, 1280], BF16)]
        raw = T("raw", [128, 3 + L + 1], F32)
        acc = T("acc", [128, L + 4], F32)
        mqs = [T("mq", [128, 2, L], BF16), T("mq1", [128, 2, L], BF16)]
        mks = [T("mk", [128, 2, L], BF16), T("mk1", [128, 2, L], BF16)]
        mq, mk = mqs[0], mks[0]
        dq = T("dq", [128, L], BF16)
        dk = T("dk", [128, L], BF16)
        dv = T("dv", [128, NT, 130], BF16)
        CTs = [T("CTa", [128, 2, 257], F32), T("CTb", [128, 2, 257], F32)]
        CTdb = T("CTdb", [128, 2, 257], BF16)
        grow = T("grow", [128, 4, D_MODEL], F32)
        identb = T("identb", [128, 128], BF16)
        identf = T("identf", [128, 128], F32)
        maskb = T("maskb", [128, 128], BF16)
        sel = T("sel", [4, 512], F32)
        convw = T("convw", [128, 16, 4], F32)
        convb = T("convb", [128, 16], F32)
        bg = T("bg", [4, 4], F32)
        lamt = T("lamt", [128, 256], F32)
        lams = T("lams", [128, 8], F32)
        gatesT = T("gatesT", [128, NT, 12], F32)
        decs = T("decs", [4, 2 * NT], F32)
        decb = T("decb", [128, 4, 2 * NT], F32)
        ssq = T("ssq", [128, NT, 12], F32)
        rsd = T("rsd", [128, 16], F32)
        ss = T("ss", [128, 8], F32)
        vext = [T("vext%d" % i_, [128, 258], BF16) for i_ in range(3)]
        th = [T("th%d" % i_, [128, 512], F32) for i_ in range(3)]
        zs = [T("zs%d" % i_, [128, 384], F32) for i_ in range(3)]
        kw = [T("kw%d" % i_, [128, 256], BF16) for i_ in range(3)]
        SwT = [T("SwT%d" % i_, [128, 128], BF16) for i_ in range(3)]
        hg = [T("hg0", [128, 384], F32), T("hg1", [128, 384], F32)]
        hout = [T("hout0", [128, 384], BF16), T("hout1", [128, 384], BF16)]
        sm = [T("sm0", [128, 8], F32), T("sm1", [128, 8], F32)]
        PT = [T("PT%d" % i_, [128, 512], BF16) for i_ in range(6)]
        a0 = [T("a00", [128, 384], F32), T("a01", [128, 384], F32)]
        dummy = T("dummy", [1, 8], F32)
        ps = [st.enter_context(nc.psum_tensor("ps%d" % i, [128, 512], F32)) for i in range(8)]
        psb = [p[:].bitcast(BF16) for p in ps]
        pb = [Buf("ps%d" % i) for i in range(8)]

        rawb = raw[:].bitcast(BF16)
        accb = acc[:].bitcast(BF16)
        hc = [rawb[:, 0:2048], rawb[:, 2048:4096]]
        hcT0 = accb[:, 0:2048].rearrange("p (f j) -> p f j", j=128)
        hcT1 = mqs[1][:].rearrange("p c l -> p (c l)")[:, 0:2048].rearrange("p (f j) -> p f j", j=128)
        hcTs = [hcT0, hcT1]
        hb = accb[:, 2048:3072]
        junkb = accb[:, 3072:4096]
        mqf = mq[:].rearrange("p c l -> p (c l)").bitcast(F32)
        mkf = mk[:].rearrange("p c l -> p (c l)").bitcast(F32)
        xo = [mqf[:, 0:1024], mqf[:, 1024:2048]]
        xn = [mkf[:, 0:1024], mkf[:, 1024:2048]]
        T1 = raw[0:4, 4:4 + L]
        T2 = acc[0:4, 0:L]
        T3 = mqf[0:4, 0:L]

        b_hT = [Buf("hT%d" % t) for t in range(NT)]
        b_W = [Buf("W0"), Buf("W1")]
        b_raw, b_acc = Buf("raw"), Buf("acc")
        b_mqs = [[Buf("mq%d_%d" % (p_, g)) for g in range(5)] for p_ in range(2)]
        b_mks = [[Buf("mk%d_%d" % (p_, g)) for g in range(5)] for p_ in range(2)]
        b_mq, b_mk = b_mqs[0], b_mks[0]
        b_dq = [Buf("dq%d" % g) for g in range(5)]
        b_dk = [Buf("dk%d" % g) for g in range(5)]
        b_dv = [Buf("dv%d" % t) for t in range(NT)]
        b_CTs = [Buf(), Buf()]
        b_CTdb = Buf()
        b_grow = [Buf("g%d" % i) for i in range(4)]
        b_const = Buf("const")
        b_lp = Buf("layerparams")
        b_lam = Buf("lam")
        b_gT, b_decs, b_decb = Buf(), Buf(), Buf()
        b_ssq = [Buf("ssq%d" % t) for t in range(NT)]
        b_rsd, b_ss = Buf(), Buf()
        b_vext, b_th, b_zs, b_kw, b_SwT, b_hg, b_hout, b_sm = ([Buf(), Buf(), Buf()] for _ in range(8))
        b_PT = [Buf() for _ in range(6)]
        b_a0 = [Buf(), Buf()]
        b_hc, b_xo, b_xn = [Buf(), Buf()], [Buf(), Buf()], [Buf(), Buf()]
        b_hcTs = [Buf(), Buf()]
        b_hb, b_junk = Buf(), Buf()
        b_dummy = Buf()
        b_xres = [Buf("xres%d" % t) for t in range(NT)]
        b_hcat = [Buf("hcat%d" % t) for t in range(NT)]
        b_out = Buf("out")

        def MM(out, lhsT, rhs, start, stop, R, W):
            P.add("pe", lambda e: e.matmul(out, lhsT, rhs, start=start, stop=stop), R, W)

        def TR(out, in_, ident, R, W):
            P.add("pe", lambda e: e.transpose(out, in_, ident), R, W)

        def ACT(out, in_, func, R, W, **kw_):
            P.add("act", lambda e: e.activation(out, in_, func, **kw_), R, W)

        def TS(out, in0, s1, s2, op0, op1, R, W, eng="dve"):
            if op1 is None:
                P.add(eng, lambda e: e.tensor_scalar(out, in0, s1, None, op0), R, W)
            else:
                P.add(eng, lambda e: e.tensor_scalar(out, in0, s1, s2, op0, op1), R, W)

        def TT(out, in0, in1, op, R, W, eng="dve"):
            P.add(eng, lambda e: e.tensor_tensor(out, in0, in1, op), R, W)

        def STT(out, in0, sc, in1, op0, op1, R, W):
            P.add("dve", lambda e: e.scalar_tensor_tensor(out, in0, sc, in1, op0, op1), R, W)

        def CP(out, in_, R, W, eng="dve"):
            if eng == "act":
                P.add("act", lambda e: e.copy(out, in_), R, W)
            else:
                P.add(eng, lambda e: e.tensor_copy(out, in_), R, W)

        def MS(ap, val, W, eng="pool"):
            P.add(eng, lambda e: e.memset(ap, val), [], W)

        def DMA(q, out, in_, R, W):
            P.add(q, lambda e: e.dma_start(out=out, in_=in_), R, W, dma=True)

        def FENCE(bufs):
            P.add("pool", lambda e: e.memset(dummy[0:1, 0:1], 0.0), [], list(bufs) + [b_dummy])

        MUL, ADD, SUB, MAX = ALU.mult, ALU.add, ALU.subtract, ALU.max

        DMA("sp", identb[:], identb_d, [], [b_const])
        DMA("sp", identf[:], identf_d, [], [b_const])
        DMA("sp", maskb[:], mask_d, [], [b_const])
        DMA("sp", sel[:], sel_d, [], [b_const])
        for i in range(3):
            MS(vext[i][:, 256:258], 1.0, [b_vext[i]])
        for t in range(NT):
            MS(dv[:, t, 128:130], 1.0, [b_dv[t]])
        MS(raw[:, 0:3], 0.0, [b_raw])

        M_BUFS = [b_raw, b_acc] + b_mqs[0] + b_mks[0] + b_mqs[1] + b_mks[1]
        O_BUFS = b_hc + b_xo + b_xn + b_hcTs + [b_hb, b_junk]

        wslot = [0]

        def load_weights(src, ncols, krows=8):
            i = wslot[0]
            wslot[0] ^= 1
            for kc in range(krows):
                DMA("pool", Wb[i][:, kc, 0:ncols], src[kc * 128:(kc + 1) * 128, :], [], [b_W[i]])
            return Wb[i], b_W[i]

        def norm_tile(t, xt, bx, bank):
            pos0, n = TP[t]
            ACT(junkb[:n, :], xt, AF.Square, [bx], [b_junk, b_ss], accum_out=ss[:n, 0:1])
            ACT(ss[:n, 1:2], ss[:n, 0:1], AF.Ln, [b_ss], [b_ss], scale=1.0 / D_MODEL, bias=EPS)
            ACT(ss[:n, 2:3], ss[:n, 1:2], AF.Exp, [b_ss], [b_ss], scale=-0.5)
            STT(hb[:n, :], xt, ss[:n, 2:3], grow[:n, 0, :], MUL, MUL, [bx, b_ss, b_grow[0]], [b_hb])
            for kc in range(8):
                TR(psb[bank][:, kc * 128:kc * 128 + n], hb[:n, kc * 128:(kc + 1) * 128], identb[:n, :n],
                   [b_hb, b_const], [pb[bank]])
            src = psb[bank][:, 0:1024].rearrange("p (k j) -> p k j", j=128)[:, :, 0:n]
            CP(hT[:, :, pos0:pos0 + n], src, [pb[bank]], [b_hT[t]], eng="act")

        def load_gpre(l):
            DMA("sp", grow[:, 0, :], grow_d[l, 0:1, :].partition_broadcast(128), [], [b_grow[0]])

        def load_layer_params(l):
            for i in range(1, 4):
                DMA("sp", grow[:, i, :], grow_d[l, i:i + 1, :].partition_broadcast(128), [], [b_grow[i]])
            DMA("sp", convw[:], convw_d[l], [], [b_lp])
            DMA("sp", convb[:], convb_d[l], [], [b_lp])
            DMA("sp", bg[:, 0:2], bg_d[l], [], [b_lp])
            DMA("sp", lamt[:], lam_d[l].partition_broadcast(128), [], [b_lam])
            TS(bg[:, 2:3], bg[:, 1:2], -1.0, None, MUL, None, [b_lp], [b_lp])
            TS(grow[:, 2, :], grow[:, 2, :], 0.25, None, MUL, None, [b_grow[2]], [b_grow[2]], eng="pool")
            lam_init = 0.8 - 0.6 * math.exp(-0.3 * l)
            TT(lamt[:, 0:64], lamt[:, 0:64], lamt[:, 64:128], MUL, [b_lam], [b_lam])
            TT(lamt[:, 128:192], lamt[:, 128:192], lamt[:, 192:256], MUL, [b_lam], [b_lam])
            P.add("dve", lambda e: e.reduce_sum(lams[:, 0:1], lamt[:, 0:64], axis=AX.X), [b_lam], [b_lam])
            P.add("dve", lambda e: e.reduce_sum(lams[:, 1:2], lamt[:, 128:192], axis=AX.X), [b_lam], [b_lam])
            ACT(lams[:, 2:4], lams[:, 0:2], AF.Exp, [b_lam], [b_lam])
            TT(lams[:, 4:5], lams[:, 2:3], lams[:, 3:4], SUB, [b_lam], [b_lam])
            TS(lams[:, 4:5], lams[:, 4:5], lam_init, None, ADD, None, [b_lam], [b_lam])
            TS(lams[:, 5:6], lams[:, 4:5], -1.0, None, MUL, None, [b_lam], [b_lam])
            return lam_init

        def gates_phase(l):
            W, bW = load_weights(wg_d[l], 8)
            allm = M_BUFS
            for g, (p0, w, ts) in enumerate(GR):
                bi, bf_ = (0, 1) if g % 2 == 0 else (2, 3)
                hr = [b_hT[t] for t in ts]
                for kc in range(8):
                    MM(ps[bi][0:4, 0:w], W[:, kc, 0:4], hT[:, kc, p0:p0 + w], kc == 0, kc == 7, hr + [bW], [pb[bi]])
                for kc in range(8):
                    MM(ps[bf_][0:4, 0:w], W[:, kc, 4:8], hT[:, kc, p0:p0 + w], kc == 0, kc == 7, hr + [bW], [pb[bf_]])
                TS(T1[:, p0:p0 + w], ps[bi][0:4, 0:w], bg[:, 0:1], None, ADD, None, [pb[bi], b_lp], allm)
                ACT(T2[:, p0:p0 + w], ps[bf_][0:4, 0:w], AF.Exp, [pb[bf_], b_lp], allm, scale=-1.0, bias=bg[:, 2:3])
            ACT(T2, T2, AF.Ln, allm, allm, bias=1.0)
            P.add("dve", lambda e: e.tensor_tensor_scan(T3, T2, T2, 0.0, ADD, MAX), allm, allm)
            TT(T1, T1, T3, ADD, allm, allm)
            P.add("dve", lambda e: e.tensor_tensor_scan(T2, T1, T1, 0.0, MAX, MAX), allm, allm)
            ge = T2[:, 15:L:128]
            TS(decs[:, 0:1], T2[:, 15:16], -1.0, None, MUL, None, allm, [b_decs])
            TT(decs[:, 1:NT], T2[:, 15:L - 128:128], T2[:, 143:L:128], SUB, allm, [b_decs])
            ACT(decs[:, 0:NT], decs[:, 0:NT], AF.Exp, [b_decs], [b_decs])
            TS(decs[:, NT:2 * NT], decs[:, 0:NT], 1.0 / 16, None, MUL, None, [b_decs], [b_decs])
            gl = T2[:, 143:L:128].unsqueeze(2).to_broadcast([4, 16, 128])
            for Tx in (T1, T3):
                TT(Tx[:, 16:L].rearrange("p (t j) -> p t j", j=128), Tx[:, 16:L].rearrange("p (t j) -> p t j", j=128),
                   gl, SUB, allm, allm)
                TS(Tx[:, 0:16], Tx[:, 0:16], T2[:, 15:16], None, SUB, None, allm, allm)
                ACT(Tx, Tx, AF.Exp, allm, allm)
            for h in range(4):
                MM(ps[4][:, h * 2 * NT:(h + 1) * 2 * NT], sel[:, h * 128:(h + 1) * 128], decs[:, :], True, True,
                   [b_const, b_decs], [pb[4]])
            CP(decb[:].rearrange("p h c -> p (h c)"), ps[4][:, 0:8 * NT], [pb[4]], [b_decb])
            for t in range(NT):
                p0, n = TP[t]
                TR(ps[5][:n, t * 8:t * 8 + 4], T1[:, p0:p0 + n], identf[0:4, 0:4], allm + [b_const], [pb[5]])
                TR(ps[5][:n, t * 8 + 4:t * 8 + 8], T3[:, p0:p0 + n], identf[0:4, 0:4], allm + [b_const], [pb[5]])
            CP(gatesT[0:16, 0, 0:8], ps[5][0:16, 0:8], [pb[5]], [b_gT])
            CP(gatesT[:, 1:NT, 0:8], ps[5][:, 8:8 * NT].rearrange("p (t c) -> p t c", c=8), [pb[5]], [b_gT])
            TS(gatesT[0:16, 0, 8:12], gatesT[0:16, 0, 0:4], 1.0 / 16, None, MUL, None, [b_gT], [b_gT])
            TS(gatesT[:, 1:NT, 8:12], gatesT[:, 1:NT, 0:4], 1.0 / 16, None, MUL, None, [b_gT], [b_gT])

        def mlstm_prep_gen(l, h, W, bW, banks):
            hp = h % 2
            MS(raw[:, 0:3], 0.0, [b_raw])
            bi = 0
            for c in range(4):
                dst, bdst = (mqs[hp], b_mqs[hp]) if c < 2 else (mks[hp], b_mks[hp])
                cc = (0 if c < 2 else 8) + h * 2 + (c % 2)
                for g, (p0, w, ts) in enumerate(GR):
                    bk = banks[bi % len(banks)]
                    bi += 1
                    hr_ = [b_hT[t] for t in ts]
                    for kc in range(8):
                        MM(ps[bk][:, 0:w], W[:, kc, c * 128:(c + 1) * 128], hT[:, kc, p0:p0 + w], kc == 0, kc == 7,
                           hr_ + [bW], [pb[bk]])
                    CP(raw[:, 3 + p0:3 + p0 + w], ps[bk][:, 0:w], [pb[bk]], [b_raw], eng="act")
                    yield
                TS(acc[:, 0:L], raw[:, 3:3 + L], convw[:, cc, 3:4], None, MUL, None, [b_raw, b_lp], [b_acc])
                yield
                for j in (2, 1, 0):
                    STT(acc[:, 0:L], raw[:, j:j + L], convw[:, cc, j:j + 1], acc[:, 0:L], MUL, ADD,
                        [b_raw, b_acc, b_lp], [b_acc])
                    yield
                ACT(dst[:, c % 2, :], acc[:, 0:L], AF.Silu, [b_acc, b_lp], bdst, bias=convb[:, cc:cc + 1])
                yield

        def mlstm_head(l, h, W, bW, hook=None):
            hp = h % 2
            mq, mk, b_mq, b_mk = mqs[hp], mks[hp], b_mqs[hp], b_mks[hp]
            MS(CTs[0][:], 0.0, [b_CTs[0]])

            def A_pe(t):
                p0, n = TP[t]
                g = TG[t]
                for kc in range(8):
                    MM(ps[0][:n, 0:256], hT[:, kc, p0:p0 + n], W[:, kc, 512:768], kc == 0, kc == 7, [b_hT[t], bW], [pb[0]])
                for kc in range(8):
                    MM(ps[1][:n, 0:512], hT[:, kc, p0:p0 + n], W[:, kc, 768:1280], kc == 0, kc == 7, [b_hT[t], bW], [pb[1]])
                for c in range(2):
                    TR(psb[2][:n, c * 128:(c + 1) * 128], mk[:, c, p0:p0 + n], identb[:, :], [b_mk[g], b_const], [pb[2]])
                for c in range(2):
                    MM(ps[2][:n, 128:128 + n], mk[:, c, p0:p0 + n], mq[:, c, p0:p0 + n], c == 0, c == 1, [b_mk[g], b_mq[g]], [pb[2]])

            def A_other(t):
                p0, n = TP[t]
                i = t % 3
                CP(vext[i][:n, 0:256], ps[0][:n, 0:256], [pb[0]], [b_vext[i]], eng="act")
                ACT(th[i][:n, :], ps[1][:n, :], AF.Tanh, [pb[1]], [b_th[i]], scale=0.5)
                TS(kw[i][:n, :], psb[2][:n, 0:256], gatesT[:n, t, h:h + 1], None, MUL, None, [pb[2], b_gT], [b_kw[i]])
                STT(SwT[i][:n, :n], ps[2][:n, 128:128 + n], gatesT[:n, t, 8 + h:9 + h], maskb[:n, :n], MUL, MUL,
                    [pb[2], b_gT, b_const], [b_SwT[i]])
                STT(zs[i][:n, 0:256], th[i][:n, 256:512], 1.0, ps[1][:n, 256:512], ADD, MUL, [pb[1], b_th[i]], [b_zs[i]])
                TT(zs[i][:n, 0:256], zs[i][:n, 0:256], grow[:n, 2, h * 256:(h + 1) * 256], MUL, [b_zs[i], b_grow[2]], [b_zs[i]])

            def CTDB(t):
                CTo, bCTo = CTs[t % 2], b_CTs[t % 2]
                TS(CTdb[:], CTo[:], decb[:, h, NT + t:NT + t + 1], None, MUL, None, [bCTo, b_decb], [b_CTdb])

            def B_pe(t):
                p0, n = TP[t]
                g = TG[t]
                ia = t % 3
                for c in range(2):
                    MM(ps[5 + c][:, 0:257], kw[ia][:n, c * 128:(c + 1) * 128], vext[ia][:n, 0:257], True, True,
                       [b_kw[ia], b_vext[ia]], [pb[5 + c]])
                MM(ps[4][:n, 0:257], SwT[ia][:n, :n], vext[ia][:n, 0:257], True, False, [b_SwT[ia], b_vext[ia]], [pb[4]])
                for c in range(2):
                    MM(ps[4][:n, 0:257], mq[:, c, p0:p0 + n], CTdb[:, c, :], False, c == 1, [b_mq[g], b_CTdb], [pb[4]])

            def B_rest(t):
                p0, n = TP[t]
                i = t % 2
                CTo, bCTo = CTs[t % 2], b_CTs[t % 2]
                CTn, bCTn = CTs[(t + 1) % 2], b_CTs[(t + 1) % 2]
                for c in range(2):
                    STT(CTn[:, c, :], CTo[:, c, :], decb[:, h, t:t + 1], ps[5 + c][:, 0:257], MUL, ADD,
                        [pb[5 + c], bCTo, b_decb], [bCTn])
                TT(sm[i][:n, 0:1], ps[4][:n, 256:257], gatesT[:n, t, 4 + h:5 + h], MAX, [pb[4], b_gT], [b_sm[i]])
                STT(sm[i][:n, 0:1], ps[4][:n, 256:257], -1.0, sm[i][:n, 0:1], MUL, MAX, [pb[4], b_sm[i]], [b_sm[i]])
                P.add("dve", lambda e, o_=sm[i][:n, 1:2], i_=sm[i][:n, 0:1]: e.reciprocal(o_, i_), [b_sm[i]], [b_sm[i]])
                ACT(hrw[i][:n, :], ps[4][:n, 0:256], AF.Copy, [pb[4], b_sm[i]], [b_hr[i]], scale=sm[i][:n, 1:2])
                if t + 1 < NT:
                    CTDB(t + 1)

            def B2(t):
                p0, n = TP[t]
                i = t % 2
                ia = t % 3
                STT(hg[i][:n, 0:256], th[ia][:n, 0:256], 1.0, hrw[i][:n, :], ADD, MUL, [b_th[ia], b_hr[i]], [b_hg[i]])
                ACT(junkh[i][:n, :], hg[i][:n, 0:256], AF.Square, [b_hg[i]], [b_junkh[i], b_ssq[t]], accum_out=ssq[:n, t, h:h + 1])
                TT(hout[i][:n, 0:256], hg[i][:n, 0:256], zs[ia][:n, 0:256], MUL, [b_hg[i], b_zs[ia]], [b_hout[i]])
                DMA("sp", hcat_d[p0:p0 + n, h * 256:(h + 1) * 256], hout[i][:n, 0:256], [b_hout[i]], [b_hcat[t]])

            CTDB(0)
            A_pe(0)
            A_other(0)
            A_pe(1)
            A_other(1)
            for t in range(2, NT):
                B_pe(t - 2)
                A_pe(t)
                B_rest(t - 2)
                A_other(t)
                B2(t - 2)
                if hook is not None:
                    hook(t)
            for t in (NT - 2, NT - 1):
                B_pe(t)
                B_rest(t)
                B2(t)

        junkh = [T("junkh0", [128, 256], BF16), T("junkh1", [128, 256], BF16)]
        b_junkh = [Buf(), Buf()]
        hrw = [T("hr0", [128, 256], F32), T("hr1", [128, 256], F32)]
        b_hr = [Buf(), Buf()]

        def diff_head(l, h, W, bW, lam_init):
            scale = 64 ** -0.5
            bankrot = [0]
            for c, (dst, bdst) in enumerate(((dq, b_dq), (dk, b_dk))):
                for g, (p0, w, ts) in enumerate(GR):
                    bk = bankrot[0]
                    bankrot[0] ^= 1
                    hr = [b_hT[t] for t in ts]
                    for kc in range(8):
                        MM(ps[bk][:, 0:w], W[:, kc, c * 128:(c + 1) * 128], hT[:, kc, p0:p0 + w], kc == 0, kc == 7,
                           hr + [bW], [pb[bk]])
                    CP(dst[:, p0:p0 + w], ps[bk][:, 0:w], [pb[bk]], [bdst[g]], eng=("act" if g % 2 else "dve"))
            for t in range(NT):
                p0, n = TP[t]
                bk = t % 2
                for kc in range(8):
                    MM(ps[bk][:n, 0:128], hT[:, kc, p0:p0 + n], W[:, kc, 256:384], kc == 0, kc == 7, [b_hT[t], bW], [pb[bk]])
                CP(dv[:n, t, 0:128], ps[bk][:n, 0:128], [pb[bk]], [b_dv[t]], eng=("act" if t % 2 else "dve"))
            QG = [[0]] + [list(range(a_, min(a_ + 3, NT))) for a_ in range(1, NT, 3)]
            items = []
            for gq, Q in enumerate(QG):
                items.append(("z", gq, None, False))
                for j in range(0, Q[-1] + 1):
                    items.append(("a", gq, j, j == Q[-1]))
            SBP = [(0, 1), (2, 3)]
            cfac = 0.5 * (1.0 - lam_init)

            def S(k):
                ty, gq, j, last = items[k]
                Q = QG[gq]
                nq = TP[Q[0]][1]
                sb = SBP[k % 2]
                if ty == "z":
                    for slot, t in enumerate(Q):
                        p0, n = TP[t]
                        for kc in range(8):
                            MM(ps[sb[0]][:n, slot * 128:slot * 128 + 128], hT[:, kc, p0:p0 + n], W[:, kc, 384:512], kc == 0, kc == 7,
                               [b_hT[t], bW], [pb[sb[0]]])
                    return
                qs = [t for t in Q if t >= j]
                ps0 = TP[qs[0]][0]
                wq = sum(TP[t][1] for t in qs)
                kp0, kn = TP[j]
                for c in range(2):
                    cs = slice(64 * c, 64 * c + 64)
                    MM(ps[sb[c]][:kn, 0:wq], dk[cs, kp0:kp0 + kn], dq[cs, ps0:ps0 + wq], True, True,
                       [b_dk[TG[j]]] + [b_dq[TG[t]] for t in qs], [pb[sb[c]]])

            def E(k):
                ty, gq, j, last = items[k]
                Q = QG[gq]
                nq = TP[Q[0]][1]
                gp = gq % 2
                sb = SBP[k % 2]
                if ty == "z":
                    wz = len(Q) * 128
                    ACT(th[gp][:nq, 0:wz], ps[sb[0]][:nq, 0:wz], AF.Tanh, [pb[sb[0]]], [b_th[gp]], scale=0.5)
                    TS(th[gp][:nq, 0:wz], th[gp][:nq, 0:wz], cfac, cfac, MUL, ADD, [b_th[gp]], [b_th[gp]])
                    TT(zs[gp][:nq, 0:wz], ps[sb[0]][:nq, 0:wz], th[gp][:nq, 0:wz], MUL, [pb[sb[0]], b_th[gp]], [b_zs[gp]])
                    zv = zs[gp][:nq, 0:wz].rearrange("p (s d) -> p s d", d=128)
                    TT(zv, zv, grow[:nq, 3, h * 128:(h + 1) * 128].unsqueeze(1).to_broadcast([nq, len(Q), 128]), MUL,
                       [b_zs[gp], b_grow[3]], [b_zs[gp]])
                    return
                qs = [t for t in Q if t >= j]
                wq = sum(TP[t][1] for t in qs)
                kp0, kn = TP[j]
                for c in range(2):
                    pi = (k % 3) * 2 + c
                    ACT(PT[pi][:kn, 0:wq], ps[sb[c]][:kn, 0:wq], AF.Exp, [pb[sb[c]]], [b_PT[pi]], scale=scale)
                    if j >= Q[0]:
                        TT(PT[pi][:kn, 0:kn], PT[pi][:kn, 0:kn], maskb[:kn, :kn], MUL, [b_PT[pi], b_const], [b_PT[pi]])

            def V(k):
                ty, gq, j, last = items[k]
                if ty == "z":
                    return
                Q = QG[gq]
                nq = TP[Q[0]][1]
                nS = len(Q)
                gp = gq % 2
                pa = [4, 5] if gp == 0 else [6, 7]
                qs = [t for t in Q if t >= j]
                kp0, kn = TP[j]
                for bi_, t in enumerate(qs):
                    slot = t - Q[0]
                    n = TP[t][1]
                    for c in range(2):
                        pi = (k % 3) * 2 + c
                        P.add("pe", lambda e, o_=ps[pa[c]][:n, slot * 129:slot * 129 + 129], l_=PT[pi][:kn, bi_ * 128:bi_ * 128 + n],
                              r_=dv[:kn, j, 0:129], st_=(j == 0 and slot == 0), sp_=(j == t):
                              e.matmul(o_, l_, r_, start=st_, stop=sp_, skip_group_check=True),
                              [b_PT[pi], b_dv[j]], [pb[pa[c]]])
                if not last:
                    return
                A0 = ps[pa[0]][:nq, 0:nS * 129].rearrange("p (s d) -> p s d", d=129)
                A1 = ps[pa[1]][:nq, 0:nS * 129].rearrange("p (s d) -> p s d", d=129)
                r0 = sm[gp][:nq, 0:nS]
                r1 = sm[gp][:nq, 4:4 + nS]
                P.add("dve", lambda e: e.reciprocal(r0.unsqueeze(2), A0[:, :, 128:129]), [pb[pa[0]]], [b_sm[gp]])
                P.add("dve", lambda e: e.reciprocal(r1.unsqueeze(2), A1[:, :, 128:129]), [pb[pa[1]]], [b_sm[gp]])
                TS(r1, r1, lams[:nq, 5:6], None, MUL, None, [b_sm[gp], b_lam], [b_sm[gp]])
                a0v = a0[gp][:nq, 0:nS * 128].rearrange("p (s d) -> p s d", d=128)
                hgv = hg[gp][:nq, 0:nS * 128].rearrange("p (s d) -> p s d", d=128)
                TT(a0v, A0[:, :, 0:128], r0.unsqueeze(2).to_broadcast([nq, nS, 128]), MUL, [pb[pa[0]], b_sm[gp]], [b_a0[gp]])
                TT(hgv, A1[:, :, 0:128], r1.unsqueeze(2).to_broadcast([nq, nS, 128]), MUL, [pb[pa[1]], b_sm[gp]], [b_hg[gp]])
                TT(hgv, hgv, a0v, ADD, [b_hg[gp], b_a0[gp]], [b_hg[gp]])
                for slot, t in enumerate(Q):
                    P.add("dve", lambda e, j_=a0[gp][:nq, 0:128], x_=hg[gp][:nq, slot * 128:(slot + 1) * 128],
                          o_=ssq[:nq, t, 4 + h:5 + h]: e.scalar_tensor_tensor(j_, x_, 1.0, x_, MUL, MUL, accum_out=o_),
                          [b_hg[gp]], [b_a0[gp], b_ssq[t]])
                TT(hout[gp][:nq, 0:nS * 128], hg[gp][:nq, 0:nS * 128], zs[gp][:nq, 0:nS * 128], MUL, [b_hg[gp], b_zs[gp]], [b_hout[gp]])
                p0 = TP[Q[0]][0]
                cols = slice(1024 + h * 128, 1024 + (h + 1) * 128)
                if nS == 1:
                    DMA("sp", hcat_d[p0:p0 + nq, cols], hout[gp][:nq, 0:128], [b_hout[gp]], [b_hcat[t] for t in Q])
                else:
                    DMA("sp", hcat_d[p0:p0 + nS * 128, cols].rearrange("(s p) d -> p s d", p=128),
                        hout[gp][:nq, 0:nS * 128].rearrange("p (s d) -> p s d", d=128), [b_hout[gp]], [b_hcat[t] for t in Q])

            NI = len(items)
            for k in range(NI):
                S(k)
                E(k)
                if k >= 1:
                    V(k - 1)
            V(NI - 1)

        def out_phase(l, s, last):
            FENCE(M_BUFS + O_BUFS)
            WA, bWA = load_weights(wo_d[l, 0:1024, :], 1024)
            WB, bWB = load_weights(wo_d[l, 1024:2048, :], 1024)
            if not last:
                load_gpre(l + 1)
            def O1a(t):
                p0, n = TP[t]
                i = t % 2
                DMA("sp", hc[i][:n, :], hcat_d[p0:p0 + n, :], [b_hcat[t]], [b_hc[i]])
                ACT(rsd[:n, 0:4], ssq[:n, t, 0:4], AF.Ln, [b_ssq[t]], [b_rsd], scale=1.0 / 1024, bias=EPS)
                ACT(rsd[:n, 4:12], ssq[:n, t, 4:12], AF.Ln, [b_ssq[t]], [b_rsd], scale=1.0 / 128, bias=EPS)
                ACT(rsd[:n, 0:12], rsd[:n, 0:12], AF.Exp, [b_rsd], [b_rsd], scale=-0.5)
                TT(hc[i][:n, 0:1024].rearrange("p (h d) -> p h d", d=256), hc[i][:n, 0:1024].rearrange("p (h d) -> p h d", d=256),
                   rsd[:n, 0:4].unsqueeze(2).to_broadcast([n, 4, 256]), MUL, [b_hc[i], b_rsd], [b_hc[i]])
                TT(hc[i][:n, 1024:2048].rearrange("p (h d) -> p h d", d=128), hc[i][:n, 1024:2048].rearrange("p (h d) -> p h d", d=128),
                   rsd[:n, 4:12].unsqueeze(2).to_broadcast([n, 8, 128]), MUL, [b_hc[i], b_rsd], [b_hc[i]])
                for half in range(2):
                    for f in range(8):
                        fc = half * 8 + f
                        TR(psb[half][:, f * 128:f * 128 + n], hc[i][:n, fc * 128:(fc + 1) * 128], identb[:n, :n],
                           [b_hc[i], b_const], [pb[half]])
                    src = psb[half][:, 0:1024].rearrange("p (k j) -> p k j", j=128)[:, :, 0:n]
                    CP(hcTs[i][:, half * 8:(half + 1) * 8, 0:n], src, [pb[half]], [b_hcTs[i]], eng=("act" if half else "dve"))

            def O1b(t):
                p0, n = TP[t]
                i = t % 2
                if l == 0:
                    src = meta_d if t == 0 else x_d[s, p0 - 16:p0 - 16 + n, :]
                    DMA("sp", xo[i][:n, :], src, [], [b_xo[i]])
                else:
                    DMA("sp", xo[i][:n, :], xres_d[p0:p0 + n, :], [b_xres[t]], [b_xo[i]])
                ya = [2, 3] if i == 0 else [4, 5]
                for half in range(2):
                    for fc in range(16):
                        Wx, bWx = (WA, bWA) if fc < 8 else (WB, bWB)
                        MM(ps[ya[half]][:n, :], hcTs[i][:, fc, 0:n], Wx[:, fc % 8, half * 512:(half + 1) * 512], fc == 0, fc == 15,
                           [b_hcTs[i], bWx], [pb[ya[half]]])

            def O2(t):
                p0, n = TP[t]
                i = t % 2
                ya = [2, 3] if i == 0 else [4, 5]
                for half in range(2):
                    ACT(junkb[:n, 0:512], ps[ya[half]][:n, :], AF.Square, [pb[ya[half]]], [b_junk, b_ss],
                        accum_out=ss[:n, 4 + half:5 + half])
                TT(ss[:n, 6:7], ss[:n, 4:5], ss[:n, 5:6], ADD, [b_ss], [b_ss])
                ACT(ss[:n, 6:7], ss[:n, 6:7], AF.Ln, [b_ss], [b_ss], scale=1.0 / D_MODEL, bias=EPS)
                ACT(ss[:n, 7:8], ss[:n, 6:7], AF.Exp, [b_ss], [b_ss], scale=-0.5)
                for half in range(2):
                    hs = slice(half * 512, (half + 1) * 512)
                    STT(xn[i][:n, hs], ps[ya[half]][:n, :], ss[:n, 7:8], grow[:n, 1, hs], MUL, MUL,
                        [pb[ya[half]], b_ss, b_grow[1]], [b_xn[i]])
                    TT(xn[i][:n, hs], xn[i][:n, hs], xo[i][:n, hs], ADD, [b_xn[i], b_xo[i]], [b_xn[i]])
                if last:
                    if t > 0:
                        DMA("sp", out_d[s, p0 - 16:p0 - 16 + n, :], xn[i][:n, :], [b_xn[i]], [b_out])
                else:
                    DMA("sp", xres_d[p0:p0 + n, :], xn[i][:n, :], [b_xn[i]], [b_xres[t]])
                    norm_tile(t, xn[i][:n, :], b_xn[i], 6 + i)

            O1a(0)
            for t in range(NT):
                if t + 1 < NT:
                    O1a(t + 1)
                O1b(t)
                if t >= 1:
                    O2(t - 1)
            O2(NT - 1)
            FENCE(M_BUFS + O_BUFS)

        for s in range(n_seq):
            load_gpre(0)
            FENCE(M_BUFS + O_BUFS)
            for t in range(NT):
                p0, n = TP[t]
                i = t % 2
                src = meta_d if t == 0 else x_d[s, p0 - 16:p0 - 16 + n, :]
                DMA("sp", xo[i][:n, :], src, [], [b_xo[i]])
                norm_tile(t, xo[i][:n, :], b_xo[i], 6 + i)
            FENCE(M_BUFS + O_BUFS)
            for l in range(n_layers):
                lam_init = load_layer_params(l)
                gates_phase(l)
                nxt = load_weights(wm_d[l, 0], 1280)
                for _ in mlstm_prep_gen(l, 0, nxt[0], nxt[1], [0, 1, 3, 7]):
                    pass
                for h in range(4):
                    W, bW = nxt
                    nxt = load_weights(wm_d[l, h + 1], 1280) if h < 3 else load_weights(wd_d[l, 0], 512)
                    hook = None
                    gen = None
                    if h < 3:
                        gen = mlstm_prep_gen(l, h + 1, nxt[0], nxt[1], [3, 7])

                        def hook(t, gen=gen):
                            if t >= 7:
                                for _ in range(4):
                                    next(gen, None)
                    mlstm_head(l, h, W, bW, hook)
                    if gen is not None:
                        for _ in gen:
                            pass
                for h in range(8):
                    W, bW = nxt
                    if h < 7:
                        nxt = load_weights(wd_d[l, h + 1], 512)
                    diff_head(l, h, W, bW, lam_init)
                out_phase(l, s, l == n_layers - 1)
        if os.environ.get('MK_SBUF'):
            print('SBUF remaining', nc.sbuf_bytes_remaining)
        P.emit(st)
    return nc


_CACHE = {}


def _host_layout(inp):
    bf = ml_dtypes.bfloat16
    w_in = np.asarray(inp["w_in"], dtype=np.float32)
    D = 1024
    sec = lambda k: w_in[:, :, k * D:(k + 1) * D] if k < 5 else None
    qm, km, vm, om, zm = (w_in[:, :, k * D:(k + 1) * D] for k in range(5))
    wg = np.ascontiguousarray(w_in[:, :, 5 * D:5 * D + 8])
    off = 5 * D + 8
    qd, kd, vd, zd = (w_in[:, :, off + k * D:off + (k + 1) * D] for k in range(4))
    wm = np.empty((DEPTH, 4, D, 1280), np.float32)
    for h in range(4):
        hs = slice(h * 256, (h + 1) * 256)
        wm[:, h] = np.concatenate([qm[:, :, hs], km[:, :, hs], vm[:, :, hs], om[:, :, hs], zm[:, :, hs]], axis=-1)
    wd = np.empty((DEPTH, 8, D, 512), np.float32)
    for h in range(8):
        hs = slice(h * 128, (h + 1) * 128)
        wd[:, h] = np.concatenate([qd[:, :, hs], kd[:, :, hs], vd[:, :, hs], zd[:, :, hs]], axis=-1)
    grow = np.stack([inp["pre_norm_g"], inp["post_norm_g"], inp["mlstm_norm_g"], inp["diff_norm_g"]], axis=1).astype(np.float32)
    convw = np.ascontiguousarray(np.asarray(inp["conv_w"], np.float32).transpose(0, 2, 1).reshape(DEPTH, 16, 128, 4).transpose(0, 2, 1, 3))
    convb = np.ascontiguousarray(np.asarray(inp["conv_b"], np.float32).reshape(DEPTH, 16, 128).transpose(0, 2, 1))
    bgt = np.ascontiguousarray(np.asarray(inp["b_gates"], np.float32).reshape(DEPTH, 2, 4).transpose(0, 2, 1))
    lam = np.concatenate([inp["lambda_q1"], inp["lambda_k1"], inp["lambda_q2"], inp["lambda_k2"]], axis=-1).astype(np.float32)[:, None, :]
    sel = np.zeros((4, 512), np.float32)
    for h in range(4):
        sel[h, h * 128:(h + 1) * 128] = 1.0
    mask = np.triu(np.ones((128, 128), np.float32)).astype(bf)
    common = {
        "meta": np.ascontiguousarray(inp["meta_tokens"], dtype=np.float32),
        "wm": wm, "wd": wd, "wg": wg, "wo": np.ascontiguousarray(inp["w_out"], dtype=np.float32),
        "grow": np.ascontiguousarray(grow), "convw": convw, "convb": convb, "bg": bgt, "lam": np.ascontiguousarray(lam),
        "identb": np.eye(128, dtype=np.float32).astype(bf), "identf": np.eye(128, dtype=np.float32),
        "mask": mask, "sel": sel,
    }
    return common


def kernel(**inputs):
    n_layers = int(os.environ.get("MK_LAYERS", DEPTH))
    key = ("nc", n_layers)
    if key not in _CACHE:
        _CACHE[key] = build_program(n_layers=n_layers, n_seq=2)
    nc = _CACHE[key]
    common = _host_layout(inputs)
    x = np.asarray(inputs["x"], dtype=np.float32)
    in_maps = []
    for c in range(8):
        m = dict(common)
        m["x"] = np.ascontiguousarray(x[2 * c:2 * c + 2])
        in_maps.append(m)
    res = run_bass_kernel_spmd(nc, in_maps, core_ids=list(range(8)))
    out = np.concatenate([r["out"] for r in res.results], axis=0)
    return out.astype(np.float32)
```

```python
import contextlib
import math
import os
import numpy as np
import ml_dtypes
import concourse.bass as bass
import concourse.mybir as mybir
from concourse.bass_utils import run_bass_kernel_spmd

F32 = mybir.dt.float32
BF16 = mybir.dt.bfloat16
AF = mybir.ActivationFunctionType
ALU = mybir.AluOpType
AX = mybir.AxisListType

ENGS = ["pe", "act", "dve", "pool", "sp"]


class Buf:
    __slots__ = ("n", "w", "r")

    def __init__(self, n=""):
        self.n = n
        self.w = None
        self.r = []


class Op:
    __slots__ = ("eng", "fn", "deps", "sig", "tok", "dma")


class Prog:
    NDMA = 8
    ROLL = 30000

    def __init__(self, nc):
        self.nc = nc
        self.ops = {e: [] for e in ENGS}
        self.dq = {e: {"next": 0, "last": [None] * self.NDMA, "cnt": [0] * self.NDMA} for e in ENGS}
        self.all_dma = []

    def add(self, eng, fn, reads=(), writes=(), dma=False):
        o = Op()
        o.eng, o.fn, o.dma, o.sig, o.tok = eng, fn, dma, False, None
        cand = []
        for b in reads:
            if b.w is not None:
                cand.append((b.w, True))
        for b in writes:
            if b.w is not None:
                cand.append((b.w, False))
            for r in b.r:
                cand.append((r, False))
        deps, seen = [], set()
        if dma:
            q = self.dq[eng]
            s = q["next"]
            q["next"] = (s + 1) % self.NDMA
            if q["last"][s] is not None:
                cand.append((q["last"][s], True))
            q["cnt"][s] += 16
            o.tok = (("d", eng, s), q["cnt"][s])
            q["last"][s] = o
            self.all_dma.append(o)
        for p, raw in cand:
            if p is o or id(p) in seen:
                continue
            if (not dma) and (not p.dma) and p.eng == eng:
                if eng == "pe":
                    continue
            seen.add(id(p))
            deps.append(p)
            if not p.dma:
                p.sig = True
        o.deps = deps
        for b in reads:
            if dma:
                b.r.append(o)
            else:
                b.r = [x for x in b.r if x.dma or x.eng != eng] + [o]
        for b in writes:
            b.w = o
            b.r = []
        self.ops[eng].append(o)
        return o

    def emit(self, stack):
        nc = self.nc
        keys = []
        for e in ENGS:
            cnt, gen, used = 0, 0, False
            for o in self.ops[e]:
                if o.dma or not o.sig:
                    continue
                cnt += 1
                if cnt > self.ROLL:
                    gen += 1
                    cnt = 1
                o.tok = (("c", e, gen), cnt)
                used = True
            if used:
                for g in range(gen + 1):
                    keys.append(("c", e, g))
            for s in range(self.NDMA):
                if self.dq[e]["cnt"][s]:
                    keys.append(("d", e, s))
        sems = {k: stack.enter_context(nc.semaphore("s_%s_%s_%d" % k)) for k in keys}
        final = {}
        for o in self.all_dma:
            final[o.tok[0]] = max(final.get(o.tok[0], 0), o.tok[1])

        def mk(e):
            def body(engine):
                waited = {}
                for o in self.ops[e]:
                    for p in o.deps:
                        k, v = p.tok
                        if waited.get(k, 0) < v:
                            engine.wait_ge(sems[k], v)
                            waited[k] = v
                    ins = o.fn(engine)
                    if o.dma:
                        ins.then_inc(sems[o.tok[0]], 16)
                    elif o.sig:
                        ins.then_inc(sems[o.tok[0]], 1)
                if e == "sp":
                    for k, v in final.items():
                        if waited.get(k, 0) < v:
                            engine.wait_ge(sems[k], v)
            return body

        with nc.Block() as block:
            block.tensor(mk("pe"))
            block.scalar(mk("act"))
            block.vector(mk("dve"))
            block.gpsimd(mk("pool"))
            block.sync(mk("sp"))


D_MODEL = 1024
SEQ = 2048
N_META = 16
L = SEQ + N_META
NT = 17
DEPTH = 4
EPS = 1e-6
TP = [(0, 16)] + [(16 + 128 * (t - 1), 128) for t in range(1, NT)]
GR = [(0, 400, [0, 1, 2, 3]), (400, 512, [4, 5, 6, 7]), (912, 512, [8, 9, 10, 11]),
      (1424, 512, [12, 13, 14, 15]), (1936, 128, [16])]
TG = {}
for _g, (_p, _w, _ts) in enumerate(GR):
    for _t in _ts:
        TG[_t] = _g


def build_program(n_layers=DEPTH, n_seq=2, dbg=False):
    nc = bass.Bass("TRN2", target_bir_lowering=False)

    def din(name, shape, dt=F32):
        return nc.dram_tensor(name, list(shape), dt, kind="ExternalInput").ap()

    x_d = din("x", [n_seq, SEQ, D_MODEL])
    meta_d = din("meta", [N_META, D_MODEL])
    wm_d = din("wm", [DEPTH, 4, D_MODEL, 1280])
    wd_d = din("wd", [DEPTH, 8, D_MODEL, 512])
    wg_d = din("wg", [DEPTH, D_MODEL, 8])
    wo_d = din("wo", [DEPTH, 2048, D_MODEL])
    grow_d = din("grow", [DEPTH, 4, D_MODEL])
    convw_d = din("convw", [DEPTH, 128, 16, 4])
    convb_d = din("convb", [DEPTH, 128, 16])
    bg_d = din("bg", [DEPTH, 4, 2])
    lam_d = din("lam", [DEPTH, 1, 256])
    identb_d = din("identb", [128, 128], BF16)
    identf_d = din("identf", [128, 128])
    mask_d = din("mask", [128, 128], BF16)
    sel_d = din("sel", [4, 512])
    out_d = nc.dram_tensor("out", [n_seq, SEQ, D_MODEL], F32, kind="ExternalOutput").ap()
    xres_d = nc.dram_tensor("xres", [L, D_MODEL], F32).ap()
    hcat_d = nc.dram_tensor("hcat", [L, 2048], BF16).ap()

    with contextlib.ExitStack() as st:
        P = Prog(nc)

        def T(name, shape, dt):
            return st.enter_context(nc.sbuf_tensor("sb_" + name, list(shape), dt))

        hT = T("hT", [128, 8, L], BF16)
        Wb = [T("W0", [128, 8, 1280], BF16), T("W1", [128, 8, 1280], BF16)]
        raw = T("raw", [128, 3 + L + 1], F32)
        acc = T("acc", [128, L + 4], F32)
        mqs = [T("mq", [128, 2, L], BF16), T("mq1", [128, 2, L], BF16)]
        mks = [T("mk", [128, 2, L], BF16), T("mk1", [128, 2, L], BF16)]
        mq, mk = mqs[0], mks[0]
        dq = T("dq", [128, L], BF16)
        dk = T("dk", [128, L], BF16)
        dv = T("dv", [128, NT, 130], BF16)
        CTs = [T("CTa", [128, 2, 257], F32), T("CTb", [128, 2, 257], F32)]
        CTdb = T("CTdb", [128, 2, 257], BF16)
        grow = T("grow", [128, 4, D_MODEL], F32)
        identb = T("identb", [128, 128], BF16)
        identf = T("identf", [128, 128], F32)
        maskb = T("maskb", [128, 128], BF16)
        sel = T("sel", [4, 512], F32)
        convw = T("convw", [128, 16, 4], F32)
        convb = T("convb", [128, 16], F32)
        bg = T("bg", [4, 4], F32)
        lamt = T("lamt", [128, 256], F32)
        lams = T("lams", [128, 8], F32)
        gatesT = T("gatesT", [128, NT, 12], F32)
        decs = T("decs", [4, 2 * NT], F32)
        decb = T("decb", [128, 4, 2 * NT], F32)
        ssq = T("ssq", [128, NT, 12], F32)
        rsd = T("rsd", [128, 16], F32)
        ss = T("ss", [128, 8], F32)
        vext = [T("vext%d" % i_, [128, 258], BF16) for i_ in range(3)]
        th = [T("th%d" % i_, [128, 512], F32) for i_ in range(3)]
        zs = [T("zs%d" % i_, [128, 384], F32) for i_ in range(3)]
        kw = [T("kw%d" % i_, [128, 256], BF16) for i_ in range(3)]
        SwT = [T("SwT%d" % i_, [128, 128], BF16) for i_ in range(3)]
        hg = [T("hg0", [128, 384], F32), T("hg1", [128, 384], F32)]
        hout = [T("hout0", [128, 384], BF16), T("hout1", [128, 384], BF16)]
        sm = [T("sm0", [128, 8], F32), T("sm1", [128, 8], F32)]
        PT = [T("PT%d" % i_, [128, 512], BF16) for i_ in range(6)]
        a0 = [T("a00", [128, 384], F32), T("a01", [128, 384], F32)]
        dummy = T("dummy", [1, 8], F32)
        ps = [st.enter_context(nc.psum_tensor("ps%d" % i, [128, 512], F32)) for i in range(8)]
        psb = [p[:].bitcast(BF16) for p in ps]
        pb = [Buf("ps%d" % i) for i in range(8)]

        rawb = raw[:].bitcast(BF16)
        accb = acc[:].bitcast(BF16)
        hc = [rawb[:, 0:2048], rawb[:, 2048:4096]]
        hcT0 = accb[:, 0:2048].rearrange("p (f j) -> p f j", j=128)
        hcT1 = mqs[1][:].rearrange("p c l -> p (c l)")[:, 0:2048].rearrange("p (f j) -> p f j", j=128)
        hcTs = [hcT0, hcT1]
        hb = accb[:, 2048:3072]
        junkb = accb[:, 3072:4096]
        mqf = mq[:].rearrange("p c l -> p (c l)").bitcast(F32)
        mkf = mk[:].rearrange("p c l -> p (c l)").bitcast(F32)
        xo = [mqf[:, 0:1024], mqf[:, 1024:2048]]
        xn = [mkf[:, 0:1024], mkf[:, 1024:2048]]
        T1 = raw[0:4, 4:4 + L]
        T2 = acc[0:4, 0:L]
        T3 = mqf[0:4, 0:L]

        b_hT = [Buf("hT%d" % t) for t in range(NT)]
        b_W = [Buf("W0"), Buf("W1")]
        b_raw, b_acc = Buf("raw"), Buf("acc")
        b_mqs = [[Buf("mq%d_%d" % (p_, g)) for g in range(5)] for p_ in range(2)]
        b_mks = [[Buf("mk%d_%d" % (p_, g)) for g in range(5)] for p_ in range(2)]
        b_mq, b_mk = b_mqs[0], b_mks[0]
        b_dq = [Buf("dq%d" % g) for g in range(5)]
        b_dk = [Buf("dk%d" % g) for g in range(5)]
        b_dv = [Buf("dv%d" % t) for t in range(NT)]
        b_CTs = [Buf(), Buf()]
        b_CTdb = Buf()
        b_grow = [Buf("g%d" % i) for i in range(4)]
        b_const = Buf("const")
        b_lp = Buf("layerparams")
        b_lam = Buf("lam")
        b_gT, b_decs, b_decb = Buf(), Buf(), Buf()
        b_ssq = [Buf("ssq%d" % t) for t in range(NT)]
        b_rsd, b_ss = Buf(), Buf()
        b_vext, b_th, b_zs, b_kw, b_SwT, b_hg, b_hout, b_sm = ([Buf(), Buf(), Buf()] for _ in range(8))
        b_PT = [Buf() for _ in range(6)]
        b_a0 = [Buf(), Buf()]
        b_hc, b_xo, b_xn = [Buf(), Buf()], [Buf(), Buf()], [Buf(), Buf()]
        b_hcTs = [Buf(), Buf()]
        b_hb, b_junk = Buf(), Buf()
        b_dummy = Buf()
        b_xres = [Buf("xres%d" % t) for t in range(NT)]
        b_hcat = [Buf("hcat%d" % t) for t in range(NT)]
        b_out = Buf("out")

        def MM(out, lhsT, rhs, start, stop, R, W):
            P.add("pe", lambda e: e.matmul(out, lhsT, rhs, start=start, stop=stop), R, W)

        def TR(out, in_, ident, R, W):
            P.add("pe", lambda e: e.transpose(out, in_, ident), R, W)

        def ACT(out, in_, func, R, W, **kw_):
            P.add("act", lambda e: e.activation(out, in_, func, **kw_), R, W)

        def TS(out, in0, s1, s2, op0, op1, R, W, eng="dve"):
            if op1 is None:
                P.add(eng, lambda e: e.tensor_scalar(out, in0, s1, None, op0), R, W)
            else:
                P.add(eng, lambda e: e.tensor_scalar(out, in0, s1, s2, op0, op1), R, W)

        def TT(out, in0, in1, op, R, W, eng="dve"):
            P.add(eng, lambda e: e.tensor_tensor(out, in0, in1, op), R, W)

        def STT(out, in0, sc, in1, op0, op1, R, W):
            P.add("dve", lambda e: e.scalar_tensor_tensor(out, in0, sc, in1, op0, op1), R, W)

        def CP(out, in_, R, W, eng="dve"):
            if eng == "act":
                P.add("act", lambda e: e.copy(out, in_), R, W)
            else:
                P.add(eng, lambda e: e.tensor_copy(out, in_), R, W)

        def MS(ap, val, W, eng="pool"):
            P.add(eng, lambda e: e.memset(ap, val), [], W)

        def DMA(q, out, in_, R, W):
            P.add(q, lambda e: e.dma_start(out=out, in_=in_), R, W, dma=True)

        def FENCE(bufs):
            P.add("pool", lambda e: e.memset(dummy[0:1, 0:1], 0.0), [], list(bufs) + [b_dummy])

        MUL, ADD, SUB, MAX = ALU.mult, ALU.add, ALU.subtract, ALU.max

        DMA("sp", identb[:], identb_d, [], [b_const])
        DMA("sp", identf[:], identf_d, [], [b_const])
        DMA("sp", maskb[:], mask_d, [], [b_const])
        DMA("sp", sel[:], sel_d, [], [b_const])
        for i in range(3):
            MS(vext[i][:, 256:258], 1.0, [b_vext[i]])
        for t in range(NT):
            MS(dv[:, t, 128:130], 1.0, [b_dv[t]])
        MS(raw[:, 0:3], 0.0, [b_raw])

        M_BUFS = [b_raw, b_acc] + b_mqs[0] + b_mks[0] + b_mqs[1] + b_mks[1]
        O_BUFS = b_hc + b_xo + b_xn + b_hcTs + [b_hb, b_junk]

        wslot = [0]

        def load_weights(src, ncols, krows=8):
            i = wslot[0]
            wslot[0] ^= 1
            for kc in range(krows):
                DMA("pool", Wb[i][:, kc, 0:ncols], src[kc * 128:(kc + 1) * 128, :], [], [b_W[i]])
            return Wb[i], b_W[i]

        def norm_tile(t, xt, bx, bank):
            pos0, n = TP[t]
            ACT(junkb[:n, :], xt, AF.Square, [bx], [b_junk, b_ss], accum_out=ss[:n, 0:1])
            ACT(ss[:n, 1:2], ss[:n, 0:1], AF.Ln, [b_ss], [b_ss], scale=1.0 / D_MODEL, bias=EPS)
            ACT(ss[:n, 2:3], ss[:n, 1:2], AF.Exp, [b_ss], [b_ss], scale=-0.5)
            STT(hb[:n, :], xt, ss[:n, 2:3], grow[:n, 0, :], MUL, MUL, [bx, b_ss, b_grow[0]], [b_hb])
            for kc in range(8):
                TR(psb[bank][:, kc * 128:kc * 128 + n], hb[:n, kc * 128:(kc + 1) * 128], identb[:n, :n],
                   [b_hb, b_const], [pb[bank]])
            src = psb[bank][:, 0:1024].rearrange("p (k j) -> p k j", j=128)[:, :, 0:n]
            CP(hT[:, :, pos0:pos0 + n], src, [pb[bank]], [b_hT[t]], eng="act")

        def load_gpre(l):
            DMA("sp", grow[:, 0, :], grow_d[l, 0:1, :].partition_broadcast(128), [], [b_grow[0]])

        def load_layer_params(l):
            for i in range(1, 4):
                DMA("sp", grow[:, i, :], grow_d[l, i:i + 1, :].partition_broadcast(128), [], [b_grow[i]])
            DMA("sp", convw[:], convw_d[l], [], [b_lp])
            DMA("sp", convb[:], convb_d[l], [], [b_lp])
            DMA("sp", bg[:, 0:2], bg_d[l], [], [b_lp])
            DMA("sp", lamt[:], lam_d[l].partition_broadcast(128), [], [b_lam])
            TS(bg[:, 2:3], bg[:, 1:2], -1.0, None, MUL, None, [b_lp], [b_lp])
            TS(grow[:, 2, :], grow[:, 2, :], 0.25, None, MUL, None, [b_grow[2]], [b_grow[2]], eng="pool")
            lam_init = 0.8 - 0.6 * math.exp(-0.3 * l)
            TT(lamt[:, 0:64], lamt[:, 0:64], lamt[:, 64:128], MUL, [b_lam], [b_lam])
            TT(lamt[:, 128:192], lamt[:, 128:192], lamt[:, 192:256], MUL, [b_lam], [b_lam])
            P.add("dve", lambda e: e.reduce_sum(lams[:, 0:1], lamt[:, 0:64], axis=AX.X), [b_lam], [b_lam])
            P.add("dve", lambda e: e.reduce_sum(lams[:, 1:2], lamt[:, 128:192], axis=AX.X), [b_lam], [b_lam])
            ACT(lams[:, 2:4], lams[:, 0:2], AF.Exp, [b_lam], [b_lam])
            TT(lams[:, 4:5], lams[:, 2:3], lams[:, 3:4], SUB, [b_lam], [b_lam])
            TS(lams[:, 4:5], lams[:, 4:5], lam_init, None, ADD, None, [b_lam], [b_lam])
            TS(lams[:, 5:6], lams[:, 4:5], -1.0, None, MUL, None, [b_lam], [b_lam])
            return lam_init

        def gates_phase(l):
            W, bW = load_weights(wg_d[l], 8)
            allm = M_BUFS
            for g, (p0, w, ts) in enumerate(GR):
                bi, bf_ = (0, 1) if g % 2 == 0 else (2, 3)
                hr = [b_hT[t] for t in ts]
                for kc in range(8):
                    MM(ps[bi][0:4, 0:w], W[:, kc, 0:4], hT[:, kc, p0:p0 + w], kc == 0, kc == 7, hr + [bW], [pb[bi]])
                for kc in range(8):
                    MM(ps[bf_][0:4, 0:w], W[:, kc, 4:8], hT[:, kc, p0:p0 + w], kc == 0, kc == 7, hr + [bW], [pb[bf_]])
                TS(T1[:, p0:p0 + w], ps[bi][0:4, 0:w], bg[:, 0:1], None, ADD, None, [pb[bi], b_lp], allm)
                ACT(T2[:, p0:p0 + w], ps[bf_][0:4, 0:w], AF.Exp, [pb[bf_], b_lp], allm, scale=-1.0, bias=bg[:, 2:3])
            ACT(T2, T2, AF.Ln, allm, allm, bias=1.0)
            P.add("dve", lambda e: e.tensor_tensor_scan(T3, T2, T2, 0.0, ADD, MAX), allm, allm)
            TT(T1, T1, T3, ADD, allm, allm)
            P.add("dve", lambda e: e.tensor_tensor_scan(T2, T1, T1, 0.0, MAX, MAX), allm, allm)
            ge = T2[:, 15:L:128]
            TS(decs[:, 0:1], T2[:, 15:16], -1.0, None, MUL, None, allm, [b_decs])
            TT(decs[:, 1:NT], T2[:, 15:L - 128:128], T2[:, 143:L:128], SUB, allm, [b_decs])
            ACT(decs[:, 0:NT], decs[:, 0:NT], AF.Exp, [b_decs], [b_decs])
            TS(decs[:, NT:2 * NT], decs[:, 0:NT], 1.0 / 16, None, MUL, None, [b_decs], [b_decs])
            gl = T2[:, 143:L:128].unsqueeze(2).to_broadcast([4, 16, 128])
            for Tx in (T1, T3):
                TT(Tx[:, 16:L].rearrange("p (t j) -> p t j", j=128), Tx[:, 16:L].rearrange("p (t j) -> p t j", j=128),
                   gl, SUB, allm, allm)
                TS(Tx[:, 0:16], Tx[:, 0:16], T2[:, 15:16], None, SUB, None, allm, allm)
                ACT(Tx, Tx, AF.Exp, allm, allm)
            for h in range(4):
                MM(ps[4][:, h * 2 * NT:(h + 1) * 2 * NT], sel[:, h * 128:(h + 1) * 128], decs[:, :], True, True,
                   [b_const, b_decs], [pb[4]])
            CP(decb[:].rearrange("p h c -> p (h c)"), ps[4][:, 0:8 * NT], [pb[4]], [b_decb])
            for t in range(NT):
                p0, n = TP[t]
                TR(ps[5][:n, t * 8:t * 8 + 4], T1[:, p0:p0 + n], identf[0:4, 0:4], allm + [b_const], [pb[5]])
                TR(ps[5][:n, t * 8 + 4:t * 8 + 8], T3[:, p0:p0 + n], identf[0:4, 0:4], allm + [b_const], [pb[5]])
            CP(gatesT[0:16, 0, 0:8], ps[5][0:16, 0:8], [pb[5]], [b_gT])
            CP(gatesT[:, 1:NT, 0:8], ps[5][:, 8:8 * NT].rearrange("p (t c) -> p t c", c=8), [pb[5]], [b_gT])
            TS(gatesT[0:16, 0, 8:12], gatesT[0:16, 0, 0:4], 1.0 / 16, None, MUL, None, [b_gT], [b_gT])
            TS(gatesT[:, 1:NT, 8:12], gatesT[:, 1:NT, 0:4], 1.0 / 16, None, MUL, None, [b_gT], [b_gT])

        def mlstm_prep_gen(l, h, W, bW, banks):
            hp = h % 2
            MS(raw[:, 0:3], 0.0, [b_raw])
            bi = 0
            for c in range(4):
                dst, bdst = (mqs[hp], b_mqs[hp]) if c < 2 else (mks[hp], b_mks[hp])
                cc = (0 if c < 2 else 8) + h * 2 + (c % 2)
                for g, (p0, w, ts) in enumerate(GR):
                    bk = banks[bi % len(banks)]
                    bi += 1
                    hr_ = [b_hT[t] for t in ts]
                    for kc in range(8):
                        MM(ps[bk][:, 0:w], W[:, kc, c * 128:(c + 1) * 128], hT[:, kc, p0:p0 + w], kc == 0, kc == 7,
                           hr_ + [bW], [pb[bk]])
                    CP(raw[:, 3 + p0:3 + p0 + w], ps[bk][:, 0:w], [pb[bk]], [b_raw], eng="act")
                    yield
                TS(acc[:, 0:L], raw[:, 3:3 + L], convw[:, cc, 3:4], None, MUL, None, [b_raw, b_lp], [b_acc])
                yield
                for j in (2, 1, 0):
                    STT(acc[:, 0:L], raw[:, j:j + L], convw[:, cc, j:j + 1], acc[:, 0:L], MUL, ADD,
                        [b_raw, b_acc, b_lp], [b_acc])
                    yield
                ACT(dst[:, c % 2, :], acc[:, 0:L], AF.Silu, [b_acc, b_lp], bdst, bias=convb[:, cc:cc + 1])
                yield

        def mlstm_head(l, h, W, bW, hook=None):
            hp = h % 2
            mq, mk, b_mq, b_mk = mqs[hp], mks[hp], b_mqs[hp], b_mks[hp]
            MS(CTs[0][:], 0.0, [b_CTs[0]])

            def A_pe(t):
                p0, n = TP[t]
                g = TG[t]
                for kc in range(8):
                    MM(ps[0][:n, 0:256], hT[:, kc, p0:p0 + n], W[:, kc, 512:768], kc == 0, kc == 7, [b_hT[t], bW], [pb[0]])
                for kc in range(8):
                    MM(ps[1][:n, 0:512], hT[:, kc, p0:p0 + n], W[:, kc, 768:1280], kc == 0, kc == 7, [b_hT[t], bW], [pb[1]])
                for c in range(2):
                    TR(psb[2][:n, c * 128:(c + 1) * 128], mk[:, c, p0:p0 + n], identb[:, :], [b_mk[g], b_const], [pb[2]])
                for c in range(2):
                    MM(ps[2][:n, 128:128 + n], mk[:, c, p0:p0 + n], mq[:, c, p0:p0 + n], c == 0, c == 1, [b_mk[g], b_mq[g]], [pb[2]])

            def A_other(t):
                p0, n = TP[t]
                i = t % 3
                CP(vext[i][:n, 0:256], ps[0][:n, 0:256], [pb[0]], [b_vext[i]], eng="act")
                ACT(th[i][:n, :], ps[1][:n, :], AF.Tanh, [pb[1]], [b_th[i]], scale=0.5)
                TS(kw[i][:n, :], psb[2][:n, 0:256], gatesT[:n, t, h:h + 1], None, MUL, None, [pb[2], b_gT], [b_kw[i]])
                STT(SwT[i][:n, :n], ps[2][:n, 128:128 + n], gatesT[:n, t, 8 + h:9 + h], maskb[:n, :n], MUL, MUL,
                    [pb[2], b_gT, b_const], [b_SwT[i]])
                STT(zs[i][:n, 0:256], th[i][:n, 256:512], 1.0, ps[1][:n, 256:512], ADD, MUL, [pb[1], b_th[i]], [b_zs[i]])
                TT(zs[i][:n, 0:256], zs[i][:n, 0:256], grow[:n, 2, h * 256:(h + 1) * 256], MUL, [b_zs[i], b_grow[2]], [b_zs[i]])

            def CTDB(t):
                CTo, bCTo = CTs[t % 2], b_CTs[t % 2]
                TS(CTdb[:], CTo[:], decb[:, h, NT + t:NT + t + 1], None, MUL, None, [bCTo, b_decb], [b_CTdb])

            def B_pe(t):
                p0, n = TP[t]
                g = TG[t]
                ia = t % 3
                for c in range(2):
                    MM(ps[5 + c][:, 0:257], kw[ia][:n, c * 128:(c + 1) * 128], vext[ia][:n, 0:257], True, True,
                       [b_kw[ia], b_vext[ia]], [pb[5 + c]])
                MM(ps[4][:n, 0:257], SwT[ia][:n, :n], vext[ia][:n, 0:257], True, False, [b_SwT[ia], b_vext[ia]], [pb[4]])
                for c in range(2):
                    MM(ps[4][:n, 0:257], mq[:, c, p0:p0 + n], CTdb[:, c, :], False, c == 1, [b_mq[g], b_CTdb], [pb[4]])

            def B_rest(t):
                p0, n = TP[t]
                i = t % 2
                CTo, bCTo = CTs[t % 2], b_CTs[t % 2]
                CTn, bCTn = CTs[(t + 1) % 2], b_CTs[(t + 1) % 2]
                for c in range(2):
                    STT(CTn[:, c, :], CTo[:, c, :], decb[:, h, t:t + 1], ps[5 + c][:, 0:257], MUL, ADD,
                        [pb[5 + c], bCTo, b_decb], [bCTn])
                TT(sm[i][:n, 0:1], ps[4][:n, 256:257], gatesT[:n, t, 4 + h:5 + h], MAX, [pb[4], b_gT], [b_sm[i]])
                STT(sm[i][:n, 0:1], ps[4][:n, 256:257], -1.0, sm[i][:n, 0:1], MUL, MAX, [pb[4], b_sm[i]], [b_sm[i]])
                P.add("dve", lambda e, o_=sm[i][:n, 1:2], i_=sm[i][:n, 0:1]: e.reciprocal(o_, i_), [b_sm[i]], [b_sm[i]])
                ACT(hrw[i][:n, :], ps[4][:n, 0:256], AF.Copy, [pb[4], b_sm[i]], [b_hr[i]], scale=sm[i][:n, 1:2])
                if t + 1 < NT:
                    CTDB(t + 1)

            def B2(t):
                p0, n = TP[t]
                i = t % 2
                ia = t % 3
                STT(hg[i][:n, 0:256], th[ia][:n, 0:256], 1.0, hrw[i][:n, :], ADD, MUL, [b_th[ia], b_hr[i]], [b_hg[i]])
                ACT(junkh[i][:n, :], hg[i][:n, 0:256], AF.Square, [b_hg[i]], [b_junkh[i], b_ssq[t]], accum_out=ssq[:n, t, h:h + 1])
                TT(hout[i][:n, 0:256], hg[i][:n, 0:256], zs[ia][:n, 0:256], MUL, [b_hg[i], b_zs[ia]], [b_hout[i]])
                DMA("sp", hcat_d[p0:p0 + n, h * 256:(h + 1) * 256], hout[i][:n, 0:256], [b_hout[i]], [b_hcat[t]])

            CTDB(0)
            A_pe(0)
            A_other(0)
            A_pe(1)
            A_other(1)
            for t in range(2, NT):
                B_pe(t - 2)
                A_pe(t)
                B_rest(t - 2)
                A_other(t)
                B2(t - 2)
                if hook is not None:
                    hook(t)
            for t in (NT - 2, NT - 1):
                B_pe(t)
                B_rest(t)
                B2(t)

        junkh = [T("junkh0", [128, 256], BF16), T("junkh1", [128, 256], BF16)]
        b_junkh = [Buf(), Buf()]
        hrw = [T("hr0", [128, 256], F32), T("hr1", [128, 256], F32)]
        b_hr = [Buf(), Buf()]

        def diff_head(l, h, W, bW, lam_init):
            scale = 64 ** -0.5
            bankrot = [0]
            for c, (dst, bdst) in enumerate(((dq, b_dq), (dk, b_dk))):
                for g, (p0, w, ts) in enumerate(GR):
                    bk = bankrot[0]
                    bankrot[0] ^= 1
                    hr = [b_hT[t] for t in ts]
                    for kc in range(8):
                        MM(ps[bk][:, 0:w], W[:, kc, c * 128:(c + 1) * 128], hT[:, kc, p0:p0 + w], kc == 0, kc == 7,
                           hr + [bW], [pb[bk]])
                    CP(dst[:, p0:p0 + w], ps[bk][:, 0:w], [pb[bk]], [bdst[g]], eng=("act" if g % 2 else "dve"))
            for t in range(NT):
                p0, n = TP[t]
                bk = t % 2
                for kc in range(8):
                    MM(ps[bk][:n, 0:128], hT[:, kc, p0:p0 + n], W[:, kc, 256:384], kc == 0, kc == 7, [b_hT[t], bW], [pb[bk]])
                CP(dv[:n, t, 0:128], ps[bk][:n, 0:128], [pb[bk]], [b_dv[t]], eng=("act" if t % 2 else "dve"))
            QG = [[0]] + [list(range(a_, min(a_ + 3, NT))) for a_ in range(1, NT, 3)]
            items = []
            for gq, Q in enumerate(QG):
                items.append(("z", gq, None, False))
                for j in range(0, Q[-1] + 1):
                    items.append(("a", gq, j, j == Q[-1]))
            SBP = [(0, 1), (2, 3)]
            cfac = 0.5 * (1.0 - lam_init)

            def S(k):
                ty, gq, j, last = items[k]
                Q = QG[gq]
                nq = TP[Q[0]][1]
                sb = SBP[k % 2]
                if ty == "z":
                    for slot, t in enumerate(Q):
                        p0, n = TP[t]
                        for kc in range(8):
                            MM(ps[sb[0]][:n, slot * 128:slot * 128 + 128], hT[:, kc, p0:p0 + n], W[:, kc, 384:512], kc == 0, kc == 7,
                               [b_hT[t], bW], [pb[sb[0]]])
                    return
                qs = [t for t in Q if t >= j]
                ps0 = TP[qs[0]][0]
                wq = sum(TP[t][1] for t in qs)
                kp0, kn = TP[j]
                for c in range(2):
                    cs = slice(64 * c, 64 * c + 64)
                    MM(ps[sb[c]][:kn, 0:wq], dk[cs, kp0:kp0 + kn], dq[cs, ps0:ps0 + wq], True, True,
                       [b_dk[TG[j]]] + [b_dq[TG[t]] for t in qs], [pb[sb[c]]])

            def E(k):
                ty, gq, j, last = items[k]
                Q = QG[gq]
                nq = TP[Q[0]][1]
                gp = gq % 2
                sb = SBP[k % 2]
                if ty == "z":
                    wz = len(Q) * 128
                    ACT(th[gp][:nq, 0:wz], ps[sb[0]][:nq, 0:wz], AF.Tanh, [pb[sb[0]]], [b_th[gp]], scale=0.5)
                    TS(th[gp][:nq, 0:wz], th[gp][:nq, 0:wz], cfac, cfac, MUL, ADD, [b_th[gp]], [b_th[gp]])
                    TT(zs[gp][:nq, 0:wz], ps[sb[0]][:nq, 0:wz], th[gp][:nq, 0:wz], MUL, [pb[sb[0]], b_th[gp]], [b_zs[gp]])
                    zv = zs[gp][:nq, 0:wz].rearrange("p (s d) -> p s d", d=128)
                    TT(zv, zv, grow[:nq, 3, h * 128:(h + 1) * 128].unsqueeze(1).to_broadcast([nq, len(Q), 128]), MUL,
                       [b_zs[gp], b_grow[3]], [b_zs[gp]])
                    return
                qs = [t for t in Q if t >= j]
                wq = sum(TP[t][1] for t in qs)
                kp0, kn = TP[j]
                for c in range(2):
                    pi = (k % 3) * 2 + c
                    ACT(PT[pi][:kn, 0:wq], ps[sb[c]][:kn, 0:wq], AF.Exp, [pb[sb[c]]], [b_PT[pi]], scale=scale)
                    if j >= Q[0]:
                        TT(PT[pi][:kn, 0:kn], PT[pi][:kn, 0:kn], maskb[:kn, :kn], MUL, [b_PT[pi], b_const], [b_PT[pi]])

            def V(k):
                ty, gq, j, last = items[k]
                if ty == "z":
                    return
                Q = QG[gq]
                nq = TP[Q[0]][1]
                nS = len(Q)
                gp = gq % 2
                pa = [4, 5] if gp == 0 else [6, 7]
                qs = [t for t in Q if t >= j]
                kp0, kn = TP[j]
                for bi_, t in enumerate(qs):
                    slot = t - Q[0]
                    n = TP[t][1]
                    for c in range(2):
                        pi = (k % 3) * 2 + c
                        P.add("pe", lambda e, o_=ps[pa[c]][:n, slot * 129:slot * 129 + 129], l_=PT[pi][:kn, bi_ * 128:bi_ * 128 + n],
                              r_=dv[:kn, j, 0:129], st_=(j == 0 and slot == 0), sp_=(j == t):
                              e.matmul(o_, l_, r_, start=st_, stop=sp_, skip_group_check=True),
                              [b_PT[pi], b_dv[j]], [pb[pa[c]]])
                if not last:
                    return
                A0 = ps[pa[0]][:nq, 0:nS * 129].rearrange("p (s d) -> p s d", d=129)
                A1 = ps[pa[1]][:nq, 0:nS * 129].rearrange("p (s d) -> p s d", d=129)
                r0 = sm[gp][:nq, 0:nS]
                r1 = sm[gp][:nq, 4:4 + nS]
                P.add("dve", lambda e: e.reciprocal(r0.unsqueeze(2), A0[:, :, 128:129]), [pb[pa[0]]], [b_sm[gp]])
                P.add("dve", lambda e: e.reciprocal(r1.unsqueeze(2), A1[:, :, 128:129]), [pb[pa[1]]], [b_sm[gp]])
                TS(r1, r1, lams[:nq, 5:6], None, MUL, None, [b_sm[gp], b_lam], [b_sm[gp]])
                a0v = a0[gp][:nq, 0:nS * 128].rearrange("p (s d) -> p s d", d=128)
                hgv = hg[gp][:nq, 0:nS * 128].rearrange("p (s d) -> p s d", d=128)
                TT(a0v, A0[:, :, 0:128], r0.unsqueeze(2).to_broadcast([nq, nS, 128]), MUL, [pb[pa[0]], b_sm[gp]], [b_a0[gp]])
                TT(hgv, A1[:, :, 0:128], r1.unsqueeze(2).to_broadcast([nq, nS, 128]), MUL, [pb[pa[1]], b_sm[gp]], [b_hg[gp]])
                TT(hgv, hgv, a0v, ADD, [b_hg[gp], b_a0[gp]], [b_hg[gp]])
                for slot, t in enumerate(Q):
                    P.add("dve", lambda e, j_=a0[gp][:nq, 0:128], x_=hg[gp][:nq, slot * 128:(slot + 1) * 128],
                          o_=ssq[:nq, t, 4 + h:5 + h]: e.scalar_tensor_tensor(j_, x_, 1.0, x_, MUL, MUL, accum_out=o_),
                          [b_hg[gp]], [b_a0[gp], b_ssq[t]])
                TT(hout[gp][:nq, 0:nS * 128], hg[gp][:nq, 0:nS * 128], zs[gp][:nq, 0:nS * 128], MUL, [b_hg[gp], b_zs[gp]], [b_hout[gp]])
                p0 = TP[Q[0]][0]
                cols = slice(1024 + h * 128, 1024 + (h + 1) * 128)
                if nS == 1:
                    DMA("sp", hcat_d[p0:p0 + nq, cols], hout[gp][:nq, 0:128], [b_hout[gp]], [b_hcat[t] for t in Q])
                else:
                    DMA("sp", hcat_d[p0:p0 + nS * 128, cols].rearrange("(s p) d -> p s d", p=128),
                        hout[gp][:nq, 0:nS * 128].rearrange("p (s d) -> p s d", d=128), [b_hout[gp]], [b_hcat[t] for t in Q])

            NI = len(items)
            for k in range(NI):
                S(k)
                E(k)
                if k >= 1:
                    V(k - 1)
            V(NI - 1)

        def out_phase(l, s, last):
            FENCE(M_BUFS + O_BUFS)
            WA, bWA = load_weights(wo_d[l, 0:1024, :], 1024)
            WB, bWB = load_weights(wo_d[l, 1024:2048, :], 1024)
            if not last:
                load_gpre(l + 1)
            def O1a(t):
                p0, n = TP[t]
                i = t % 2
                DMA("sp", hc[i][:n, :], hcat_d[p0:p0 + n, :], [b_hcat[t]], [b_hc[i]])
                ACT(rsd[:n, 0:4], ssq[:n, t, 0:4], AF.Ln, [b_ssq[t]], [b_rsd], scale=1.0 / 1024, bias=EPS)
                ACT(rsd[:n, 4:12], ssq[:n, t, 4:12], AF.Ln, [b_ssq[t]], [b_rsd], scale=1.0 / 128, bias=EPS)
                ACT(rsd[:n, 0:12], rsd[:n, 0:12], AF.Exp, [b_rsd], [b_rsd], scale=-0.5)
                TT(hc[i][:n, 0:1024].rearrange("p (h d) -> p h d", d=256), hc[i][:n, 0:1024].rearrange("p (h d) -> p h d", d=256),
                   rsd[:n, 0:4].unsqueeze(2).to_broadcast([n, 4, 256]), MUL, [b_hc[i], b_rsd], [b_hc[i]])
                TT(hc[i][:n, 1024:2048].rearrange("p (h d) -> p h d", d=128), hc[i][:n, 1024:2048].rearrange("p (h d) -> p h d", d=128),
                   rsd[:n, 4:12].unsqueeze(2).to_broadcast([n, 8, 128]), MUL, [b_hc[i], b_rsd], [b_hc[i]])
                for half in range(2):
                    for f in range(8):
                        fc = half * 8 + f
                        TR(psb[half][:, f * 128:f * 128 + n], hc[i][:n, fc * 128:(fc + 1) * 128], identb[:n, :n],
                           [b_hc[i], b_const], [pb[half]])
                    src = psb[half][:, 0:1024].rearrange("p (k j) -> p k j", j=128)[:, :, 0:n]
                    CP(hcTs[i][:, half * 8:(half + 1) * 8, 0:n], src, [pb[half]], [b_hcTs[i]], eng="dve")

            def O1b(t):
                p0, n = TP[t]
                i = t % 2
                if l == 0:
                    src = meta_d if t == 0 else x_d[s, p0 - 16:p0 - 16 + n, :]
                    DMA("sp", xo[i][:n, :], src, [], [b_xo[i]])
                else:
                    DMA("sp", xo[i][:n, :], xres_d[p0:p0 + n, :], [b_xres[t]], [b_xo[i]])
                ya = [2, 3] if i == 0 else [4, 5]
                for half in range(2):
                    for fc in range(16):
                        Wx, bWx = (WA, bWA) if fc < 8 else (WB, bWB)
                        MM(ps[ya[half]][:n, :], hcTs[i][:, fc, 0:n], Wx[:, fc % 8, half * 512:(half + 1) * 512], fc == 0, fc == 15,
                           [b_hcTs[i], bWx], [pb[ya[half]]])

            def O2(t):
                p0, n = TP[t]
                i = t % 2
                ya = [2, 3] if i == 0 else [4, 5]
                for half in range(2):
                    ACT(junkb[:n, 0:512], ps[ya[half]][:n, :], AF.Square, [pb[ya[half]]], [b_junk, b_ss],
                        accum_out=ss[:n, 4 + half:5 + half])
                TT(ss[:n, 6:7], ss[:n, 4:5], ss[:n, 5:6], ADD, [b_ss], [b_ss])
                ACT(ss[:n, 6:7], ss[:n, 6:7], AF.Ln, [b_ss], [b_ss], scale=1.0 / D_MODEL, bias=EPS)
                ACT(ss[:n, 7:8], ss[:n, 6:7], AF.Exp, [b_ss], [b_ss], scale=-0.5)
                for half in range(2):
                    hs = slice(half * 512, (half + 1) * 512)
                    STT(xn[i][:n, hs], ps[ya[half]][:n, :], ss[:n, 7:8], grow[:n, 1, hs], MUL, MUL,
                        [pb[ya[half]], b_ss, b_grow[1]], [b_xn[i]])
                    TT(xn[i][:n, hs], xn[i][:n, hs], xo[i][:n, hs], ADD, [b_xn[i], b_xo[i]], [b_xn[i]])
                if last:
                    if t > 0:
                        DMA("sp", out_d[s, p0 - 16:p0 - 16 + n, :], xn[i][:n, :], [b_xn[i]], [b_out])
                else:
                    DMA("sp", xres_d[p0:p0 + n, :], xn[i][:n, :], [b_xn[i]], [b_xres[t]])
                    norm_tile(t, xn[i][:n, :], b_xn[i], 6 + i)

            O1a(0)
            for t in range(NT):
                if t + 1 < NT:
                    O1a(t + 1)
                O1b(t)
                if t >= 1:
                    O2(t - 1)
            O2(NT - 1)
            FENCE(M_BUFS + O_BUFS)

        for s in range(n_seq):
            load_gpre(0)
            FENCE(M_BUFS + O_BUFS)
            for t in range(NT):
                p0, n = TP[t]
                i = t % 2
                src = meta_d if t == 0 else x_d[s, p0 - 16:p0 - 16 + n, :]
                DMA("sp", xo[i][:n, :], src, [], [b_xo[i]])
                norm_tile(t, xo[i][:n, :], b_xo[i], 6 + i)
            FENCE(M_BUFS + O_BUFS)
            for l in range(n_layers):
                lam_init = load_layer_params(l)
                gates_phase(l)
                nxt = load_weights(wm_d[l, 0], 1280)
                for _ in mlstm_prep_gen(l, 0, nxt[0], nxt[1], [0, 1, 3, 7]):
                    pass
                for h in range(4):
                    W, bW = nxt
                    nxt = load_weights(wm_d[l, h + 1], 1280) if h < 3 else load_weights(wd_d[l, 0], 512)
                    hook = None
                    gen = None
                    if h < 3:
                        gen = mlstm_prep_gen(l, h + 1, nxt[0], nxt[1], [3, 7])

                        def hook(t, gen=gen):
                            if t >= 7:
                                for _ in range(4):
                                    next(gen, None)
                    mlstm_head(l, h, W, bW, hook)
                    if gen is not None:
                        for _ in gen:
                            pass
                for h in range(8):
                    W, bW = nxt
                    if h < 7:
                        nxt = load_weights(wd_d[l, h + 1], 512)
                    diff_head(l, h, W, bW, lam_init)
                out_phase(l, s, l == n_layers - 1)
        if os.environ.get('MK_SBUF'):
            print('SBUF remaining', nc.sbuf_bytes_remaining)
        P.emit(st)
    return nc


_CACHE = {}


def _host_layout(inp):
    bf = ml_dtypes.bfloat16
    w_in = np.asarray(inp["w_in"], dtype=np.float32)
    D = 1024
    sec = lambda k: w_in[:, :, k * D:(k + 1) * D] if k < 5 else None
    qm, km, vm, om, zm = (w_in[:, :, k * D:(k + 1) * D] for k in range(5))
    wg = np.ascontiguousarray(w_in[:, :, 5 * D:5 * D + 8])
    off = 5 * D + 8
    qd, kd, vd, zd = (w_in[:, :, off + k * D:off + (k + 1) * D] for k in range(4))
    wm = np.empty((DEPTH, 4, D, 1280), np.float32)
    for h in range(4):
        hs = slice(h * 256, (h + 1) * 256)
        wm[:, h] = np.concatenate([qm[:, :, hs], km[:, :, hs], vm[:, :, hs], om[:, :, hs], zm[:, :, hs]], axis=-1)
    wd = np.empty((DEPTH, 8, D, 512), np.float32)
    for h in range(8):
        hs = slice(h * 128, (h + 1) * 128)
        wd[:, h] = np.concatenate([qd[:, :, hs], kd[:, :, hs], vd[:, :, hs], zd[:, :, hs]], axis=-1)
    grow = np.stack([inp["pre_norm_g"], inp["post_norm_g"], inp["mlstm_norm_g"], inp["diff_norm_g"]], axis=1).astype(np.float32)
    convw = np.ascontiguousarray(np.asarray(inp["conv_w"], np.float32).transpose(0, 2, 1).reshape(DEPTH, 16, 128, 4).transpose(0, 2, 1, 3))
    convb = np.ascontiguousarray(np.asarray(inp["conv_b"], np.float32).reshape(DEPTH, 16, 128).transpose(0, 2, 1))
    bgt = np.ascontiguousarray(np.asarray(inp["b_gates"], np.float32).reshape(DEPTH, 2, 4).transpose(0, 2, 1))
    lam = np.concatenate([inp["lambda_q1"], inp["lambda_k1"], inp["lambda_q2"], inp["lambda_k2"]], axis=-1).astype(np.float32)[:, None, :]
    sel = np.zeros((4, 512), np.float32)
    for h in range(4):
        sel[h, h * 128:(h + 1) * 128] = 1.0
    mask = np.triu(np.ones((128, 128), np.float32)).astype(bf)
    common = {
        "meta": np.ascontiguousarray(inp["meta_tokens"], dtype=np.float32),
        "wm": wm, "wd": wd, "wg": wg, "wo": np.ascontiguousarray(inp["w_out"], dtype=np.float32),
        "grow": np.ascontiguousarray(grow), "convw": convw, "convb": convb, "bg": bgt, "lam": np.ascontiguousarray(lam),
        "identb": np.eye(128, dtype=np.float32).astype(bf), "identf": np.eye(128, dtype=np.float32),
        "mask": mask, "sel": sel,
    }
    return common


def kernel(**inputs):
    n_layers = int(os.environ.get("MK_LAYERS", DEPTH))
    key = ("nc", n_layers)
    if key not in _CACHE:
        _CACHE[key] = build_program(n_layers=n_layers, n_seq=2)
    nc = _CACHE[key]
    common = _host_layout(inputs)
    x = np.asarray(inputs["x"], dtype=np.float32)
    in_maps = []
    for c in range(8):
        m = dict(common)
        m["x"] = np.ascontiguousarray(x[2 * c:2 * c + 2])
        in_maps.append(m)
    res = run_bass_kernel_spmd(nc, in_maps, core_ids=list(range(8)))
    out = np.concatenate([r["out"] for r in res.results], axis=0)
    return out.astype(np.float32)
```
